# Optimizing a Trainium2 kernel written in Bass

```python
import math
import jax, jax.numpy as jnp
from jax import lax
import numpy as np

D_MODEL = 2048
BATCH = 2
SEQ = 8192
DEPTH = 1

HEAD_DIM = 64
ATTN_WIDTH = D_MODEL // 2
RWKV_WIDTH = D_MODEL - ATTN_WIDTH
N_ATTN_HEADS = ATTN_WIDTH // HEAD_DIM
N_RWKV_HEADS = RWKV_WIDTH // HEAD_DIM
DILATED_PATTERNS = ((128, 1), (512, 4), (2048, 16))
REL_BUCKETS = 32
REL_MAX_DIST = 2048
RWKV_DECAY_RANK = max(32, int(round(1.8 * RWKV_WIDTH ** 0.5 / 32)) * 32)
RWKV_A_RANK = max(32, int(round(1.8 * RWKV_WIDTH ** 0.5 / 32)) * 32)
RWKV_GATE_RANK = max(32, int(round(0.6 * RWKV_WIDTH ** 0.8 / 32)) * 32)
RWKV_COLS = 3 * RWKV_WIDTH + RWKV_DECAY_RANK + RWKV_A_RANK + RWKV_GATE_RANK
RWKV_SPLITS = (RWKV_WIDTH, 2 * RWKV_WIDTH, 3 * RWKV_WIDTH,
               3 * RWKV_WIDTH + RWKV_DECAY_RANK,
               3 * RWKV_WIDTH + RWKV_DECAY_RANK + RWKV_A_RANK)
IN_COLS = 3 * ATTN_WIDTH + RWKV_COLS
FFN_HIDDEN = -(-(8 * D_MODEL) // (3 * 256)) * 256
RMS_EPS = 1e-6
GN_EPS = 64e-5
DECAY_SCALE = math.exp(-0.5)
ATTN_SCALE = HEAD_DIM ** -0.5

kernel_name = 'hymba_rwkv7_dilated_attn_block'


def rms_norm(x, g):
    xf = x.astype(jnp.float32)
    y = xf * lax.rsqrt(jnp.mean(xf * xf, axis=-1, keepdims=True) + RMS_EPS)
    return (y * g.astype(jnp.float32)).astype(x.dtype)


def t5_bucket(dist):
    exact = REL_BUCKETS // 2
    d_f = jnp.maximum(dist, 1).astype(jnp.float32)
    large = exact + (jnp.log(d_f / exact) / math.log(REL_MAX_DIST / exact)
                     * (REL_BUCKETS - exact)).astype(jnp.int32)
    large = jnp.minimum(large, REL_BUCKETS - 1)
    return jnp.where(dist < exact, dist, large)


def dilated_branch(q, k, v, bias_table, window, dilation):
    b, s, h, dh = q.shape
    blk = window // dilation
    span = blk * dilation
    sp = -(-s // span) * span
    nblk = sp // span

    def blocks(t):
        t = jnp.pad(t, ((0, 0), (0, sp - s), (0, 0), (0, 0)))
        return t.reshape(b, nblk, blk, dilation, h, dh)

    def with_prev(t):
        prev = jnp.pad(t, ((0, 0), (1, 0), (0, 0), (0, 0), (0, 0), (0, 0)))[:, :-1]
        return jnp.concatenate([prev, t], axis=2)

    qb = blocks(q)
    kc = with_prev(blocks(k))
    vc = with_prev(blocks(v))
    qi = jnp.arange(blk)[:, None]
    ki = jnp.arange(2 * blk)[None, :]
    rel = qi + blk - ki
    band = (rel >= 0) & (rel <= blk)
    bias = bias_table[t5_bucket(jnp.clip(rel, 0, blk) * dilation)]
    bias = jnp.transpose(bias, (2, 0, 1)).astype(jnp.float32)
    not_first = jnp.arange(nblk)[:, None, None] > 0
    valid = band[None] & (not_first | (ki >= blk)[None])
    logits = jnp.einsum('bnqrhd,bnkrhd->bnrhqk', qb, kc) * ATTN_SCALE + bias
    logits = jnp.where(valid[None, :, None, None], logits, -jnp.inf)
    m = jnp.max(logits, axis=-1, keepdims=True)
    p = jnp.exp(logits - m)
    l = jnp.sum(p, axis=-1, keepdims=True)
    o = jnp.einsum('bnrhqk,bnkrhd->bnqrhd', p / l, vc)
    lse = (m + jnp.log(l))[..., 0]
    o = o.reshape(b, sp, h, dh)[:, :s]
    lse = jnp.transpose(lse, (0, 1, 4, 2, 3)).reshape(b, sp, h)[:, :s]
    return o, lse


def dilated_attention(q, k, v, bias_table):
    outs, lses = [], []
    for window, dilation in DILATED_PATTERNS:
        o, lse = dilated_branch(q, k, v, bias_table, window, dilation)
        outs.append(o)
        lses.append(lse)
    wts = jax.nn.softmax(jnp.stack(lses), axis=0)
    return jnp.sum(wts[..., None] * jnp.stack(outs), axis=0)


def rwkv7_step(state, inp):
    r, w, k, v, a, bb = inp
    sa = jnp.einsum('bhij,bhj->bhi', state, a)
    state = (state * w[:, :, None, :] + sa[..., None] * bb[:, :, None, :]
             + v[..., None] * k[:, :, None, :])
    y = jnp.einsum('bhij,bhj->bhi', state, r)
    return state, y


def rwkv7_time_mix(z, shift_mix, w0, w_up, a0, a_up, g_up, k_k, k_a, r_k, ln_w, ln_b):
    b, s, _ = z.shape
    prev = jnp.pad(z, ((0, 0), (1, 0), (0, 0)))[:, :-1]
    z = z + (prev - z) * shift_mix
    r, k, v, w_lo, a_lo, g_lo = jnp.split(z, list(RWKV_SPLITS), axis=-1)
    decay = jnp.exp(-DECAY_SCALE * jax.nn.sigmoid(w0 + jnp.tanh(w_lo) @ w_up))
    a = jax.nn.sigmoid(a0 + a_lo @ a_up)
    g = jax.nn.sigmoid(g_lo) @ g_up

    def heads(t):
        return t.reshape(b, s, N_RWKV_HEADS, HEAD_DIM)

    kk = heads(k * k_k)
    kk = kk / jnp.maximum(jnp.sqrt(jnp.sum(kk * kk, axis=-1, keepdims=True)), 1e-12)
    k = k * (1.0 + (a - 1.0) * k_a)
    rh, kh, vh, wh, ah = heads(r), heads(k), heads(v), heads(decay), heads(a)
    xs = tuple(jnp.moveaxis(t, 1, 0) for t in (rh, wh, kh, vh, -kk, kk * ah))
    state0 = jnp.zeros((b, N_RWKV_HEADS, HEAD_DIM, HEAD_DIM), jnp.float32)
    _, y = lax.scan(rwkv7_step, state0, xs)
    y = jnp.moveaxis(y, 0, 1)
    mu = jnp.mean(y, axis=-1, keepdims=True)
    var = jnp.mean(jnp.square(y - mu), axis=-1, keepdims=True)
    y = ((y - mu) * lax.rsqrt(var + GN_EPS)).reshape(b, s, RWKV_WIDTH) * ln_w + ln_b
    bonus = jnp.sum(rh * kh * r_k, axis=-1, keepdims=True) * vh
    return (y + bonus.reshape(b, s, RWKV_WIDTH)) * g


def setup_inputs(seed: int = 0) -> dict:
    key = jax.random.key(seed)
    ks = jax.random.split(key, 20)
    L = DEPTH

    def nrm(k, shape, scale):
        return scale * jax.random.normal(k, shape, jnp.float32)

    return {
        'x': nrm(ks[0], (BATCH, SEQ, D_MODEL), 1.0),
        'norm1_g': 1.0 + nrm(ks[1], (L, D_MODEL), 0.05),
        'w_in': nrm(ks[2], (L, D_MODEL, IN_COLS), D_MODEL ** -0.5),
        'rwkv_shift_mix': jax.random.uniform(ks[3], (L, RWKV_COLS), jnp.float32),
        'rwkv_w0': jnp.linspace(-4.0, 2.0, RWKV_WIDTH, dtype=jnp.float32) + nrm(ks[4], (L, RWKV_WIDTH), 0.1),
        'rwkv_w_up': nrm(ks[5], (L, RWKV_DECAY_RANK, RWKV_WIDTH), 0.5 * RWKV_DECAY_RANK ** -0.5),
        'rwkv_a0': nrm(ks[6], (L, RWKV_WIDTH), 0.1),
        'rwkv_a_up': nrm(ks[7], (L, RWKV_A_RANK, RWKV_WIDTH), RWKV_A_RANK ** -0.5),
        'rwkv_g_up': nrm(ks[8], (L, RWKV_GATE_RANK, RWKV_WIDTH), RWKV_GATE_RANK ** -0.5),
        'rwkv_k_k': 0.85 + nrm(ks[9], (L, RWKV_WIDTH), 0.05),
        'rwkv_k_a': 1.0 + nrm(ks[10], (L, RWKV_WIDTH), 0.05),
        'rwkv_r_k': nrm(ks[11], (L, N_RWKV_HEADS, HEAD_DIM), 0.1),
        'rwkv_ln_w': 1.0 + nrm(ks[12], (L, RWKV_WIDTH), 0.05),
        'rwkv_ln_b': nrm(ks[13], (L, RWKV_WIDTH), 0.02),
        'rel_bias_table': nrm(ks[14], (REL_BUCKETS, N_ATTN_HEADS), 0.5),
        'w_out': nrm(ks[15], (L, D_MODEL, D_MODEL), D_MODEL ** -0.5),
        'norm2_g': 1.0 + nrm(ks[16], (L, D_MODEL), 0.05),
        'w_gate_up': nrm(ks[17], (L, D_MODEL, 2 * FFN_HIDDEN), D_MODEL ** -0.5),
        'w_down': nrm(ks[18], (L, FFN_HIDDEN, D_MODEL), FFN_HIDDEN ** -0.5),
        'final_g': 1.0 + nrm(ks[19], (D_MODEL,), 0.05),
    }


def reference(x, norm1_g, w_in, rwkv_shift_mix, rwkv_w0, rwkv_w_up, rwkv_a0, rwkv_a_up,
              rwkv_g_up, rwkv_k_k, rwkv_k_a, rwkv_r_k, rwkv_ln_w, rwkv_ln_b, rel_bias_table,
              w_out, norm2_g, w_gate_up, w_down, final_g):
    b, s, _ = x.shape
    h = x
    for l in range(DEPTH):
        u = rms_norm(h, norm1_g[l])
        proj = (u @ w_in[l]).astype(jnp.float32)
        q, k, v, z = jnp.split(proj, [ATTN_WIDTH, 2 * ATTN_WIDTH, 3 * ATTN_WIDTH], axis=-1)
        hd = (b, s, N_ATTN_HEADS, HEAD_DIM)
        attn = dilated_attention(q.reshape(hd), k.reshape(hd), v.reshape(hd),
                                 rel_bias_table.astype(jnp.float32))
        rw = rwkv7_time_mix(z, rwkv_shift_mix[l], rwkv_w0[l], rwkv_w_up[l], rwkv_a0[l],
                            rwkv_a_up[l], rwkv_g_up[l], rwkv_k_k[l], rwkv_k_a[l],
                            rwkv_r_k[l], rwkv_ln_w[l], rwkv_ln_b[l])
        mixed = jnp.concatenate([attn.reshape(b, s, ATTN_WIDTH), rw], axis=-1).astype(h.dtype)
        h = h + mixed @ w_out[l]
        u = rms_norm(h, norm2_g[l])
        gate, up = jnp.split(u @ w_gate_up[l], 2, axis=-1)
        h = h + (jax.nn.silu(gate) * up) @ w_down[l]
    return rms_norm(h, final_g)
```

```python
from contextlib import ExitStack
import math
import numpy as np
import ml_dtypes
import concourse.bass as bass
import concourse.mybir as mybir
from concourse.bass_utils import run_bass_kernel_spmd

F32 = mybir.dt.float32
BF16 = mybir.dt.bfloat16
AF = mybir.ActivationFunctionType
ALU = mybir.AluOpType
AX = mybir.AxisListType

ENGS = ("sp", "act", "dve", "pool", "pe")

D_MODEL = 2048
SEQ = 8192
NBATCH = 2
HD = 64
FFN = 5632
NCORES = 8
PATTERNS = ((128, 1), (512, 4), (2048, 16))
RMS_EPS = 1e-6
GN_EPS = 64e-5
DECAY_SCALE = math.exp(-0.5)
NEG = -30000.0
RW_COLS = 1056
CH = 64


class Prog:
    def __init__(self, nc, same_eng_sync=True):
        self.nc = nc
        self.same_eng_sync = same_eng_sync
        self.stack = ExitStack()
        self.sems = {}
        self.cnt = {}
        self.ops = {e: [] for e in ENGS}
        self.waited = {e: {} for e in ENGS}
        self.bufs = {}
        self.pending = {e: ([], []) for e in ENGS}
        self.nops = 0
        self.psum_names = {"pT", "pp", "pS", "pO", "pb", "ptr", "pg"}

    def _excl(self, reads, writes):
        reads = list(reads)
        writes = list(writes)
        for k in reads:
            n = k[0] if isinstance(k, tuple) else k
            if n in self.psum_names and k not in writes:
                writes.append(k)
        return reads, writes

    def sem(self, key):
        if key not in self.sems:
            name = "s_" + "_".join(str(k) for k in (key if isinstance(key, tuple) else (key,)))
            self.sems[key] = self.stack.enter_context(self.nc.semaphore(name))
            self.cnt[key] = 0
        return self.sems[key]

    def _deps(self, reads, writes):
        deps = []
        for b in reads:
            st = self.bufs.get(b)
            if st is not None and st[0] is not None:
                deps.append(st[0])
        for b in writes:
            st = self.bufs.get(b)
            if st is not None:
                if st[0] is not None:
                    deps.append(st[0])
                deps.extend(st[1])
        return deps

    def _commit(self, tok, reads, writes):
        for b in reads:
            st = self.bufs.get(b)
            if st is None:
                self.bufs[b] = [None, [tok]]
            else:
                st[1].append(tok)
        for b in writes:
            self.bufs[b] = [tok, []]

    def _filter(self, eng, deps):
        w = self.waited[eng]
        best = {}
        for d in deps:
            if d is None:
                continue
            key, val = d
            if key == eng and (eng == "pe" or (not self.same_eng_sync and eng != "pool")):
                continue
            if w.get(key, 0) >= val:
                continue
            if best.get(key, 0) < val:
                best[key] = val
        for key, val in best.items():
            w[key] = val
        return list(best.items())

    def op(self, eng, fn, reads=(), writes=(), extra=(), inc=True):
        reads, writes = self._excl(reads, writes)
        deps = self._deps(reads, writes) + list(extra)
        waits = self._filter(eng, deps)
        self.sem(eng)
        self.nops += 1
        if not inc:
            self.ops[eng].append((waits, fn, None, 0))
            self.pending[eng][0].extend(reads)
            self.pending[eng][1].extend(writes)
            return None
        self.cnt[eng] += 1
        tok = (eng, self.cnt[eng])
        self.ops[eng].append((waits, fn, eng, 1))
        pr, pw = self.pending[eng]
        self._commit(tok, list(pr) + list(reads), list(pw) + list(writes))
        self.pending[eng] = ([], [])
        return tok

    def dma(self, q, out, in_, slot, reads=(), writes=(), extra=(), **kw):
        key = ("d", slot)
        self.sem(key)
        deps = self._deps(reads, writes) + list(extra)
        if self.cnt[key] > 0:
            deps.append((key, self.cnt[key]))
        waits = self._filter(q, deps)
        self.cnt[key] += 16
        tok = (key, self.cnt[key])
        self.ops[q].append((waits, (lambda e, o=out, i=in_, k=kw: e.dma_start(out=o, in_=i, **k)), key, 16))
        self._commit(tok, reads, writes)
        self.nops += 1
        return tok

    def dma_fn(self, q, fn, slot, reads=(), writes=(), extra=()):
        return self.custom(q, fn, ("d", slot), 16, reads=reads, writes=writes, extra=extra)

    def custom(self, eng, fn, key, amt, reads=(), writes=(), extra=()):
        self.sem(key)
        deps = self._deps(reads, writes) + list(extra)
        if self.cnt[key] > 0:
            deps.append((key, self.cnt[key]))
        waits = self._filter(eng, deps)
        self.cnt[key] += amt
        tok = (key, self.cnt[key])
        self.ops[eng].append((waits, fn, key, amt))
        self._commit(tok, reads, writes)
        return tok

    def flush(self):
        for e in ENGS:
            assert not self.pending[e][0] and not self.pending[e][1], f"pending non-inc ops on {e}"
        final = [(k, v) for k, v in self.cnt.items() if v > 0]
        for e in ENGS:
            self.sem(e)
            waits = self._filter(e, final)
            if waits:
                self.ops[e].append((waits, None, None, 0))
        ops = self.ops
        sems = self.sems

        def emit(engine, ename):
            for waits, fn, key, amt in ops[ename]:
                for (k, v) in waits:
                    engine.wait_ge(sems[k], v)
                if fn is None:
                    continue
                ins = fn(engine)
                if key is not None:
                    ins.then_inc(sems[key], amt)

        with self.nc.Block() as block:
            @block.sync
            def _(e):
                emit(e, "sp")

            @block.scalar
            def _(e):
                emit(e, "act")

            @block.vector
            def _(e):
                emit(e, "dve")

            @block.gpsimd
            def _(e):
                emit(e, "pool")

            @block.tensor
            def _(e):
                emit(e, "pe")

        self.ops = {e: [] for e in ENGS}
        self.bufs = {}
        for e in ENGS:
            self.waited[e] = dict(self.cnt)

    def close(self):
        self.stack.close()


class Alloc:
    def __init__(self, nc):
        self.nc = nc
        self.st = ExitStack()

    def sb(self, name, shape, dt):
        return self.st.enter_context(self.nc.sbuf_tensor(name, list(shape), dt))

    def ps(self, name, shape, dt):
        return self.st.enter_context(self.nc.psum_tensor(name, list(shape), dt))

    def close(self):
        self.st.close()


def _t5_bucket_np(dist):
    exact = 16
    d_f = np.maximum(dist, 1).astype(np.float32)
    large = exact + (np.log(d_f / np.float32(exact)) / np.float32(math.log(2048 / exact))
                     * np.float32(32 - exact)).astype(np.int32)
    large = np.minimum(large, 31)
    return np.where(dist < exact, dist, large)


def _attn_tables():
    ki = np.arange(128)[:, None]
    qi = np.arange(128)[None, :]
    bidx = np.zeros((3, 128, 256), np.int64)
    mask = np.zeros((3, 128, 256), np.float32)
    for p, (w, dil) in enumerate(PATTERNS):
        rel_prev = qi + 128 - ki
        rel_cur = qi - ki
        vp = ki >= qi
        vc = ki <= qi
        bidx[p, :, 0:128] = _t5_bucket_np(np.clip(rel_prev, 0, 128) * dil)
        bidx[p, :, 128:256] = _t5_bucket_np(np.clip(rel_cur, 0, 128) * dil)
        mask[p, :, 0:128] = np.where(vp, 0.0, NEG)
        mask[p, :, 128:256] = np.where(vc, 0.0, NEG)
    return bidx, mask


def wtile_index(N, kg, nb):
    return kg * (N // 512) + nb


def phase_w(P, nc, specs, pre=None, engs=("dve", "pool")):
    A = Alloc(nc)
    if pre is not None:
        pre()
    NS = 5
    stg = [A.sb(f"w_stg{i}", [128, 8, 512], F32) for i in range(NS)]
    ob = [A.sb(f"w_ob{i}", [128, 8, 512], BF16) for i in range(NS)]
    items = []
    for (src, dst, K, N) in specs:
        nkc = K // 128
        for kg in range((nkc + 15) // 16):
            nk = min(16, nkc - kg * 16)
            for nb in range(N // 512):
                tix = wtile_index(N, kg, nb)
                for h0 in range(0, nk, 8):
                    hk = min(8, nk - h0)
                    r0 = (kg * 16 + h0) * 128
                    src_ap = src[r0:r0 + hk * 128, nb * 512:(nb + 1) * 512].rearrange("(kc p) c -> p kc c", p=128)
                    dst_ap = dst[tix].rearrange("p (kc c) -> p kc c", c=512)[:, h0:h0 + hk, :]
                    items.append((src_ap, dst_ap, hk))

    def load(i):
        src_ap, _, hk = items[i]
        s = i % NS
        P.dma("sp", stg[s][:, 0:hk, :], src_ap, ("wl", s), writes=[("wstg", s)])

    for i in range(min(NS - 1, len(items))):
        load(i)
    for i, (src_ap, dst_ap, hk) in enumerate(items):
        s = i % NS
        eng = engs[i % len(engs)]
        if eng == "act":
            P.op("act", lambda e, s=s, hk=hk: e.activation(out=ob[s][:, 0:hk, :], in_=stg[s][:, 0:hk, :], func=AF.Copy),
                 reads=[("wstg", s)], writes=[("wob", s)])
        else:
            P.op(eng, lambda e, s=s, hk=hk: e.tensor_copy(out=ob[s][:, 0:hk, :], in_=stg[s][:, 0:hk, :]),
                 reads=[("wstg", s)], writes=[("wob", s)])
        if i + NS - 1 < len(items):
            load(i + NS - 1)
        P.dma("sp", dst_ap, ob[s][:, 0:hk, :], ("ws", s), reads=[("wob", s)])
    P.flush()
    A.close()


def norm_front(P, xin, xkey, gbc, junk, ssq, ub, ubkey, tagk):
    P.op("act", lambda e: e.activation(out=junk[:], in_=xin, func=AF.Square, accum_out=ssq[:, 0:1]),
         reads=[xkey], writes=["junk", ("ssq", tagk)])
    P.op("act", lambda e: e.activation(out=ssq[:, 1:2], in_=ssq[:, 0:1], func=AF.Sqrt, scale=1.0 / D_MODEL, bias=RMS_EPS),
         reads=[("ssq", tagk)], writes=[("ssq", tagk)])
    P.op("dve", lambda e: e.reciprocal(out=ssq[:, 2:3], in_=ssq[:, 1:2]), reads=[("ssq", tagk)], writes=[("ssq", tagk)])
    P.op("dve", lambda e: e.scalar_tensor_tensor(out=ub[:], in0=xin, scalar=ssq[:, 2:3], in1=gbc[:], op0=ALU.mult, op1=ALU.mult),
         reads=[xkey, ("ssq", tagk), "gbc"], writes=[ubkey])


def norm_back(P, ub, ubkey, pT, uT_dst, uTkey, ident, evac_eng="dve"):
    pk = [("pT", 0), ("pT", 1)]
    for kc in range(16):
        P.op("pe", lambda e, kc=kc: e.transpose(out=pT[:, kc, :], in_=ub[:, kc * 128:(kc + 1) * 128], identity=ident[:]),
             reads=[ubkey, "ident"], writes=pk, inc=(kc == 15))
    for half in range(2):
        hsl = slice(half * 8, half * 8 + 8)
        eng = evac_eng if half == 0 else ("act" if evac_eng == "dve" else "dve")
        if eng == "act":
            P.op("act", lambda e, hsl=hsl: e.activation(out=uT_dst[:, hsl, :], in_=pT[:, hsl, :], func=AF.Copy), reads=[pk[half]], writes=[uTkey])
        else:
            P.op(eng, lambda e, hsl=hsl: e.tensor_copy(out=uT_dst[:, hsl, :], in_=pT[:, hsl, :]), reads=[pk[half]], writes=[uTkey])


def norm_transpose_block(P, xin, xkey, gbc, junk, ssq, ub, ubkey, pT, pTkey, uT_dst, uTkey, ident, tagk,
                         evac_eng="dve"):
    norm_front(P, xin, xkey, gbc, junk, ssq, ub, ubkey, tagk)
    norm_back(P, ub, ubkey, pT, uT_dst, uTkey, ident, evac_eng)


def phase_a(P, nc, io, nsc=SEQ // 2048):
    A = Alloc(nc)
    x = io["xb"]
    wA = A.sb("wA", [128, 16, 768], BF16)
    wst = [A.sb(f"a_wst{i}", [128, 768], F32) for i in range(2)]
    xb = [A.sb(f"a_xb{i}", [128, 2048], F32) for i in range(2)]
    gbc = A.sb("a_gbc", [128, 2048], F32)
    junk = A.sb("a_junk", [128, 2048], BF16)
    ub = [A.sb(f"a_ub{i}", [128, 2048], BF16) for i in range(2)]
    ssq = [A.sb(f"a_ssq{i}", [128, 4], F32) for i in range(2)]
    uT = [A.sb(f"a_uT{i}", [128, 16, 512], BF16) for i in range(2)]
    ident = A.sb("a_ident", [128, 128], BF16)
    identf = A.sb("a_identf", [128, 128], F32)
    ones_f = A.sb("a_ones", [128, 64], F32)
    QT = [A.sb(f"a_QT{i}", [128, 2048], BF16) for i in range(2)]
    KT = [A.sb(f"a_KT{i}", [128, 4096], BF16) for i in range(2)]
    VT = [A.sb(f"a_VT{i}", [128, 4096], BF16) for i in range(2)]
    RING = (4, 8, 32)
    Vtok = [A.sb(f"a_Vtok{p}", [128, RING[p], 4, 65], BF16) for p in range(3)]
    acc = [A.sb(f"a_acc{h}", [65, 2048], F32) for h in range(2)]
    PT = [A.sb(f"a_PT{i}", [128, 512], BF16) for i in range(3)]
    biasT = A.sb("a_biasT", [128, 3, 4, 256], BF16)
    maskf = A.sb("a_maskf", [128, 3, 256], F32)
    recs = [A.sb(f"a_rec{i}", [64, 512], F32) for i in range(2)]
    mo = [A.sb(f"a_mo{i}", [64, 2048], BF16) for i in range(2)]
    pT = [A.ps(f"a_pT{i}", [128, 16, 128], BF16) for i in range(1)]
    pproj = [A.ps(f"a_pp{i}", [128, 512], F32) for i in range(2)]
    pS = [A.ps(f"a_pS{i}", [128, 512], F32) for i in range(2)]
    pO = [A.ps(f"a_pO{i}", [128, 512], F32) for i in range(2)]

    P.dma("sp", identf[:], io["ident"], "c0", writes=["identf"])
    P.op("dve", lambda e: e.tensor_copy(out=ident[:], in_=identf[:]), reads=["identf"], writes=["ident"])
    P.op("pool", lambda e: e.memset(ones_f[:], 1.0), writes=["ones"])
    P.dma("sp", gbc[:], io["norm1_g"].partition_broadcast(128), "c1", writes=["gbc"])
    P.dma("sp", maskf[:], io["amask"].rearrange("k (p q) -> k p q", p=3), "c2", writes=["maskf"])
    for p in range(3):
        P.dma("sp", xb[0][:, 0:1024], io["abias"][:, p * 1024:(p + 1) * 1024], ("xl", 0), writes=[("xb", 0)])
        P.op("dve", lambda e, p=p: e.tensor_tensor(
            out=biasT[:, p, :, :], in0=xb[0][:, 0:1024].rearrange("k (h q) -> k h q", h=4),
            in1=maskf[:, p:p + 1, :].to_broadcast([128, 4, 256]), op=ALU.add),
            reads=[("xb", 0), "maskf"], writes=["biasT"])
    for kc in range(16):
        s = kc % 2
        P.dma("sp", wst[s][:], io["w_in_a"][kc * 128:(kc + 1) * 128, :], ("awl", s), writes=[("wst", s)])
        P.op("pool" if kc % 2 else "dve", lambda e, s=s, kc=kc: e.tensor_copy(out=wA[:, kc, :], in_=wst[s][:]),
             reads=[("wst", s)], writes=["wA"])
    for p in range(3):
        P.op("pool", lambda e, p=p: e.memset(Vtok[p][:, :, :, 64:65], 1.0), writes=[("Vones", p)])

    ctr = {"o": 0, "t": 0, "s": 0, "m": 0, "pp": 0}

    def nfront(t, tb):
        blk = t * 4 + tb
        xs = blk % 2
        P.dma("sp", xb[xs][:], x[blk * 128:(blk + 1) * 128, :], ("xl", xs), writes=[("xb", xs)])
        norm_front(P, xb[xs][:], ("xb", xs), gbc, junk, ssq[xs], ub[xs], ("ub", xs), xs)

    def nback(t, tb):
        us = t % 2
        xs = (t * 4 + tb) % 2
        norm_back(P, ub[xs], ("ub", xs), pT[0], uT[us][:, :, tb * 128:(tb + 1) * 128], ("uT", us), ident)

    def ut_store(t):
        us = t % 2
        P.dma("sp", io["uT_scr"][t].rearrange("p (kc t) -> p kc t", t=512), uT[us][:], ("uts", us), reads=[("uT", us)])

    def proj_group(t, cb):
        us = t % 2
        ring0 = (t % 8) * 512
        loc0 = (t % 4) * 512
        pp = pproj[ctr["pp"] % 2]
        ppk = ("pp", ctr["pp"] % 2)
        ctr["pp"] += 1
        for kc in range(16):
            P.op("pe", lambda e, pp=pp, cb=cb, kc=kc, us=us: e.matmul(
                pp[:], lhsT=wA[:, kc, cb * 128:(cb + 1) * 128], rhs=uT[us][:, kc, :], start=(kc == 0), stop=(kc == 15)),
                reads=["wA", ("uT", us)], writes=[ppk], inc=(kc == 15))
        hp = cb % 2
        if cb < 2:
            P.op("act", lambda e, pp=pp, hp=hp, loc0=loc0: e.activation(out=QT[hp][:, loc0:loc0 + 512], in_=pp[:], func=AF.Copy, scale=0.125),
                 reads=[ppk], writes=[("QT", hp)])
        elif cb < 4:
            P.op("dve", lambda e, pp=pp, hp=hp, ring0=ring0: e.tensor_copy(out=KT[hp][:, ring0:ring0 + 512], in_=pp[:]),
                 reads=[ppk], writes=[("KT", hp)])
        else:
            P.op("act", lambda e, pp=pp, hp=hp, ring0=ring0: e.activation(out=VT[hp][:, ring0:ring0 + 512], in_=pp[:], func=AF.Copy),
                 reads=[ppk], writes=[("VT", hp)])

    for sc in range(nsc):
        if sc == 0:
            nfront(0, 0)
            for tb in range(4):
                if tb + 1 < 4:
                    nfront(0, tb + 1)
                nback(0, tb)
            ut_store(0)
        for tl in range(4):
            t = sc * 4 + tl
            nxt = t + 1 if t + 1 < nsc * 4 else None
            sched = ["f0", "g0", "b0", "f1", "g1", "b1", "f2", "g2", "b2", "f3", "g3", "b3", "g4", "g5"]
            for it in sched:
                if it[0] == "g":
                    proj_group(t, int(it[1]))
                elif nxt is not None:
                    (nfront if it[0] == "f" else nback)(nxt, int(it[1]))
            if nxt is not None:
                ut_store(nxt)
        rbase = (sc % 2) * 2048
        for hp in range(2):
            for p, (win, dil) in enumerate(PATTERNS):
                pdist = dil
                pend = []
                po = pok = None
                for bi in range(16):
                    grp, gi = bi // 2, bi % 2
                    if gi == 0:
                        po = pO[ctr["o"] % 2]
                        pok = ("pO", ctr["o"] % 2)
                        ctr["o"] += 1
                    if dil == 1:
                        nl, r = bi, 0
                    elif dil == 4:
                        nl, r = bi // 4, bi % 4
                    else:
                        nl, r = 0, bi
                    gblk = sc * 16 + bi
                    has_prev = (sc * 2048 + nl * win) >= win
                    cur_slot = gblk % RING[p]
                    prev_slot = (gblk - pdist) % RING[p]
                    c0 = rbase + nl * win + r
                    q0 = nl * win + r
                    pr0 = (c0 - win) % 4096
                    cur_sl = slice(c0, c0 + 127 * dil + 1, dil)
                    prev_sl = slice(pr0, pr0 + 127 * dil + 1, dil)
                    q_sl = slice(q0, q0 + 127 * dil + 1, dil)
                    tb_ = ctr["t"] % 2
                    tj = tb_ * 8
                    ctr["t"] += 1
                    P.op("pe", lambda e, hp=hp, cur_sl=cur_sl, tj=tj: e.transpose(out=pT[0][:, tj, :], in_=VT[hp][:, cur_sl], identity=ident[:]),
                         reads=[("VT", hp), "ident"], writes=[("pT", tb_)])
                    P.op("dve", lambda e, p=p, cur_slot=cur_slot, hp=hp, tj=tj: e.tensor_copy(
                        out=Vtok[p][:, cur_slot, 2 * hp:2 * hp + 2, 0:64], in_=pT[0][:, tj, :].rearrange("k (h d) -> k h d", h=2)),
                        reads=[("pT", tb_)], writes=[("Vtok", p, cur_slot, hp)])
                    ps_ = pS[ctr["s"] % 2]
                    psk = ("pS", ctr["s"] % 2)
                    pt_ = PT[ctr["s"] % 3]
                    ptk = ("PT", ctr["s"] % 3)
                    ctr["s"] += 1
                    parts = ([0] if has_prev else []) + [1]
                    nmm = len(parts) * 2
                    mi = 0
                    for hl in range(2):
                        for part in parts:
                            ksl = prev_sl if part == 0 else cur_sl
                            col = (hl * 2 + part) * 128
                            mi += 1
                            P.op("pe", lambda e, ps_=ps_, hp=hp, hl=hl, ksl=ksl, q_sl=q_sl, col=col, first=(mi == 1), last=(mi == nmm): e.matmul(
                                ps_[:, col:col + 128], lhsT=KT[hp][hl * 64:(hl + 1) * 64, ksl], rhs=QT[hp][hl * 64:(hl + 1) * 64, q_sl],
                                start=first, stop=last, skip_group_check=True),
                                reads=[("KT", hp), ("QT", hp)], writes=[psk], inc=(mi == nmm))
                        if hl == 0:
                            P.op("pe", lambda e, ps_=ps_, p=p, hp=hp: e.matmul(
                                ps_[:], lhsT=ident[:], rhs=biasT[:, p, 2 * hp:2 * hp + 2, :].rearrange("k h q -> k (h q)"),
                                start=False, stop=False, skip_group_check=True),
                                reads=["ident", "biasT"], writes=[psk], inc=False)
                    if has_prev:
                        P.op("act", lambda e, ps_=ps_, pt_=pt_: e.activation(out=pt_[:], in_=ps_[:], func=AF.Exp),
                             reads=[psk], writes=[ptk])
                    else:
                        P.op("act", lambda e, ps_=ps_, pt_=pt_: e.activation(
                            out=pt_[:].rearrange("k (h c) -> k h c", h=2)[:, :, 128:256],
                            in_=ps_[:].rearrange("k (h c) -> k h c", h=2)[:, :, 128:256], func=AF.Exp),
                            reads=[psk], writes=[ptk])

                    def pv(gi=gi, parts=parts, prev_slot=prev_slot, cur_slot=cur_slot, pt_=pt_, ptk=ptk, po=po, pok=pok,
                           nl=nl, r=r, bi=bi, p=p, hp=hp, dil=dil, win=win):
                        for hl in range(2):
                            h = 2 * hp + hl
                            for pi_, part in enumerate(parts):
                                slot = prev_slot if part == 0 else cur_slot
                                col = (hl * 2 + part) * 128
                                last = (gi == 1 and hl == 1 and pi_ == len(parts) - 1)
                                oc = (hl * 2 + gi) * 128
                                P.op("pe", lambda e, slot=slot, h=h, col=col, oc=oc, pi_=pi_, np_=len(parts): e.matmul(
                                    po[0:65, oc:oc + 128], lhsT=Vtok[p][:, slot, h, :], rhs=pt_[:, col:col + 128],
                                    start=(pi_ == 0), stop=(pi_ == np_ - 1)),
                                    reads=[("Vtok", p, slot, hp), ("Vones", p), ptk], writes=[pok], inc=last)
                        if gi == 1:
                            if dil == 1:
                                nl0, r0 = nl - 1, 0
                            else:
                                nl0, r0 = nl, r - 1
                            for hl in range(2):
                                src = po[0:65, hl * 256:(hl + 1) * 256].rearrange("d (j i) -> d j i", j=2)
                                if dil == 1:
                                    dst = acc[hl][:, nl0 * 128:nl0 * 128 + 256].rearrange("d (j i) -> d j i", j=2)
                                else:
                                    dst = acc[hl][:, nl0 * win:nl0 * win + 128 * dil].rearrange("d (i r) -> d r i", r=dil)[:, r0:r0 + 2, :]
                                if p == 0:
                                    P.op("dve", lambda e, dst=dst, src=src: e.tensor_copy(out=dst, in_=src), reads=[pok], writes=[("acc", hl)])
                                else:
                                    P.op("dve", lambda e, dst=dst, src=src: e.tensor_tensor(out=dst, in0=src, in1=dst, op=ALU.add),
                                         reads=[pok, ("acc", hl)], writes=[("acc", hl)])
                    if pend:
                        pend.pop()()
                    pend.append(pv)
                if pend:
                    pend.pop()()
            for hl in range(2):
                h = 2 * hp + hl
                m = mo[ctr["m"] % 2]
                mk = ("mo", ctr["m"] % 2)
                ctr["m"] += 1
                for pc in range(4):
                    po = pO[ctr["o"] % 2]
                    pok = ("pO", ctr["o"] % 2)
                    ctr["o"] += 1
                    P.op("pe", lambda e, po=po, hl=hl, pc=pc: e.matmul(po[0:64, :], lhsT=ones_f[64:65, 0:64], rhs=acc[hl][64:65, pc * 512:(pc + 1) * 512],
                                                                    start=True, stop=True),
                         reads=[("acc", hl), "ones"], writes=[pok])
                    rc = recs[pc % 2]
                    rck = ("rec", pc % 2)
                    P.op("dve", lambda e, po=po, rc=rc: e.reciprocal(out=rc[:], in_=po[0:64, :]), reads=[pok], writes=[rck])
                    P.op("pool", lambda e, m=m, hl=hl, pc=pc, rc=rc: e.tensor_tensor(out=m[:, pc * 512:(pc + 1) * 512], in0=acc[hl][0:64, pc * 512:(pc + 1) * 512],
                                                                           in1=rc[:], op=ALU.mult),
                         reads=[("acc", hl), rck], writes=[mk])
                P.dma("sp", io["mix_loc"][2 * sc:2 * sc + 2, h * 64:(h + 1) * 64, :].rearrange("j r t -> r j t"),
                      m[:].rearrange("r (j t) -> r j t", j=2), ("mos", mk[1]), reads=[mk])
    P.flush()
    A.close()


def phase_d(P, nc, io, ntile=4):
    A = Alloc(nc)
    xs = io["xs"]
    mix_gath = io["mix_gath"]
    out = io["out"]
    RS = 4
    ring = [A.sb(f"d_wr{i}", [128, 16, 512], BF16) for i in range(RS)]
    aT16 = A.sb("d_aT16", [128, 16, 512], BF16)
    mT16 = A.sb("d_mT16", [128, 16, 512], BF16)
    actT = A.sb("d_actT", [128, 44, 512], BF16)
    h = A.sb("d_h", [128, 4, 2048], F32)
    g2 = A.sb("d_g2", [128, 2048], F32)
    gF = A.sb("d_gF", [128, 2048], F32)
    junk = A.sb("d_junk", [128, 2048], BF16)
    ub = [A.sb(f"d_ub{i}", [128, 2048], BF16) for i in range(2)]
    ssq = [A.sb(f"d_ssq{i}", [128, 4], F32) for i in range(4)]
    sg = [A.sb(f"d_sg{i}", [128, 512], F32) for i in range(3)]
    ident = A.sb("d_ident", [128, 128], BF16)
    identf = A.sb("d_identf", [128, 128], F32)
    pb = [A.ps(f"d_pb{i}", [128, 512], F32) for i in range(6)]
    pT = A.ps("d_pT", [128, 16, 128], BF16)

    P.dma("pool", identf[:], io["ident"], "c0", writes=["identf"])
    P.op("dve", lambda e: e.tensor_copy(out=ident[:], in_=identf[:]), reads=["identf"], writes=["ident"])
    P.dma("pool", g2[:], io["norm2_g"].partition_broadcast(128), "c1", writes=["gbc"])
    P.dma("pool", gF[:], io["final_g"].partition_broadcast(128), "c2", writes=["gF"])

    seq = []
    for tt in range(ntile):
        for db in range(4):
            seq.append((io["wo_scr"], wtile_index(2048, 0, db), 16))
        for j in range(11):
            seq.append((io["wgu_scr"], wtile_index(11264, 0, j), 16))
            seq.append((io["wgu_scr"], wtile_index(11264, 0, 11 + j), 16))
        for db in range(4):
            for kg in range(3):
                seq.append((io["wd_scr"], wtile_index(2048, kg, db), 16 if kg < 2 else 12))
    st = {"issued": 0, "used": 0}

    def issue_one():
        n = st["issued"]
        if n >= len(seq):
            return
        scr, tix, nk = seq[n]
        s = n % RS
        P.dma("sp", ring[s][:, 0:nk, :], scr[tix].rearrange("p (kc c) -> p kc c", c=512)[:, 0:nk, :], ("wr", s), writes=[("ring", s)])
        st["issued"] += 1

    def next_w():
        n = st["used"]
        while st["issued"] < min(len(seq), n + RS - 1):
            issue_one()
        st["used"] += 1
        return ring[n % RS], ("ring", n % RS)

    bctr = {"b": 0}

    def bank():
        i = bctr["b"] % 6
        bctr["b"] += 1
        return pb[i], ("pb", i)

    def mix_load(tt):
        t0 = tt * 512

        def _mix_load(e, t0=t0):
            q = e.partition_id() % 4
            tl0 = (t0 % 1024)
            src = mix_gath.rearrange("j (cc p) t -> p j cc t", p=128)[:, bass.ds(q * 2 + t0 // 1024, 1), :, tl0:tl0 + 512]
            return e.dma_start(out=mT16[:].unsqueeze(1), in_=src)
        P.dma_fn("pool", _mix_load, "mixl", writes=["mT16"])

    def x_load(tt, tb):
        r0 = tt * 512 + tb * 128
        P.dma("pool", h[:, tb, :], xs[r0:r0 + 128, :], ("xl", tb), writes=[("h", tb)])

    for tt in range(ntile):
        t0 = tt * 512
        if tt == 0:
            mix_load(0)
            for tb in range(4):
                x_load(0, tb)
        for db in range(4):
            w, wk = next_w()
            for tb in range(4):
                b_, bk = bank()
                for cc in range(16):
                    P.op("pe", lambda e, b_=b_, w=w, cc=cc, tb=tb: e.matmul(
                        b_[:], lhsT=mT16[:, cc, tb * 128:(tb + 1) * 128], rhs=w[:, cc, :], start=(cc == 0), stop=(cc == 15)),
                        reads=["mT16", wk], writes=[bk], inc=(cc == 15))
                hs = h[:, tb, db * 512:(db + 1) * 512]
                P.op("dve", lambda e, hs=hs, b_=b_: e.tensor_tensor(out=hs, in0=b_[:], in1=hs, op=ALU.add),
                     reads=[bk, ("h", tb)], writes=[("h", tb)])
        if tt + 1 < ntile:
            mix_load(tt + 1)
        for tb in range(4):
            norm_transpose_block(P, h[:, tb, :], ("h", tb), g2, junk, ssq[tb], ub[tb % 2], ("ub", tb % 2),
                                 pT, "pT", aT16[:, :, tb * 128:(tb + 1) * 128], "aT16", ident, tb,
                                 evac_eng="act" if tb % 2 else "dve")
        for j in range(11):
            wg, wgk = next_w()
            wu, wuk = next_w()
            for fs in range(4):
                bg, bgk = bank()
                bu, buk = bank()
                for kc in range(16):
                    P.op("pe", lambda e, bg=bg, wg=wg, kc=kc, fs=fs: e.matmul(
                        bg[:], lhsT=wg[:, kc, fs * 128:(fs + 1) * 128], rhs=aT16[:, kc, :], start=(kc == 0), stop=(kc == 15)),
                        reads=["aT16", wgk], writes=[bgk], inc=(kc == 15))
                for kc in range(16):
                    P.op("pe", lambda e, bu=bu, wu=wu, kc=kc, fs=fs: e.matmul(
                        bu[:], lhsT=wu[:, kc, fs * 128:(fs + 1) * 128], rhs=aT16[:, kc, :], start=(kc == 0), stop=(kc == 15)),
                        reads=["aT16", wuk], writes=[buk], inc=(kc == 15))
                fi = j * 4 + fs
                s_ = sg[fi % 3]
                sk = ("sg", fi % 3)
                P.op("act", lambda e, s_=s_, bg=bg: e.activation(out=s_[:], in_=bg[:], func=AF.Silu), reads=[bgk], writes=[sk])
                P.op("dve", lambda e, s_=s_, bu=bu, fi=fi: e.tensor_tensor(out=actT[:, fi, :], in0=bu[:], in1=s_[:], op=ALU.mult),
                     reads=[buk, sk], writes=[("actT", fi)])
        for db in range(4):
            accb = [bank() for _ in range(4)]
            for kg in range(3):
                w, wk = next_w()
                nk = 16 if kg < 2 else 12
                for tb in range(4):
                    b_, bk = accb[tb]
                    for kl in range(nk):
                        fc = kg * 16 + kl
                        first = (fc == 0)
                        last = (fc == 43)
                        P.op("pe", lambda e, b_=b_, w=w, kl=kl, fc=fc, tb=tb, first=first, last=last: e.matmul(
                            b_[:], lhsT=actT[:, fc, tb * 128:(tb + 1) * 128], rhs=w[:, kl, :], start=first, stop=last),
                            reads=[("actT", fc), wk], writes=[bk], inc=(kl == nk - 1))
            for tb in range(4):
                b_, bk = accb[tb]
                hs = h[:, tb, db * 512:(db + 1) * 512]
                P.op("dve", lambda e, hs=hs, b_=b_: e.tensor_tensor(out=hs, in0=b_[:], in1=hs, op=ALU.add),
                     reads=[bk, ("h", tb)], writes=[("h", tb)])
        for tb in range(4):
            sq = ssq[tb]
            hs = h[:, tb, :]
            P.op("act", lambda e, hs=hs, sq=sq: e.activation(out=junk[:], in_=hs, func=AF.Square, accum_out=sq[:, 0:1]),
                 reads=[("h", tb)], writes=["junk", ("ssq", tb)])
            P.op("act", lambda e, sq=sq: e.activation(out=sq[:, 1:2], in_=sq[:, 0:1], func=AF.Sqrt, scale=1.0 / D_MODEL, bias=RMS_EPS),
                 reads=[("ssq", tb)], writes=[("ssq", tb)])
            P.op("dve", lambda e, sq=sq: e.reciprocal(out=sq[:, 2:3], in_=sq[:, 1:2]), reads=[("ssq", tb)], writes=[("ssq", tb)])
            P.op("dve", lambda e, hs=hs, sq=sq: e.scalar_tensor_tensor(out=hs, in0=hs, scalar=sq[:, 2:3], in1=gF[:], op0=ALU.mult, op1=ALU.mult),
                 reads=[("h", tb), ("ssq", tb), "gF"], writes=[("h", tb)])
            P.dma("pool", out[t0 + tb * 128:t0 + (tb + 1) * 128, :], h[:, tb, :], ("outs", tb), reads=[("h", tb)])
            if tt + 1 < ntile:
                x_load(tt + 1, tb)
    P.flush()
    A.close()


V_MIX_R, V_MIX_K, V_MIX_V = 0, 4, 8
V_MIX_W, V_MIX_A, V_MIX_G0, V_MIX_G1 = 12, 13, 14, 15
V_W0, V_A0, V_KK, V_KA, V_RK, V_LNW, V_LNB, V_OMKA = 16, 20, 24, 28, 36, 40, 44, 48
C_SU, C_IU, C_SL, C_ID, C_AVG, C_ONE, C_RST = 0, 64, 128, 192, 256, 320, 384


R_STOP = 99
R_GS = 1


def phase_r(P, nc, io, ntile=SEQ // 512, wspecs=None, gather=False):
    A = Alloc(nc)
    GS = R_GS
    DS = DECAY_SCALE
    wR = A.sb("r_wR", [128, 16, RW_COLS], BF16)
    wstg = A.sb("r_wstg", [128, 2, 2, 512], F32)
    wob = A.sb("r_wob", [128, 2, 2, 512], BF16)
    wst = [wstg[:].rearrange("p a b c -> p (a b c)")[:, 0:RW_COLS]]
    uT = A.sb("r_uT", [128, 16, 512], BF16)
    vec = A.sb("r_vec", [128, 64], F32)
    cst = A.sb("r_cst", [64, 1024], F32)
    wupb = A.sb("r_wupb", [64, 256], BF16)
    aupb = A.sb("r_aupb", [64, 256], BF16)
    gupb0 = A.sb("r_gupb0", [128, 256], BF16)
    gupb1 = A.sb("r_gupb1", [32, 256], BF16)
    Ib = A.sb("r_Ib", [64, 64], BF16)
    mask4 = A.sb("r_mask4", [64, 2, 2, 64], F32)
    zf = [A.sb(f"r_zf{i}", [128, 516], F32) for i in range(2)]
    tmpd = [A.sb(f"r_tmpd{i}", [128, 512], F32) for i in range(2)]
    cr = A.sb("r_cr", [128, 16], F32)
    zs_rkv = A.sb("r_zs", [64, 12, 512], F32)
    zs_w = A.sb("r_zsw", [64, 512], F32)
    zs_a = A.sb("r_zsa", [64, 512], F32)
    zs_g0 = A.sb("r_zsg0", [128, 512], F32)
    zs_g1 = A.sb("r_zsg1", [32, 512], F32)
    lob_w = A.sb("r_lobw", [64, 512], BF16)
    lob_a = A.sb("r_loba", [64, 512], BF16)
    gsb0 = A.sb("r_gsb0", [128, 512], BF16)
    gsb1 = A.sb("r_gsb1", [32, 512], BF16)
    tn = ["cs", "ei", "ev", "ex", "er", "sq", "n", "kk", "km", "bv", "rk"]
    T = {n: A.sb(f"r_t_{n}", [64, 512], F32) for n in tn}
    T["yc"] = T["rk"]
    T["sw"] = zs_w
    T["as"] = zs_a
    RA = A.sb("r_RA", [64, 4, 8, 2, 64], BF16)
    KB = A.sb("r_KB", [64, 4, 8, 2, 64], BF16)
    TR = A.sb("r_TR", [64, 4, 8, 4, 64], BF16)
    PC = A.sb("r_PC", [64, 4, 8], F32)
    bon = A.sb("r_bon", [64, 4, 512], F32)
    gT = A.sb("r_gT", [64, 4, 512], F32)
    yT = A.sb("r_yT", [64, 4, 512], F32)
    mo = A.sb("r_mo", [64, 4, 512], BF16)
    TK = A.sb("r_TK", [64, 4, 4, 64], BF16)
    MS = A.sb("r_MS", [64, 4, 2, 2, 64], BF16)
    Lb = [A.sb(f"r_L{i}", [64, 4, 64], BF16) for i in range(2)]
    Nb = [A.sb(f"r_N{i}", [64, 4, 64], BF16) for i in range(2)]
    Pb = [A.sb(f"r_P{i}", [64, 4, 64], BF16) for i in range(2)]
    ILb = A.sb("r_IL", [64, 4, 64], BF16)
    WT = A.sb("r_WT", [64, 4, 64], BF16)
    Zs = A.sb("r_Zs", [64, 4, 64], BF16)
    Ut = A.sb("r_Ut", [64, 4, 64], F32)
    Ub = A.sb("r_Ub", [64, 4, 64], BF16)
    Hf = A.sb("r_Hf", [64, 4, 64], F32)
    Hd = A.sb("r_Hd", [64, 4, 64], F32)
    Hb = A.sb("r_Hb", [64, 4, 64], BF16)
    pproj = [A.ps(f"r_pp{i}", [128, 512], F32) for i in range(2)]
    ptrs = [A.ps(f"r_ptr{i}", [64, 2, 4, 64], BF16) for i in range(2)]
    pgen = [A.ps(f"r_pg{i}", [64, 512], F32) for i in range(4)]
    paux = pgen[0]
    banks = [(pgen[i], ("pg", i)) for i in range(4)] + [(pproj[i][0:64, :], ("pp", i)) for i in range(2)]
    bctr = {"b": 0}

    def bank():
        b_ = banks[bctr["b"] % len(banks)]
        bctr["b"] += 1
        return b_

    P.dma("sp", vec[:], io["rw_vec"], "c0", writes=["vec"])
    P.dma("sp", cst[:], io["rw_const"], "c1", writes=["cst"])
    P.dma("sp", tmpd[0][:], io["rw_mats"][:, 0:512], "c2", writes=[("tmpd", 0)])
    P.dma("sp", tmpd[1][:], io["rw_mats"][:, 512:1024], "c3", writes=[("tmpd", 1)])
    P.op("dve", lambda e: e.tensor_copy(out=wupb[:], in_=tmpd[0][0:64, 0:256]), reads=[("tmpd", 0)], writes=["wupb"])
    P.op("dve", lambda e: e.tensor_copy(out=aupb[:], in_=tmpd[0][0:64, 256:512]), reads=[("tmpd", 0)], writes=["aupb"])
    P.op("dve", lambda e: e.tensor_copy(out=gupb0[:], in_=tmpd[1][:, 0:256]), reads=[("tmpd", 1)], writes=["gupb0"])
    P.op("dve", lambda e: e.tensor_copy(out=gupb1[:], in_=tmpd[1][0:32, 256:512]), reads=[("tmpd", 1)], writes=["gupb1"])
    P.op("dve", lambda e: e.tensor_copy(out=Ib[:], in_=cst[:, C_ID:C_ID + 64]), reads=["cst"], writes=["Ib"])
    for a in range(2):
        P.op("dve", lambda e, a=a: e.tensor_copy(out=mask4[:, a, 0, :], in_=cst[:, C_SU:C_SU + 64]), reads=["cst"], writes=["mask4"])
        P.op("dve", lambda e, a=a: e.tensor_copy(out=mask4[:, a, 1, :], in_=cst[:, C_IU:C_IU + 64]), reads=["cst"], writes=["mask4"])
    P.op("dve", lambda e: e.tensor_scalar(out=vec[:, V_OMKA:V_OMKA + 4], in0=vec[:, V_KA:V_KA + 4], scalar1=-1.0, scalar2=1.0, op0=ALU.mult, op1=ALU.add),
         reads=["vec"], writes=["vec"])
    for kc in range(16):
        s = 0
        P.dma("sp", wst[s], io["w_in_r"][kc * 128:(kc + 1) * 128, :], ("rwl", s), writes=[("wstg", 0), ("wstg", 1)])
        P.op("pool" if kc % 2 else "dve", lambda e, s=s, kc=kc: e.tensor_copy(out=wR[:, kc, :], in_=wst[s]),
             reads=[("wstg", 0), ("wstg", 1)], writes=["wR"])
    P.op("pool", lambda e: e.memset(cr[:], 0.0), writes=["cr"])
    P.op("pool", lambda e: e.memset(Hf[:], 0.0), writes=[("Hf", 0), ("Hf", 1)])
    P.op("pool", lambda e: e.memset(Hb[:], 0.0), writes=[("Hb", 0), ("Hb", 1)])

    groups = []
    for qi, mv in enumerate((V_MIX_R, V_MIX_K, V_MIX_V)):
        for h in range(4):
            groups.append((qi * 256 + h * 64, 64, zs_rkv[:, qi * 4 + h, :], mv + h, ("zs", qi, h)))
    groups.append((768, 64, zs_w[:], V_MIX_W, "zsw"))
    groups.append((832, 64, zs_a[:], V_MIX_A, "zsa"))
    groups.append((896, 128, zs_g0[:], V_MIX_G0, "zsg0"))
    groups.append((1024, 32, zs_g1[:], V_MIX_G1, "zsg1"))

    ctr = {"pp": 0, "z": 0}

    def vcol(c, m=64):
        return vec[0:m, c:c + 1]

    def make_tile_fns():
        def proj_group(gi):
            (c0, M, dst, mcol, zkey) = groups[gi]
            pp = pproj[ctr["pp"] % 2]
            ppk = ("pp", ctr["pp"] % 2)
            ctr["pp"] += 1
            for kc in range(16):
                P.op("pe", lambda e, pp=pp, M=M, c0=c0, kc=kc: e.matmul(pp[0:M, :], lhsT=wR[:, kc, c0:c0 + M], rhs=uT[:, kc, :],
                                                                     start=(kc == 0), stop=(kc == 15)),
                     reads=["wR", "uT"], writes=[ppk], inc=(kc == 15))
            zi = ctr["z"] % 2
            ctr["z"] += 1
            z = zf[zi]
            zk = ("zf", zi)
            td = tmpd[zi]
            tdk = ("tmpd", zi)
            P.op("act", lambda e, z=z, pp=pp, M=M: e.activation(out=z[0:M, 1:513], in_=pp[0:M, :], func=AF.Copy), reads=[ppk], writes=[zk])
            P.op("pool", lambda e, z=z, M=M, gi=gi: e.tensor_copy(out=z[0:M, 0:1], in_=cr[0:M, gi:gi + 1]), reads=["cr", zk], writes=[zk])
            P.op("dve", lambda e, z=z, td=td, pp=pp, M=M: e.tensor_tensor(out=td[0:M, :], in0=z[0:M, 0:512], in1=pp[0:M, :], op=ALU.subtract),
                 reads=[zk, ppk], writes=[tdk])
            P.op("dve", lambda e, td=td, pp=pp, M=M, dst=dst, mcol=mcol: e.scalar_tensor_tensor(
                out=dst, in0=td[0:M, :], scalar=vec[0:M, mcol:mcol + 1], in1=pp[0:M, :], op0=ALU.mult, op1=ALU.add),
                reads=[tdk, ppk, "vec"], writes=[zkey])
            P.op("pool", lambda e, z=z, M=M, gi=gi: e.tensor_copy(out=cr[0:M, gi:gi + 1], in_=z[0:M, 512:513]), reads=[zk], writes=["cr"])
        def lora_acts():
            P.op("act", lambda e: e.activation(out=lob_w[:], in_=zs_w[:], func=AF.Tanh), reads=["zsw"], writes=["lobw"])
            P.op("pool", lambda e: e.tensor_copy(out=lob_a[:], in_=zs_a[:]), reads=["zsa"], writes=["loba"])
            P.op("act", lambda e: e.activation(out=gsb0[:], in_=zs_g0[:], func=AF.Sigmoid), reads=["zsg0"], writes=["gsb0"])
            P.op("act", lambda e: e.activation(out=gsb1[:], in_=zs_g1[:], func=AF.Sigmoid), reads=["zsg1"], writes=["gsb1"])
        def preproc(h):
            r_ = zs_rkv[:, h, :]
            k_ = zs_rkv[:, 4 + h, :]
            v_ = zs_rkv[:, 8 + h, :]
            rk_, kk_, vk_ = ("zs", 0, h), ("zs", 1, h), ("zs", 2, h)
            hs = slice(h * 64, (h + 1) * 64)
            P.op("pe", lambda e, hs=hs: e.matmul(paux[:], lhsT=wupb[:, hs], rhs=lob_w[:], start=True, stop=True),
                 reads=["wupb", "lobw"], writes=[("pg", 0)])
            P.op("act", lambda e, h=h: e.activation(out=T["sw"][:], in_=paux[:], func=AF.Sigmoid, bias=vcol(V_W0 + h)),
                 reads=[("pg", 0), "vec"], writes=["zsw"])
            P.op("dve", lambda e: e.tensor_tensor_scan(out=T["cs"][:], data0=cst[:, C_RST:C_RST + 512], data1=T["sw"][:], initial=0.0,
                                                       op0=ALU.mult, op1=ALU.add), reads=["zsw", "cst"], writes=["t_cs"])
            P.op("act", lambda e: e.activation(out=T["ei"][:], in_=T["cs"][:], func=AF.Exp, scale=-DS), reads=["t_cs"], writes=["t_ei"])
            P.op("act", lambda e: e.activation(out=T["ev"][:], in_=T["cs"][:], func=AF.Exp, scale=DS), reads=["t_cs"], writes=["t_ev"])
            P.op("pool", lambda e: e.tensor_tensor(out=T["ex"][:], in0=T["cs"][:], in1=T["sw"][:], op=ALU.subtract),
                 reads=["t_cs", "zsw"], writes=["t_ex"])
            P.op("act", lambda e: e.activation(out=T["ex"][:], in_=T["ex"][:], func=AF.Exp, scale=-DS), reads=["t_ex"], writes=["t_ex"])
            P.op("pool", lambda e, h=h: e.tensor_copy(out=PC[:, h, :], in_=T["ei"][:, 63:512:64]), reads=["t_ei"], writes=["PC"])
            P.op("dve", lambda e: e.tensor_tensor(out=T["er"][:].rearrange("j (c s) -> j c s", s=64),
                                                  in0=T["ev"][:].rearrange("j (c s) -> j c s", s=64),
                                                  in1=T["ei"][:].rearrange("j (c s) -> j c s", s=64)[:, :, 63:64].to_broadcast([64, 8, 64]),
                                                  op=ALU.mult), reads=["t_ev", "t_ei"], writes=["t_er"])
            P.op("pe", lambda e, hs=hs: e.matmul(paux[:], lhsT=aupb[:, hs], rhs=lob_a[:], start=True, stop=True),
                 reads=["aupb", "loba"], writes=[("pg", 0)])
            P.op("act", lambda e, h=h: e.activation(out=T["as"][:], in_=paux[:], func=AF.Sigmoid, bias=vcol(V_A0 + h)),
                 reads=[("pg", 0), "vec"], writes=["zsa"])
            P.op("act", lambda e, k_=k_, h=h: e.activation(out=T["sq"][:], in_=k_, func=AF.Square, scale=vcol(V_KK + h)),
                 reads=[kk_, "vec"], writes=["t_sq"])
            P.op("pe", lambda e: e.matmul(paux[:], lhsT=cst[:, C_ONE:C_ONE + 64], rhs=T["sq"][:], start=True, stop=True),
                 reads=["cst", "t_sq"], writes=[("pg", 0)])
            P.op("act", lambda e: e.activation(out=T["n"][:], in_=paux[:], func=AF.Sqrt), reads=[("pg", 0)], writes=["t_n"])
            P.op("dve", lambda e: e.tensor_scalar(out=T["n"][:], in0=T["n"][:], scalar1=1e-12, scalar2=None, op0=ALU.max), reads=["t_n"], writes=["t_n"])
            P.op("dve", lambda e: e.reciprocal(out=T["n"][:], in_=T["n"][:]), reads=["t_n"], writes=["t_n"])
            P.op("dve", lambda e, k_=k_, h=h: e.scalar_tensor_tensor(out=T["kk"][:], in0=k_, scalar=vcol(V_KK + h), in1=T["n"][:],
                                                                   op0=ALU.mult, op1=ALU.mult), reads=[kk_, "vec", "t_n"], writes=["t_kk"])
            P.op("pool", lambda e, h=h: e.tensor_scalar(out=T["km"][:], in0=T["as"][:], scalar1=vcol(V_KA + h), scalar2=vcol(V_OMKA + h),
                                                     op0=ALU.mult, op1=ALU.add), reads=["zsa", "vec"], writes=["t_km"])
            P.op("pool", lambda e, k_=k_: e.tensor_tensor(out=T["km"][:], in0=T["km"][:], in1=k_, op=ALU.mult), reads=["t_km", kk_], writes=["t_km"])
            P.op("pool", lambda e: e.tensor_tensor(out=T["bv"][:], in0=T["kk"][:], in1=T["as"][:], op=ALU.mult), reads=["t_kk", "zsa"], writes=["t_bv"])

            def c3(ap):
                return ap.rearrange("j (c s) -> j c s", s=64)
            P.op("dve", lambda e, h=h: e.scalar_tensor_tensor(out=RA[:, h, :, 0, :], in0=c3(T["kk"][:]), scalar=-1.0, in1=c3(T["ex"][:]),
                                                            op0=ALU.mult, op1=ALU.mult), reads=["t_kk", "t_ex"], writes=[("RA", h)])
            P.op("pool", lambda e, h=h, r_=r_: e.tensor_tensor(out=RA[:, h, :, 1, :], in0=c3(r_), in1=c3(T["ei"][:]), op=ALU.mult),
                 reads=[rk_, "t_ei"], writes=[("RA", h)])
            P.op("dve", lambda e, h=h: e.tensor_tensor(out=KB[:, h, :, 0, :], in0=c3(T["km"][:]), in1=c3(T["ev"][:]), op=ALU.mult),
                 reads=["t_km", "t_ev"], writes=[("KB", h)])
            P.op("pool", lambda e, h=h: e.tensor_tensor(out=KB[:, h, :, 1, :], in0=c3(T["bv"][:]), in1=c3(T["ev"][:]), op=ALU.mult),
                 reads=["t_bv", "t_ev"], writes=[("KB", h)])
            P.op("dve", lambda e, h=h: e.tensor_tensor(out=TR[:, h, :, 0, :], in0=c3(T["km"][:]), in1=c3(T["er"][:]), op=ALU.mult),
                 reads=["t_km", "t_er"], writes=[("TR", h)])
            P.op("pool", lambda e, h=h: e.tensor_tensor(out=TR[:, h, :, 1, :], in0=c3(T["bv"][:]), in1=c3(T["er"][:]), op=ALU.mult),
                 reads=["t_bv", "t_er"], writes=[("TR", h)])
            P.op("act", lambda e, h=h, v_=v_: e.activation(out=TR[:, h, :, 2, :], in_=c3(v_), func=AF.Copy), reads=[vk_], writes=[("TR", h)])
            P.op("pool", lambda e, h=h: e.tensor_copy(out=TR[:, h, :, 3, :], in_=RA[:, h, :, 0, :]), reads=[("RA", h)], writes=[("TR", h)])
            P.op("dve", lambda e, h=h, r_=r_: e.scalar_tensor_tensor(out=T["rk"][:], in0=r_, scalar=vcol(V_RK + h), in1=T["km"][:],
                                                                   op0=ALU.mult, op1=ALU.mult), reads=[rk_, "vec", "t_km"], writes=["t_rk"])
            P.op("pe", lambda e: e.matmul(paux[:], lhsT=cst[:, C_ONE:C_ONE + 64], rhs=T["rk"][:], start=True, stop=True),
                 reads=["cst", "t_rk"], writes=[("pg", 0)])
            P.op("dve", lambda e, h=h, v_=v_: e.tensor_tensor(out=bon[:, h, :], in0=paux[:], in1=v_, op=ALU.mult), reads=[("pg", 0), vk_], writes=[("bon", h)])
            P.op("pe", lambda e, hs=hs: e.matmul(paux[:], lhsT=gupb0[:, hs], rhs=gsb0[:], start=True, stop=False),
                 reads=["gupb0", "gsb0"], writes=[("pg", 0)], inc=False)
            P.op("pe", lambda e, hs=hs: e.matmul(paux[:], lhsT=gupb1[:, hs], rhs=gsb1[:], start=False, stop=True),
                 reads=["gupb1", "gsb1"], writes=[("pg", 0)])
            P.op("act", lambda e, h=h: e.activation(out=gT[:, h, :], in_=paux[:], func=AF.Copy), reads=[("pg", 0)], writes=[("gT", h)])

        def chunks():
            def chunk_group(c, g):
                hh = tuple(range(GS * g, GS * g + GS))
                hs2 = slice(GS * g, GS * g + GS)
                K_ = lambda n, *a: (n, g) + a
                if GS == 2:
                    pt, ptk = ptrs[g], ("ptr", g)
                else:
                    pt, ptk = ptrs[g // 2][:, g % 2:g % 2 + 1], ("ptr", g // 2)
                for hi, h in enumerate(hh):
                    for q in range(4):
                        P.op("pe", lambda e, hi=hi, h=h, q=q: e.transpose(out=pt[:, hi, q, :], in_=TR[:, h, c, q, :], identity=Ib[:]),
                             reads=[("TR", h), "Ib"], writes=[ptk], inc=(hi == GS - 1 and q == 3))
                P.op("act", lambda e: e.activation(out=TK[:, hs2], in_=pt[:], func=AF.Copy), reads=[ptk], writes=[K_("TK")])
                yield
                b1, b1k = bank()
                v1 = b1[:, 0:GS * 256].rearrange("s (h a x) -> s h a x", h=GS, a=2)
                for hi, h in enumerate(hh):
                    rhs = RA[:, h, c, :, :].rearrange("j a t -> j (a t)")
                    P.op("pe", lambda e, hi=hi, h=h, rhs=rhs: e.matmul(v1[:, hi, 0, :], lhsT=KB[:, h, c, 1, :], rhs=rhs, start=True, stop=True),
                         reads=[("KB", h), ("RA", h)], writes=[b1k], inc=False)
                    P.op("pe", lambda e, hi=hi, h=h, rhs=rhs: e.matmul(v1[:, hi, 1, :], lhsT=KB[:, h, c, 0, :], rhs=rhs, start=True, stop=True),
                         reads=[("KB", h), ("RA", h)], writes=[b1k], inc=(hi == GS - 1))
                b2, b2k = bank()
                v2 = b2[:, 0:GS * 64].rearrange("s (h t) -> s h t", h=GS)
                for hi, h in enumerate(hh):
                    P.op("pe", lambda e, hi=hi, h=h: e.matmul(v2[:, hi, :], lhsT=RA[:, h, c, 0, :], rhs=KB[:, h, c, 1, :], start=True, stop=True),
                         reads=[("KB", h), ("RA", h)], writes=[b2k], inc=(hi == GS - 1))
                P.op("dve", lambda e: e.tensor_tensor(
                    out=MS[:, hs2].rearrange("s h a b t -> s h (a b t)"), in0=v1.rearrange("s h a x -> s h (a x)"),
                    in1=mask4[:].rearrange("s a b t -> s (a b t)").unsqueeze(1).to_broadcast([64, GS, 256]), op=ALU.mult),
                    reads=[b1k, "mask4"], writes=[K_("MS")])
                P.op("dve", lambda e: e.tensor_tensor(out=Lb[0][:, hs2], in0=v2, in1=cst[:, C_SL:C_SL + 64].unsqueeze(1).to_broadcast([64, GS, 64]),
                                                      op=ALU.mult), reads=[b2k, "cst"], writes=[K_("L", 0)])
                P.op("pool", lambda e: e.tensor_tensor(out=Pb[0][:, hs2], in0=MS[:, hs2, 0, 0, :], in1=cst[:, C_ID:C_ID + 64].unsqueeze(1).to_broadcast([64, GS, 64]),
                                                       op=ALU.add), reads=[K_("MS"), "cst"], writes=[K_("P", 0)])
                yield
                for k in range(5):
                    li, lo = k % 2, (k + 1) % 2
                    Nk = (lambda h: MS[:, h, 0, 0, :]) if k == 0 else (lambda h, li=li: Nb[li][:, h, :])
                    nkey = K_("MS") if k == 0 else K_("N", li)
                    bl, blk = bank()
                    vl = bl[:, 0:GS * 64].rearrange("s (h t) -> s h t", h=GS)
                    for hi, h in enumerate(hh):
                        P.op("pe", lambda e, hi=hi, h=h, Nk=Nk, li=li, vl=vl: e.matmul(vl[:, hi, :], lhsT=Nk(h), rhs=Lb[li][:, h, :], start=True, stop=True),
                             reads=[nkey, K_("L", li)], writes=[blk], inc=(hi == GS - 1))
                    if k < 4:
                        bn, bnk = bank()
                        vn = bn[:, 0:GS * 64].rearrange("s (h t) -> s h t", h=GS)
                        for hi, h in enumerate(hh):
                            P.op("pe", lambda e, hi=hi, h=h, Nk=Nk, li=li, vn=vn: e.matmul(vn[:, hi, :], lhsT=Lb[li][:, h, :], rhs=Nk(h), start=True, stop=True),
                                 reads=[nkey, K_("L", li)], writes=[bnk], inc=(hi == GS - 1))
                    P.op("act", lambda e, lo=lo, vl=vl: e.activation(out=Lb[lo][:, hs2], in_=vl, func=AF.Copy), reads=[blk], writes=[K_("L", lo)])
                    if k < 4:
                        P.op("dve", lambda e, lo=lo, vn=vn: e.tensor_copy(out=Nb[lo][:, hs2], in_=vn), reads=[bnk], writes=[K_("N", lo)])
                    yield
                    bp, bpk = bank()
                    vp = bp[:, 0:GS * 64].rearrange("s (h t) -> s h t", h=GS)
                    for hi, h in enumerate(hh):
                        P.op("pe", lambda e, hi=hi, h=h, li=li, lo=lo, vp=vp: e.matmul(vp[:, hi, :], lhsT=Lb[lo][:, h, :], rhs=Pb[li][:, h, :], start=True, stop=False),
                             reads=[K_("L", lo), K_("P", li)], writes=[bpk], inc=False)
                        P.op("pe", lambda e, hi=hi, h=h, li=li, vp=vp: e.matmul(vp[:, hi, :], lhsT=Ib[:], rhs=Pb[li][:, h, :], start=False, stop=True),
                             reads=["Ib", K_("P", li)], writes=[bpk], inc=(hi == GS - 1))
                    if k % 2 == 0:
                        P.op("dve", lambda e, lo=lo, vp=vp: e.tensor_copy(out=Pb[lo][:, hs2], in_=vp), reads=[bpk], writes=[K_("P", lo)])
                    else:
                        P.op("act", lambda e, lo=lo, vp=vp: e.activation(out=Pb[lo][:, hs2], in_=vp, func=AF.Copy), reads=[bpk], writes=[K_("P", lo)])
                    yield
                TT = Pb[1]
                ttk = K_("P", 1)
                bw, bwk = bank()
                vw = bw[:, 0:GS * 64].rearrange("s (h t) -> s h t", h=GS)
                for hi, h in enumerate(hh):
                    P.op("pe", lambda e, hi=hi, h=h: e.matmul(vw[:, hi, :], lhsT=TK[:, h, 3, :], rhs=TT[:, h, :], start=True, stop=True),
                         reads=[K_("TK"), ttk], writes=[bwk], inc=(hi == GS - 1))
                bz, bzk = bank()
                vz = bz[:, 0:GS * 64].rearrange("s (h t) -> s h t", h=GS)
                for hi, h in enumerate(hh):
                    P.op("pe", lambda e, hi=hi, h=h: e.matmul(vz[:, hi, :], lhsT=MS[:, h, 1, 0, :], rhs=TK[:, h, 2, :], start=True, stop=True),
                         reads=[K_("MS"), K_("TK")], writes=[bzk], inc=(hi == GS - 1))
                P.op("act", lambda e: e.activation(out=WT[:, hs2], in_=vw, func=AF.Copy), reads=[bwk], writes=[K_("WT")])
                P.op("dve", lambda e: e.tensor_copy(out=Zs[:, hs2], in_=vz), reads=[bzk], writes=[K_("Zs")])
                yield
                bu, buk = bank()
                vu = bu[:, 0:GS * 64].rearrange("s (h t) -> s h t", h=GS)
                for hi, h in enumerate(hh):
                    P.op("pe", lambda e, hi=hi, h=h: e.matmul(vu[:, hi, :], lhsT=TT[:, h, :], rhs=Zs[:, h, :], start=True, stop=True),
                         reads=[ttk, K_("Zs")], writes=[buk], inc=(hi == GS - 1))
                P.op("act", lambda e: e.activation(out=Ut[:, hs2], in_=vu, func=AF.Copy), reads=[buk], writes=[K_("Ut")])
                P.op("pool", lambda e: e.tensor_tensor(out=Hd[:, hs2], in0=Hf[:, hs2], in1=PC[:, hs2, c:c + 1].to_broadcast([64, GS, 64]), op=ALU.mult),
                     reads=[K_("Hf"), "PC"], writes=[K_("Hd")])
                yield
                b5, b5k = bank()
                v5 = b5[:, 0:GS * 64].rearrange("s (h t) -> s h t", h=GS)
                for hi, h in enumerate(hh):
                    P.op("pe", lambda e, hi=hi, h=h: e.matmul(v5[:, hi, :], lhsT=WT[:, h, :], rhs=Hb[:, h, :], start=True, stop=True),
                         reads=[K_("WT"), K_("Hb")], writes=[b5k], inc=(hi == GS - 1))
                P.op("dve", lambda e: e.tensor_tensor(out=Ub[:, hs2], in0=v5, in1=Ut[:, hs2], op=ALU.add), reads=[b5k, K_("Ut")], writes=[K_("Ub")])
                yield
                by, byk = bank()
                vy = by[:, 0:GS * 64].rearrange("s (h t) -> s h t", h=GS)
                for hi, h in enumerate(hh):
                    P.op("pe", lambda e, hi=hi, h=h: e.matmul(vy[:, hi, :], lhsT=Hb[:, h, :], rhs=RA[:, h, c, 1, :], start=True, stop=False),
                         reads=[K_("Hb"), ("RA", h)], writes=[byk], inc=False)
                    P.op("pe", lambda e, hi=hi, h=h: e.matmul(vy[:, hi, :], lhsT=Ub[:, h, :], rhs=MS[:, h, 0, 1, :], start=False, stop=False),
                         reads=[K_("Ub"), K_("MS")], writes=[byk], inc=False)
                    P.op("pe", lambda e, hi=hi, h=h: e.matmul(vy[:, hi, :], lhsT=TK[:, h, 2, :], rhs=MS[:, h, 1, 1, :], start=False, stop=True),
                         reads=[K_("TK"), K_("MS")], writes=[byk], inc=(hi == GS - 1))
                bh, bhk = bank()
                vh = bh[:, 0:GS * 64].rearrange("s (h t) -> s h t", h=GS)
                for hi, h in enumerate(hh):
                    P.op("pe", lambda e, hi=hi, h=h: e.matmul(vh[:, hi, :], lhsT=TK[:, h, 1, :], rhs=Ub[:, h, :], start=True, stop=False),
                         reads=[K_("TK"), K_("Ub")], writes=[bhk], inc=False)
                    P.op("pe", lambda e, hi=hi, h=h: e.matmul(vh[:, hi, :], lhsT=TK[:, h, 0, :], rhs=TK[:, h, 2, :], start=False, stop=True),
                         reads=[K_("TK")], writes=[bhk], inc=(hi == GS - 1))
                P.op("dve", lambda e: e.tensor_tensor(out=Hb[:, hs2], in0=vh, in1=Hd[:, hs2], op=ALU.add), reads=[bhk, K_("Hd")], writes=[K_("Hb")])
                P.op("dve", lambda e: e.tensor_tensor(out=Hf[:, hs2], in0=vh, in1=Hd[:, hs2], op=ALU.add), reads=[bhk, K_("Hd")], writes=[K_("Hf")])
                P.op("act", lambda e: e.activation(out=yT[:, hs2, c * 64:(c + 1) * 64], in_=vy, func=AF.Copy), reads=[byk], writes=[K_("yT")])
                yield

            for c in range(8):
                w_steps(per_chunk)
                gens = [chunk_group(c, g) for g in range(4 // GS)]
                live = [True] * len(gens)
                while any(live):
                    for gi_, gn in enumerate(gens):
                        if live[gi_]:
                            try:
                                next(gn)
                            except StopIteration:
                                live[gi_] = False

        def output_head(h):
            y_ = yT[:, h, :]
            P.op("pe", lambda e, y_=y_: e.matmul(paux[:], lhsT=cst[:, C_AVG:C_AVG + 64], rhs=y_, start=True, stop=True),
                 reads=["cst", ("yT", 0), ("yT", 1)], writes=[("pg", 0)])
            P.op("dve", lambda e, y_=y_: e.tensor_tensor(out=T["yc"][:], in0=y_, in1=paux[:], op=ALU.subtract), reads=[("yT", 0), ("yT", 1), ("pg", 0)], writes=["t_rk"])
            P.op("act", lambda e: e.activation(out=T["sq"][:], in_=T["yc"][:], func=AF.Square), reads=["t_rk"], writes=["t_sq"])
            P.op("pe", lambda e: e.matmul(paux[:], lhsT=cst[:, C_AVG:C_AVG + 64], rhs=T["sq"][:], start=True, stop=True),
                 reads=["cst", "t_sq"], writes=[("pg", 0)])
            P.op("act", lambda e: e.activation(out=T["n"][:], in_=paux[:], func=AF.Sqrt, bias=GN_EPS), reads=[("pg", 0)], writes=["t_n"])
            P.op("dve", lambda e: e.reciprocal(out=T["n"][:], in_=T["n"][:]), reads=["t_n"], writes=["t_n"])
            P.op("pool", lambda e: e.tensor_tensor(out=T["yc"][:], in0=T["yc"][:], in1=T["n"][:], op=ALU.mult), reads=["t_rk", "t_n"], writes=["t_rk"])
            P.op("pool", lambda e, h=h: e.tensor_scalar(out=T["yc"][:], in0=T["yc"][:], scalar1=vcol(V_LNW + h), scalar2=vcol(V_LNB + h),
                                                     op0=ALU.mult, op1=ALU.add), reads=["t_rk", "vec"], writes=["t_rk"])
            P.op("pool", lambda e, h=h: e.tensor_tensor(out=T["yc"][:], in0=T["yc"][:], in1=bon[:, h, :], op=ALU.add), reads=["t_rk", ("bon", h)], writes=["t_rk"])
            P.op("dve", lambda e, h=h: e.tensor_tensor(out=mo[:, h, :], in0=T["yc"][:], in1=gT[:, h, :], op=ALU.mult),
                 reads=["t_rk", ("gT", h)], writes=["mo"])
        def store_mo(t):
            P.dma("sp", io["mix_loc"][t // 2, 256:512, (t % 2) * 512:(t % 2) * 512 + 512].rearrange("(h i) t -> i h t", h=4), mo[:], "rmo",
                  reads=["mo"], writes=[("mixslab", t // 2)])
        return proj_group, lora_acts, preproc, chunks, output_head, store_mo

    proj_group, lora_acts, preproc, chunks, output_head, store_mo = make_tile_fns()

    def gather_slab(j):
        P.custom("pool", lambda e, j=j: e.collective_compute(
            "AllGather", ALU.bypass, replica_groups=[[0, 1, 2, 3], [4, 5, 6, 7]],
            ins=[io["mix_loc"][j].opt()], outs=[io["mix_gath"][j].opt()]),
            key="cc", amt=1, reads=[("mixslab", j)], writes=[])

    witems = []
    for (src, dst, K, N) in (wspecs or []):
        nkc = K // 128
        for kg in range((nkc + 15) // 16):
            nk = min(16, nkc - kg * 16)
            for nb in range(N // 512):
                tix = wtile_index(N, kg, nb)
                for h0 in range(0, nk, 2):
                    r0 = (kg * 16 + h0) * 128
                    witems.append((src[r0:r0 + 256, nb * 512:(nb + 1) * 512].rearrange("(kc p) c -> p kc c", p=128),
                                   dst[tix].rearrange("p (kc c) -> p kc c", c=512)[:, h0:h0 + 2, :]))
    wst_ = {"i": 0}

    def w_load(i):
        if i < len(witems):
            P.dma("sp", wstg[:, i % 2], witems[i][0], ("wl", i % 2), writes=[("wstg", i % 2)])

    def w_steps(n):
        for _ in range(n):
            i = wst_["i"]
            if i >= len(witems):
                return
            if i == 0:
                w_load(0)
            w_load(i + 1)
            sl = i % 2
            P.op("pool", lambda e, sl=sl: e.tensor_copy(out=wob[:, sl], in_=wstg[:, sl]), reads=[("wstg", sl)], writes=[("wob", sl)])
            P.dma("sp", witems[i][1], wob[:, sl], ("ws", sl), reads=[("wob", sl)])
            wst_["i"] += 1
    per_chunk = -(-len(witems) // max(1, (ntile * 8 - 8))) if witems else 0
    for t in range(ntile):
        P.dma("sp", uT[:], io["uT_scr"][t].rearrange("p (kc t) -> p kc t", t=512), "utl", writes=["uT"])
        for gi in (12, 13, 14, 15):
            proj_group(gi)
        lora_acts()
        for h in range(4):
            if t > 0:
                output_head(h)
            for gi in (h, 4 + h, 8 + h):
                proj_group(gi)
            if h >= 1:
                preproc(h - 1)
        if t > 0:
            store_mo(t - 1)
        preproc(3)
        if gather and t > 0 and (t - 1) % 2 == 1:
            gather_slab((t - 1) // 2)
        chunks()
    for h in range(4):
        output_head(h)
    store_mo(ntile - 1)
    if gather:
        gather_slab((ntile - 1) // 2)
    w_steps(len(witems))
    P.flush()
    A.close()


def phase_r2(P, nc, io, ntile=SEQ // 512, wspecs=None, gather=False):
    A = Alloc(nc)
    UW = 256
    NCU = UW // 64
    nunit = ntile * 2
    P0 = P
    DS = DECAY_SCALE
    wR = A.sb("r_wR", [128, 16, RW_COLS], BF16)
    wstg = A.sb("r_wstg", [128, 2, 2, 512], F32)
    wob = A.sb("r_wob", [128, 2, 2, 512], BF16)
    wst = [wstg[:].rearrange("p a b c -> p (a b c)")[:, 0:RW_COLS]]
    uT2 = [A.sb("r_uT", [128, 16, 512], BF16)] * 2
    vec = A.sb("r_vec", [128, 64], F32)
    cst = A.sb("r_cst", [64, 1024], F32)
    wupb = A.sb("r_wupb", [64, 256], BF16)
    aupb = A.sb("r_aupb", [64, 256], BF16)
    gupb0 = A.sb("r_gupb0", [128, 256], BF16)
    gupb1 = A.sb("r_gupb1", [32, 256], BF16)
    Ib = A.sb("r_Ib", [64, 64], BF16)
    mask4 = A.sb("r_mask4", [64, 2, 2, 64], F32)
    zf = [A.sb(f"r_zf{i}", [128, 516], F32) for i in range(2)]
    tmpd = [A.sb(f"r_tmpd{i}", [128, 512], F32) for i in range(2)]
    cr = A.sb("r_cr", [128, 16], F32)
    zs_rkv2 = [A.sb("r_zs%d" % i, [64, 12, UW], F32) for i in range(2)]
    zs_w2 = [A.sb("r_zsw%d" % i, [64, UW], F32) for i in range(2)]
    zs_a2 = [A.sb("r_zsa%d" % i, [64, UW], F32) for i in range(2)]
    zs_g02 = [A.sb("r_zsg0%d" % i, [128, UW], F32) for i in range(2)]
    zs_g12 = [A.sb("r_zsg1%d" % i, [32, UW], F32) for i in range(2)]
    lob_w2 = [A.sb("r_lobw%d" % i, [64, UW], BF16) for i in range(2)]
    lob_a2 = [A.sb("r_loba%d" % i, [64, UW], BF16) for i in range(2)]
    gsb02 = [A.sb("r_gsb0%d" % i, [128, UW], BF16) for i in range(2)]
    gsb12 = [A.sb("r_gsb1%d" % i, [32, UW], BF16) for i in range(2)]
    tn = ["cs", "ei", "ev", "ex", "er", "sq", "n", "kk", "km", "bv", "rk"]
    T0 = {n: A.sb(f"r_t_{n}", [64, UW], F32) for n in tn}
    To = {n: A.sb(f"r_o_{n}", [64, UW], F32) for n in ("sq", "n", "yc")}
    RA2 = [A.sb("r_RA%d" % i, [64, 4, NCU, 2, 64], BF16) for i in range(2)]
    KB2 = [A.sb("r_KB%d" % i, [64, 4, NCU, 2, 64], BF16) for i in range(2)]
    TR2 = [A.sb("r_TR%d" % i, [64, 4, NCU, 4, 64], BF16) for i in range(2)]
    PC2 = [A.sb("r_PC%d" % i, [64, 4, NCU], F32) for i in range(2)]
    bon2 = [A.sb("r_bon%d" % i, [64, 4, UW], F32) for i in range(2)]
    gT2 = [A.sb("r_gT%d" % i, [64, 4, UW], F32) for i in range(2)]
    yT2 = [A.sb("r_yT%d" % i, [64, 4, UW], F32) for i in range(2)]
    mo2 = [A.sb("r_mo%d" % i, [64, 4, UW], BF16) for i in range(2)]
    TK = A.sb("r_TK", [64, 4, 4, 64], BF16)
    MS = A.sb("r_MS", [64, 4, 2, 2, 64], BF16)
    Lb = [A.sb(f"r_L{i}", [64, 4, 64], BF16) for i in range(2)]
    Nb = [A.sb(f"r_N{i}", [64, 4, 64], BF16) for i in range(2)]
    Pb = [A.sb(f"r_P{i}", [64, 4, 64], BF16) for i in range(2)]
    ILb = A.sb("r_IL", [64, 4, 64], BF16)
    WT = A.sb("r_WT", [64, 4, 64], BF16)
    Zs = A.sb("r_Zs", [64, 4, 64], BF16)
    Ut = A.sb("r_Ut", [64, 4, 64], F32)
    Ub = A.sb("r_Ub", [64, 4, 64], BF16)
    Hf = A.sb("r_Hf", [64, 4, 64], F32)
    Hd = A.sb("r_Hd", [64, 4, 64], F32)
    Hb = A.sb("r_Hb", [64, 4, 64], BF16)
    pproj = [A.ps(f"r_pp{i}", [128, 512], F32) for i in range(2)]
    ptr1 = A.ps("r_ptr", [64, 2, 2, 4, 64], BF16)
    ptrs = [ptr1[:, 0], ptr1[:, 1]]
    pgen = [A.ps(f"r_pg{i}", [64, 512], F32) for i in range(5)]
    paux = pgen[0]
    banks = [(pgen[i], ("pg", i)) for i in range(1, 5)]
    bctr = {"b": 0}

    def bank():
        b_ = banks[bctr["b"] % len(banks)]
        bctr["b"] += 1
        return b_

    P.dma("sp", vec[:], io["rw_vec"], "c0", writes=["vec"])
    P.dma("sp", cst[:], io["rw_const"], "c1", writes=["cst"])
    P.dma("sp", tmpd[0][:], io["rw_mats"][:, 0:512], "c2", writes=[("tmpd", 0)])
    P.dma("sp", tmpd[1][:], io["rw_mats"][:, 512:1024], "c3", writes=[("tmpd", 1)])
    P.op("dve", lambda e: e.tensor_copy(out=wupb[:], in_=tmpd[0][0:64, 0:256]), reads=[("tmpd", 0)], writes=["wupb"])
    P.op("dve", lambda e: e.tensor_copy(out=aupb[:], in_=tmpd[0][0:64, 256:512]), reads=[("tmpd", 0)], writes=["aupb"])
    P.op("dve", lambda e: e.tensor_copy(out=gupb0[:], in_=tmpd[1][:, 0:256]), reads=[("tmpd", 1)], writes=["gupb0"])
    P.op("dve", lambda e: e.tensor_copy(out=gupb1[:], in_=tmpd[1][0:32, 256:512]), reads=[("tmpd", 1)], writes=["gupb1"])
    P.op("dve", lambda e: e.tensor_copy(out=Ib[:], in_=cst[:, C_ID:C_ID + 64]), reads=["cst"], writes=["Ib"])
    for a in range(2):
        P.op("dve", lambda e, a=a: e.tensor_copy(out=mask4[:, a, 0, :], in_=cst[:, C_SU:C_SU + 64]), reads=["cst"], writes=["mask4"])
        P.op("dve", lambda e, a=a: e.tensor_copy(out=mask4[:, a, 1, :], in_=cst[:, C_IU:C_IU + 64]), reads=["cst"], writes=["mask4"])
    P.op("dve", lambda e: e.tensor_scalar(out=vec[:, V_OMKA:V_OMKA + 4], in0=vec[:, V_KA:V_KA + 4], scalar1=-1.0, scalar2=1.0, op0=ALU.mult, op1=ALU.add),
         reads=["vec"], writes=["vec"])
    for kc in range(16):
        s = 0
        P.dma("sp", wst[s], io["w_in_r"][kc * 128:(kc + 1) * 128, :], ("rwl", s), writes=[("wstg", 0), ("wstg", 1)])
        P.op("pool" if kc % 2 else "dve", lambda e, s=s, kc=kc: e.tensor_copy(out=wR[:, kc, :], in_=wst[s]),
             reads=[("wstg", 0), ("wstg", 1)], writes=["wR"])
    P.op("pool", lambda e: e.memset(cr[:], 0.0), writes=["cr"])
    P.op("pool", lambda e: e.memset(Hf[:], 0.0), writes=[("Hf", 0), ("Hf", 1)])
    P.op("pool", lambda e: e.memset(Hb[:], 0.0), writes=[("Hb", 0), ("Hb", 1)])

    def mk_groups(zs_rkv, zs_w, zs_a, zs_g0, zs_g1):
        groups = []
        for qi, mv in enumerate((V_MIX_R, V_MIX_K, V_MIX_V)):
            for h in range(4):
                groups.append((qi * 256 + h * 64, 64, zs_rkv[:, qi * 4 + h, :], mv + h, ("zs", qi, h)))
        groups.append((768, 64, zs_w[:], V_MIX_W, "zsw"))
        groups.append((832, 64, zs_a[:], V_MIX_A, "zsa"))
        groups.append((896, 128, zs_g0[:], V_MIX_G0, "zsg0"))
        groups.append((1024, 32, zs_g1[:], V_MIX_G1, "zsg1"))
        return groups

    ctr = {"pp": 0, "z": 0}

    def vcol(c, m=64):
        return vec[0:m, c:c + 1]

    PZ_NAMES = {"zs", "zsw", "zsa", "zsg0", "zsg1", "lobw", "loba", "gsb0", "gsb1", "RA", "KB", "TR", "PC", "bon", "gT", "yT", "mo"}

    class PX:
        def __init__(self, pz):
            self.pz = pz
            self.buf = None

        def _k(self, k):
            n = k[0] if isinstance(k, tuple) else k
            return (k, "pz", self.pz) if n in PZ_NAMES else k

        def op(self, eng, fn, reads=(), writes=(), **kw):
            rd, wr = [self._k(k) for k in reads], [self._k(k) for k in writes]
            if self.buf is not None:
                self.buf.append(lambda: P0.op(eng, fn, reads=rd, writes=wr, **kw))
                return None
            return P0.op(eng, fn, reads=rd, writes=wr, **kw)

        def dma(self, q, out, in_, slot, reads=(), writes=(), **kw):
            rd, wr = [self._k(k) for k in reads], [self._k(k) for k in writes]
            if self.buf is not None:
                self.buf.append(lambda: P0.dma(q, out, in_, slot, reads=rd, writes=wr, **kw))
                return None
            return P0.dma(q, out, in_, slot, reads=rd, writes=wr, **kw)

    def make_tile_fns(pz):
        P = PX(pz)
        zs_rkv, zs_w, zs_a, zs_g0, zs_g1 = zs_rkv2[pz], zs_w2[pz], zs_a2[pz], zs_g02[pz], zs_g12[pz]
        lob_w, lob_a, gsb0, gsb1 = lob_w2[pz], lob_a2[pz], gsb02[pz], gsb12[pz]
        RA, KB, TR, PC, bon, gT, yT, mo = RA2[pz], KB2[pz], TR2[pz], PC2[pz], bon2[pz], gT2[pz], yT2[pz], mo2[pz]
        T = dict(T0)
        T["sw"] = zs_w
        T["as"] = zs_a
        groups = mk_groups(zs_rkv, zs_w, zs_a, zs_g0, zs_g1)
        uts = {"ap": None}
        def proj_group(gi):
            (c0, M, dst, mcol, zkey) = groups[gi]
            pp = pproj[ctr["pp"] % 2]
            ppk = ("pp", ctr["pp"] % 2)
            ctr["pp"] += 1
            for kc in range(16):
                P.op("pe", lambda e, pp=pp, M=M, c0=c0, kc=kc, ua=uts["ap"]: e.matmul(pp[0:M, 0:UW], lhsT=wR[:, kc, c0:c0 + M], rhs=ua[:, kc, :],
                                                                     start=(kc == 0), stop=(kc == 15)),
                     reads=["wR", uts["key"]], writes=[ppk], inc=(kc == 15))
            zi = ctr["z"] % 2
            ctr["z"] += 1
            z = zf[zi]
            zk = ("zf", zi)
            td = tmpd[zi]
            tdk = ("tmpd", zi)
            P.op("act", lambda e, z=z, pp=pp, M=M: e.activation(out=z[0:M, 1:UW + 1], in_=pp[0:M, 0:UW], func=AF.Copy), reads=[ppk], writes=[zk])
            P.op("pool", lambda e, z=z, M=M, gi=gi: e.tensor_copy(out=z[0:M, 0:1], in_=cr[0:M, gi:gi + 1]), reads=["cr", zk], writes=[zk])
            P.op("dve", lambda e, z=z, td=td, pp=pp, M=M: e.tensor_tensor(out=td[0:M, 0:UW], in0=z[0:M, 0:UW], in1=pp[0:M, 0:UW], op=ALU.subtract),
                 reads=[zk, ppk], writes=[tdk])
            P.op("dve", lambda e, td=td, pp=pp, M=M, dst=dst, mcol=mcol: e.scalar_tensor_tensor(
                out=dst, in0=td[0:M, 0:UW], scalar=vec[0:M, mcol:mcol + 1], in1=pp[0:M, 0:UW], op0=ALU.mult, op1=ALU.add),
                reads=[tdk, ppk, "vec"], writes=[zkey])
            P.op("pool", lambda e, z=z, M=M, gi=gi: e.tensor_copy(out=cr[0:M, gi:gi + 1], in_=z[0:M, UW:UW + 1]), reads=[zk], writes=["cr"])
        def lora_acts():
            P.op("act", lambda e: e.activation(out=lob_w[:], in_=zs_w[:], func=AF.Tanh), reads=["zsw"], writes=["lobw"])
            P.op("pool", lambda e: e.tensor_copy(out=lob_a[:], in_=zs_a[:]), reads=["zsa"], writes=["loba"])
            P.op("act", lambda e: e.activation(out=gsb0[:], in_=zs_g0[:], func=AF.Sigmoid), reads=["zsg0"], writes=["gsb0"])
            P.op("act", lambda e: e.activation(out=gsb1[:], in_=zs_g1[:], func=AF.Sigmoid), reads=["zsg1"], writes=["gsb1"])
        def preproc(h):
            r_ = zs_rkv[:, h, :]
            k_ = zs_rkv[:, 4 + h, :]
            v_ = zs_rkv[:, 8 + h, :]
            rk_, kk_, vk_ = ("zs", 0, h), ("zs", 1, h), ("zs", 2, h)
            hs = slice(h * 64, (h + 1) * 64)
            P.op("pe", lambda e, hs=hs: e.matmul(paux[:, 0:UW], lhsT=wupb[:, hs], rhs=lob_w[:], start=True, stop=True),
                 reads=["wupb", "lobw"], writes=[("pg", 0)])
            P.op("act", lambda e, h=h: e.activation(out=T["sw"][:], in_=paux[:, 0:UW], func=AF.Sigmoid, bias=vcol(V_W0 + h)),
                 reads=[("pg", 0), "vec"], writes=["zsw"])
            P.op("dve", lambda e: e.tensor_tensor_scan(out=T["cs"][:], data0=cst[:, C_RST:C_RST + UW], data1=T["sw"][:], initial=0.0,
                                                       op0=ALU.mult, op1=ALU.add), reads=["zsw", "cst"], writes=["t_cs"])
            P.op("act", lambda e: e.activation(out=T["ei"][:], in_=T["cs"][:], func=AF.Exp, scale=-DS), reads=["t_cs"], writes=["t_ei"])
            P.op("act", lambda e: e.activation(out=T["ev"][:], in_=T["cs"][:], func=AF.Exp, scale=DS), reads=["t_cs"], writes=["t_ev"])
            P.op("pool", lambda e: e.tensor_tensor(out=T["ex"][:], in0=T["cs"][:], in1=T["sw"][:], op=ALU.subtract),
                 reads=["t_cs", "zsw"], writes=["t_ex"])
            P.op("act", lambda e: e.activation(out=T["ex"][:], in_=T["ex"][:], func=AF.Exp, scale=-DS), reads=["t_ex"], writes=["t_ex"])
            P.op("pool", lambda e, h=h: e.tensor_copy(out=PC[:, h, :], in_=T["ei"][:, 63:UW:64]), reads=["t_ei"], writes=["PC"])
            P.op("dve", lambda e: e.tensor_tensor(out=T["er"][:].rearrange("j (c s) -> j c s", s=64),
                                                  in0=T["ev"][:].rearrange("j (c s) -> j c s", s=64),
                                                  in1=T["ei"][:].rearrange("j (c s) -> j c s", s=64)[:, :, 63:64].to_broadcast([64, NCU, 64]),
                                                  op=ALU.mult), reads=["t_ev", "t_ei"], writes=["t_er"])
            P.op("pe", lambda e, hs=hs: e.matmul(paux[:, 0:UW], lhsT=aupb[:, hs], rhs=lob_a[:], start=True, stop=True),
                 reads=["aupb", "loba"], writes=[("pg", 0)])
            P.op("act", lambda e, h=h: e.activation(out=T["as"][:], in_=paux[:, 0:UW], func=AF.Sigmoid, bias=vcol(V_A0 + h)),
                 reads=[("pg", 0), "vec"], writes=["zsa"])
            P.op("act", lambda e, k_=k_, h=h: e.activation(out=T["sq"][:], in_=k_, func=AF.Square, scale=vcol(V_KK + h)),
                 reads=[kk_, "vec"], writes=["t_sq"])
            P.op("pe", lambda e: e.matmul(paux[:, 0:UW], lhsT=cst[:, C_ONE:C_ONE + 64], rhs=T["sq"][:], start=True, stop=True),
                 reads=["cst", "t_sq"], writes=[("pg", 0)])
            P.op("act", lambda e: e.activation(out=T["n"][:], in_=paux[:, 0:UW], func=AF.Sqrt), reads=[("pg", 0)], writes=["t_n"])
            P.op("dve", lambda e: e.tensor_scalar(out=T["n"][:], in0=T["n"][:], scalar1=1e-12, scalar2=None, op0=ALU.max), reads=["t_n"], writes=["t_n"])
            P.op("dve", lambda e: e.reciprocal(out=T["n"][:], in_=T["n"][:]), reads=["t_n"], writes=["t_n"])
            P.op("dve", lambda e, k_=k_, h=h: e.scalar_tensor_tensor(out=T["kk"][:], in0=k_, scalar=vcol(V_KK + h), in1=T["n"][:],
                                                                   op0=ALU.mult, op1=ALU.mult), reads=[kk_, "vec", "t_n"], writes=["t_kk"])
            P.op("pool", lambda e, h=h: e.tensor_scalar(out=T["km"][:], in0=T["as"][:], scalar1=vcol(V_KA + h), scalar2=vcol(V_OMKA + h),
                                                     op0=ALU.mult, op1=ALU.add), reads=["zsa", "vec"], writes=["t_km"])
            P.op("pool", lambda e, k_=k_: e.tensor_tensor(out=T["km"][:], in0=T["km"][:], in1=k_, op=ALU.mult), reads=["t_km", kk_], writes=["t_km"])
            P.op("pool", lambda e: e.tensor_tensor(out=T["bv"][:], in0=T["kk"][:], in1=T["as"][:], op=ALU.mult), reads=["t_kk", "zsa"], writes=["t_bv"])

            def c3(ap):
                return ap.rearrange("j (c s) -> j c s", s=64)
            P.op("dve", lambda e, h=h: e.scalar_tensor_tensor(out=RA[:, h, :, 0, :], in0=c3(T["kk"][:]), scalar=-1.0, in1=c3(T["ex"][:]),
                                                            op0=ALU.mult, op1=ALU.mult), reads=["t_kk", "t_ex"], writes=[("RA", h)])
            P.op("pool", lambda e, h=h, r_=r_: e.tensor_tensor(out=RA[:, h, :, 1, :], in0=c3(r_), in1=c3(T["ei"][:]), op=ALU.mult),
                 reads=[rk_, "t_ei"], writes=[("RA", h)])
            P.op("dve", lambda e, h=h: e.tensor_tensor(out=KB[:, h, :, 0, :], in0=c3(T["km"][:]), in1=c3(T["ev"][:]), op=ALU.mult),
                 reads=["t_km", "t_ev"], writes=[("KB", h)])
            P.op("pool", lambda e, h=h: e.tensor_tensor(out=KB[:, h, :, 1, :], in0=c3(T["bv"][:]), in1=c3(T["ev"][:]), op=ALU.mult),
                 reads=["t_bv", "t_ev"], writes=[("KB", h)])
            P.op("dve", lambda e, h=h: e.tensor_tensor(out=TR[:, h, :, 0, :], in0=c3(T["km"][:]), in1=c3(T["er"][:]), op=ALU.mult),
                 reads=["t_km", "t_er"], writes=[("TR", h)])
            P.op("pool", lambda e, h=h: e.tensor_tensor(out=TR[:, h, :, 1, :], in0=c3(T["bv"][:]), in1=c3(T["er"][:]), op=ALU.mult),
                 reads=["t_bv", "t_er"], writes=[("TR", h)])
            P.op("act", lambda e, h=h, v_=v_: e.activation(out=TR[:, h, :, 2, :], in_=c3(v_), func=AF.Copy), reads=[vk_], writes=[("TR", h)])
            P.op("pool", lambda e, h=h: e.tensor_copy(out=TR[:, h, :, 3, :], in_=RA[:, h, :, 0, :]), reads=[("RA", h)], writes=[("TR", h)])
            P.op("dve", lambda e, h=h, r_=r_: e.scalar_tensor_tensor(out=T["rk"][:], in0=r_, scalar=vcol(V_RK + h), in1=T["km"][:],
                                                                   op0=ALU.mult, op1=ALU.mult), reads=[rk_, "vec", "t_km"], writes=["t_rk"])
            P.op("pe", lambda e: e.matmul(paux[:, 0:UW], lhsT=cst[:, C_ONE:C_ONE + 64], rhs=T["rk"][:], start=True, stop=True),
                 reads=["cst", "t_rk"], writes=[("pg", 0)])
            P.op("dve", lambda e, h=h, v_=v_: e.tensor_tensor(out=bon[:, h, :], in0=paux[:, 0:UW], in1=v_, op=ALU.mult), reads=[("pg", 0), vk_], writes=[("bon", h)])
            P.op("pe", lambda e, hs=hs: e.matmul(paux[:, 0:UW], lhsT=gupb0[:, hs], rhs=gsb0[:], start=True, stop=False),
                 reads=["gupb0", "gsb0"], writes=[("pg", 0)], inc=False)
            P.op("pe", lambda e, hs=hs: e.matmul(paux[:, 0:UW], lhsT=gupb1[:, hs], rhs=gsb1[:], start=False, stop=True),
                 reads=["gupb1", "gsb1"], writes=[("pg", 0)])
            P.op("act", lambda e, h=h: e.activation(out=gT[:, h, :], in_=paux[:, 0:UW], func=AF.Copy), reads=[("pg", 0)], writes=[("gT", h)])

        def chunks():
            def chunk_group(c, g):
                hh = (2 * g, 2 * g + 1)
                hs2 = slice(2 * g, 2 * g + 2)
                K_ = lambda n, *a: (n, g) + a
                pt = ptrs[g]
                ptk = ("ptr", 0)
                for hi, h in enumerate(hh):
                    for q in range(4):
                        P.op("pe", lambda e, hi=hi, h=h, q=q: e.transpose(out=pt[:, hi, q, :], in_=TR[:, h, c, q, :], identity=Ib[:]),
                             reads=[("TR", h), "Ib"], writes=[ptk], inc=(hi == 1 and q == 3))
                P.op("act", lambda e: e.activation(out=TK[:, hs2], in_=pt[:], func=AF.Copy), reads=[ptk], writes=[K_("TK")])
                yield
                b1, b1k = bank()
                v1 = b1[:, 0:512].rearrange("s (h a x) -> s h a x", h=2, a=2)
                for hi, h in enumerate(hh):
                    rhs = RA[:, h, c, :, :].rearrange("j a t -> j (a t)")
                    P.op("pe", lambda e, hi=hi, h=h, rhs=rhs: e.matmul(v1[:, hi, 0, :], lhsT=KB[:, h, c, 1, :], rhs=rhs, start=True, stop=True),
                         reads=[("KB", h), ("RA", h)], writes=[b1k], inc=False)
                    P.op("pe", lambda e, hi=hi, h=h, rhs=rhs: e.matmul(v1[:, hi, 1, :], lhsT=KB[:, h, c, 0, :], rhs=rhs, start=True, stop=True),
                         reads=[("KB", h), ("RA", h)], writes=[b1k], inc=(hi == 1))
                b2, b2k = bank()
                v2 = b2[:, 0:128].rearrange("s (h t) -> s h t", h=2)
                for hi, h in enumerate(hh):
                    P.op("pe", lambda e, hi=hi, h=h: e.matmul(v2[:, hi, :], lhsT=RA[:, h, c, 0, :], rhs=KB[:, h, c, 1, :], start=True, stop=True),
                         reads=[("KB", h), ("RA", h)], writes=[b2k], inc=(hi == 1))
                P.op("dve", lambda e: e.tensor_tensor(
                    out=MS[:, hs2].rearrange("s h a b t -> s h (a b t)"), in0=v1.rearrange("s h a x -> s h (a x)"),
                    in1=mask4[:].rearrange("s a b t -> s (a b t)").unsqueeze(1).to_broadcast([64, 2, 256]), op=ALU.mult),
                    reads=[b1k, "mask4"], writes=[K_("MS")])
                P.op("dve", lambda e: e.tensor_tensor(out=Lb[0][:, hs2], in0=v2, in1=cst[:, C_SL:C_SL + 64].unsqueeze(1).to_broadcast([64, 2, 64]),
                                                      op=ALU.mult), reads=[b2k, "cst"], writes=[K_("L", 0)])
                P.op("pool", lambda e: e.tensor_tensor(out=Pb[0][:, hs2], in0=MS[:, hs2, 0, 0, :], in1=cst[:, C_ID:C_ID + 64].unsqueeze(1).to_broadcast([64, 2, 64]),
                                                       op=ALU.add), reads=[K_("MS"), "cst"], writes=[K_("P", 0)])
                yield
                for k in range(5):
                    li, lo = k % 2, (k + 1) % 2
                    Nk = (lambda h: MS[:, h, 0, 0, :]) if k == 0 else (lambda h, li=li: Nb[li][:, h, :])
                    nkey = K_("MS") if k == 0 else K_("N", li)
                    bl, blk = bank()
                    vl = bl[:, 0:128].rearrange("s (h t) -> s h t", h=2)
                    for hi, h in enumerate(hh):
                        P.op("pe", lambda e, hi=hi, h=h, Nk=Nk, li=li, vl=vl: e.matmul(vl[:, hi, :], lhsT=Nk(h), rhs=Lb[li][:, h, :], start=True, stop=True),
                             reads=[nkey, K_("L", li)], writes=[blk], inc=(hi == 1))
                    if k < 4:
                        bn, bnk = bank()
                        vn = bn[:, 0:128].rearrange("s (h t) -> s h t", h=2)
                        for hi, h in enumerate(hh):
                            P.op("pe", lambda e, hi=hi, h=h, Nk=Nk, li=li, vn=vn: e.matmul(vn[:, hi, :], lhsT=Lb[li][:, h, :], rhs=Nk(h), start=True, stop=True),
                                 reads=[nkey, K_("L", li)], writes=[bnk], inc=(hi == 1))
                    P.op("act", lambda e, lo=lo, vl=vl: e.activation(out=Lb[lo][:, hs2], in_=vl, func=AF.Copy), reads=[blk], writes=[K_("L", lo)])
                    if k < 4:
                        P.op("dve", lambda e, lo=lo, vn=vn: e.tensor_copy(out=Nb[lo][:, hs2], in_=vn), reads=[bnk], writes=[K_("N", lo)])
                    yield
                    bp, bpk = bank()
                    vp = bp[:, 0:128].rearrange("s (h t) -> s h t", h=2)
                    for hi, h in enumerate(hh):
                        P.op("pe", lambda e, hi=hi, h=h, li=li, lo=lo, vp=vp: e.matmul(vp[:, hi, :], lhsT=Lb[lo][:, h, :], rhs=Pb[li][:, h, :], start=True, stop=False),
                             reads=[K_("L", lo), K_("P", li)], writes=[bpk], inc=False)
                        P.op("pe", lambda e, hi=hi, h=h, li=li, vp=vp: e.matmul(vp[:, hi, :], lhsT=Ib[:], rhs=Pb[li][:, h, :], start=False, stop=True),
                             reads=["Ib", K_("P", li)], writes=[bpk], inc=(hi == 1))
                    if k % 2 == 0:
                        P.op("dve", lambda e, lo=lo, vp=vp: e.tensor_copy(out=Pb[lo][:, hs2], in_=vp), reads=[bpk], writes=[K_("P", lo)])
                    else:
                        P.op("act", lambda e, lo=lo, vp=vp: e.activation(out=Pb[lo][:, hs2], in_=vp, func=AF.Copy), reads=[bpk], writes=[K_("P", lo)])
                    yield
                TT = Pb[1]
                ttk = K_("P", 1)
                bw, bwk = bank()
                vw = bw[:, 0:128].rearrange("s (h t) -> s h t", h=2)
                for hi, h in enumerate(hh):
                    P.op("pe", lambda e, hi=hi, h=h: e.matmul(vw[:, hi, :], lhsT=TK[:, h, 3, :], rhs=TT[:, h, :], start=True, stop=True),
                         reads=[K_("TK"), ttk], writes=[bwk], inc=(hi == 1))
                bz, bzk = bank()
                vz = bz[:, 0:128].rearrange("s (h t) -> s h t", h=2)
                for hi, h in enumerate(hh):
                    P.op("pe", lambda e, hi=hi, h=h: e.matmul(vz[:, hi, :], lhsT=MS[:, h, 1, 0, :], rhs=TK[:, h, 2, :], start=True, stop=True),
                         reads=[K_("MS"), K_("TK")], writes=[bzk], inc=(hi == 1))
                P.op("act", lambda e: e.activation(out=WT[:, hs2], in_=vw, func=AF.Copy), reads=[bwk], writes=[K_("WT")])
                P.op("dve", lambda e: e.tensor_copy(out=Zs[:, hs2], in_=vz), reads=[bzk], writes=[K_("Zs")])
                yield
                bu, buk = bank()
                vu = bu[:, 0:128].rearrange("s (h t) -> s h t", h=2)
                for hi, h in enumerate(hh):
                    P.op("pe", lambda e, hi=hi, h=h: e.matmul(vu[:, hi, :], lhsT=TT[:, h, :], rhs=Zs[:, h, :], start=True, stop=True),
                         reads=[ttk, K_("Zs")], writes=[buk], inc=(hi == 1))
                P.op("act", lambda e: e.activation(out=Ut[:, hs2], in_=vu, func=AF.Copy), reads=[buk], writes=[K_("Ut")])
                P.op("pool", lambda e: e.tensor_tensor(out=Hd[:, hs2], in0=Hf[:, hs2], in1=PC[:, hs2, c:c + 1].to_broadcast([64, 2, 64]), op=ALU.mult),
                     reads=[K_("Hf"), "PC"], writes=[K_("Hd")])
                yield
                b5, b5k = bank()
                v5 = b5[:, 0:128].rearrange("s (h t) -> s h t", h=2)
                for hi, h in enumerate(hh):
                    P.op("pe", lambda e, hi=hi, h=h: e.matmul(v5[:, hi, :], lhsT=WT[:, h, :], rhs=Hb[:, h, :], start=True, stop=True),
                         reads=[K_("WT"), K_("Hb")], writes=[b5k], inc=(hi == 1))
                P.op("dve", lambda e: e.tensor_tensor(out=Ub[:, hs2], in0=v5, in1=Ut[:, hs2], op=ALU.add), reads=[b5k, K_("Ut")], writes=[K_("Ub")])
                yield
                by, byk = bank()
                vy = by[:, 0:128].rearrange("s (h t) -> s h t", h=2)
                for hi, h in enumerate(hh):
                    P.op("pe", lambda e, hi=hi, h=h: e.matmul(vy[:, hi, :], lhsT=Hb[:, h, :], rhs=RA[:, h, c, 1, :], start=True, stop=False),
                         reads=[K_("Hb"), ("RA", h)], writes=[byk], inc=False)
                    P.op("pe", lambda e, hi=hi, h=h: e.matmul(vy[:, hi, :], lhsT=Ub[:, h, :], rhs=MS[:, h, 0, 1, :], start=False, stop=False),
                         reads=[K_("Ub"), K_("MS")], writes=[byk], inc=False)
                    P.op("pe", lambda e, hi=hi, h=h: e.matmul(vy[:, hi, :], lhsT=TK[:, h, 2, :], rhs=MS[:, h, 1, 1, :], start=False, stop=True),
                         reads=[K_("TK"), K_("MS")], writes=[byk], inc=(hi == 1))
                bh, bhk = bank()
                vh = bh[:, 0:128].rearrange("s (h t) -> s h t", h=2)
                for hi, h in enumerate(hh):
                    P.op("pe", lambda e, hi=hi, h=h: e.matmul(vh[:, hi, :], lhsT=TK[:, h, 1, :], rhs=Ub[:, h, :], start=True, stop=False),
                         reads=[K_("TK"), K_("Ub")], writes=[bhk], inc=False)
                    P.op("pe", lambda e, hi=hi, h=h: e.matmul(vh[:, hi, :], lhsT=TK[:, h, 0, :], rhs=TK[:, h, 2, :], start=False, stop=True),
                         reads=[K_("TK")], writes=[bhk], inc=(hi == 1))
                P.op("dve", lambda e: e.tensor_tensor(out=Hb[:, hs2], in0=vh, in1=Hd[:, hs2], op=ALU.add), reads=[bhk, K_("Hd")], writes=[K_("Hb")])
                P.op("dve", lambda e: e.tensor_tensor(out=Hf[:, hs2], in0=vh, in1=Hd[:, hs2], op=ALU.add), reads=[bhk, K_("Hd")], writes=[K_("Hf")])
                P.op("act", lambda e: e.activation(out=yT[:, hs2, c * 64:(c + 1) * 64], in_=vy, func=AF.Copy), reads=[byk], writes=[K_("yT")])
                yield

            return chunk_group

        def output_head(h):
            y_ = yT[:, h, :]
            P.op("pe", lambda e, y_=y_: e.matmul(paux[:, 0:UW], lhsT=cst[:, C_AVG:C_AVG + 64], rhs=y_, start=True, stop=True),
                 reads=["cst", ("yT", 0), ("yT", 1)], writes=[("pg", 0)])
            P.op("dve", lambda e, y_=y_: e.tensor_tensor(out=To["yc"][:], in0=y_, in1=paux[:, 0:UW], op=ALU.subtract), reads=[("yT", 0), ("yT", 1), ("pg", 0)], writes=["o_yc"])
            P.op("act", lambda e: e.activation(out=To["sq"][:], in_=To["yc"][:], func=AF.Square), reads=["o_yc"], writes=["o_sq"])
            P.op("pe", lambda e: e.matmul(paux[:, 0:UW], lhsT=cst[:, C_AVG:C_AVG + 64], rhs=To["sq"][:], start=True, stop=True),
                 reads=["cst", "o_sq"], writes=[("pg", 0)])
            P.op("act", lambda e: e.activation(out=To["n"][:], in_=paux[:, 0:UW], func=AF.Sqrt, bias=GN_EPS), reads=[("pg", 0)], writes=["o_n"])
            P.op("dve", lambda e: e.reciprocal(out=To["n"][:], in_=To["n"][:]), reads=["o_n"], writes=["o_n"])
            P.op("pool", lambda e: e.tensor_tensor(out=To["yc"][:], in0=To["yc"][:], in1=To["n"][:], op=ALU.mult), reads=["o_yc", "o_n"], writes=["o_yc"])
            P.op("pool", lambda e, h=h: e.tensor_scalar(out=To["yc"][:], in0=To["yc"][:], scalar1=vcol(V_LNW + h), scalar2=vcol(V_LNB + h),
                                                     op0=ALU.mult, op1=ALU.add), reads=["o_yc", "vec"], writes=["o_yc"])
            P.op("pool", lambda e, h=h: e.tensor_tensor(out=To["yc"][:], in0=To["yc"][:], in1=bon[:, h, :], op=ALU.add), reads=["o_yc", ("bon", h)], writes=["o_yc"])
            P.op("dve", lambda e, h=h: e.tensor_tensor(out=mo[:, h, :], in0=To["yc"][:], in1=gT[:, h, :], op=ALU.mult),
                 reads=["o_yc", ("gT", h)], writes=["mo"])
        def store_mo(u):
            off = (u % 4) * UW
            P.dma("sp", io["mix_loc"][u // 4, 256:512, off:off + UW].rearrange("(h i) t -> i h t", h=4), mo[:], ("rmo", pz),
                  reads=["mo"], writes=[("mixslab", u // 4)])
        return proj_group, lora_acts, preproc, chunks, output_head, store_mo, uts, P

    F = [make_tile_fns(0), make_tile_fns(1)]

    def gather_slab(j):
        P0.custom("pool", lambda e, j=j: e.collective_compute(
            "AllGather", ALU.bypass, replica_groups=[[0, 1, 2, 3], [4, 5, 6, 7]],
            ins=[io["mix_loc"][j].opt()], outs=[io["mix_gath"][j].opt()]),
            key="cc", amt=1, reads=[("mixslab", j)], writes=[])

    witems = []
    for (src, dst, K, N) in (wspecs or []):
        nkc = K // 128
        for kg in range((nkc + 15) // 16):
            nk = min(16, nkc - kg * 16)
            for nb in range(N // 512):
                tix = wtile_index(N, kg, nb)
                for h0 in range(0, nk, 2):
                    r0 = (kg * 16 + h0) * 128
                    witems.append((src[r0:r0 + 256, nb * 512:(nb + 1) * 512].rearrange("(kc p) c -> p kc c", p=128),
                                   dst[tix].rearrange("p (kc c) -> p kc c", c=512)[:, h0:h0 + 2, :]))
    wst_ = {"i": 0}

    def w_load(i):
        if i < len(witems):
            P.dma("sp", wstg[:, i % 2], witems[i][0], ("wl", i % 2), writes=[("wstg", i % 2)])

    def w_steps(n):
        for _ in range(n):
            i = wst_["i"]
            if i >= len(witems):
                return
            if i == 0:
                w_load(0)
            w_load(i + 1)
            sl = i % 2
            P.op("pool", lambda e, sl=sl: e.tensor_copy(out=wob[:, sl], in_=wstg[:, sl]), reads=[("wstg", sl)], writes=[("wob", sl)])
            P.dma("sp", witems[i][1], wob[:, sl], ("ws", sl), reads=[("wob", sl)])
            wst_["i"] += 1
    per_chunk = -(-len(witems) // max(1, (nunit * NCU - NCU))) if witems else 0

    def load_uT(t):
        P0.dma("sp", uT2[0][:], io["uT_scr"][t].rearrange("p (kc t) -> p kc t", t=512), "utl", writes=["uT"])

    def stage1(u):
        proj_group, lora_acts, preproc, chunks, output_head, store_mo, uts, px = F[u % 2]
        t, half = u // 2, u % 2
        uts["ap"] = uT2[0][:, :, half * UW:(half + 1) * UW]
        uts["key"] = "uT"
        for gi in (12, 13, 14, 15):
            proj_group(gi)
            yield
        lora_acts()
        yield
        for h in range(4):
            for gi in (h, 4 + h, 8 + h):
                proj_group(gi)
                yield
            if h >= 1:
                yield from sliced(px, preproc, h - 1)
        yield from sliced(px, preproc, 3)
        if half == 1 and t + 1 < ntile:
            load_uT(t + 1)

    def sliced(px, fn, *a):
        px.buf = []
        fn(*a)
        ops, px.buf = px.buf, None
        for i, th in enumerate(ops):
            th()
            if i % 3 == 2:
                yield
        yield

    def stage3(u):
        proj_group, lora_acts, preproc, chunks, output_head, store_mo, uts, px = F[u % 2]
        for h in range(4):
            output_head(h)
            yield
        store_mo(u)
        if gather and u % 4 == 3:
            gather_slab(u // 4)
        yield

    def drain(g):
        for _ in g:
            pass

    load_uT(0)
    drain(stage1(0))
    for u in range(nunit):
        chunk_group = F[u % 2][3]()
        side = []
        if u + 1 < nunit:
            side.append(stage1(u + 1))
        if u > 0:
            side.append(stage3(u - 1))
        for c in range(NCU):
            w_steps(per_chunk)
            gens = [chunk_group(c, 0), chunk_group(c, 1)] + side
            live = [True] * len(gens)
            nchunk_live = 2
            while nchunk_live > 0:
                for gi_, gn in enumerate(gens):
                    if live[gi_]:
                        try:
                            next(gn)
                        except StopIteration:
                            live[gi_] = False
                            if gi_ < 2:
                                nchunk_live -= 1
            side = [g for gi_, g in enumerate(gens) if gi_ >= 2 and live[gi_]]
        for g in side:
            drain(g)
    drain(stage3(nunit - 1))
    w_steps(len(witems))
    P.flush()
    A.close()


def phase_x(P, nc, io, flush=True):
    groups = [[0, 1, 2, 3], [4, 5, 6, 7]]
    for j in range(8):
        P.custom("pool", lambda e, j=j: e.collective_compute(
            "AllGather", ALU.bypass, replica_groups=groups,
            ins=[io["mix_loc"][j].opt()], outs=[io["mix_gath"][j].opt()]),
            key="cc", amt=1, reads=[], writes=[])
    if flush:
        P.flush()


def build(cfg):
    nc = bass.Bass("TRN2", target_bir_lowering=False)
    phases = cfg.get("phases", "WARXD")
    io = {}

    def ein(name, shape, dt=F32):
        io[name] = nc.dram_tensor(name, list(shape), dt, kind="ExternalInput").ap()

    ein("xb", [SEQ, D_MODEL])
    ein("xs", [2048, D_MODEL])
    ein("norm1_g", [1, D_MODEL])
    ein("norm2_g", [1, D_MODEL])
    ein("final_g", [1, D_MODEL])
    ein("w_in_a", [D_MODEL, 768])
    ein("w_in_r", [D_MODEL, RW_COLS])
    ein("abias", [128, 3 * 4 * 256])
    ein("amask", [128, 3 * 256])
    ein("ident", [128, 128])
    ein("w_out_p", [D_MODEL, D_MODEL])
    ein("w_gu", [D_MODEL, 2 * FFN])
    ein("w_down", [FFN, D_MODEL])
    ein("rw_vec", [128, 64])
    ein("rw_mats", [128, 1024])
    ein("rw_const", [64, 1024])
    io["out"] = nc.dram_tensor("out", [2048, D_MODEL], F32, kind="ExternalOutput").ap()
    if cfg.get("mix_in"):
        ein("mix_in", [8, 512, 1024], BF16)
    io["uT_scr"] = nc.dram_tensor("uT_scr", [SEQ // 512, 128, 16 * 512], BF16).ap()
    io["wo_scr"] = nc.dram_tensor("wo_scr", [4, 128, 16 * 512], BF16).ap()
    io["wgu_scr"] = nc.dram_tensor("wgu_scr", [22, 128, 16 * 512], BF16).ap()
    io["wd_scr"] = nc.dram_tensor("wd_scr", [12, 128, 16 * 512], BF16).ap()
    if cfg.get("dump_mix"):
        io["mix_loc"] = nc.dram_tensor("mix_loc", [8, 512, 1024], BF16, kind="ExternalOutput").ap()
    else:
        io["mix_loc"] = nc.dram_tensor("mix_loc", [8, 512, 1024], BF16).ap()
    io["mix_gath"] = nc.dram_tensor("mix_gath", [8, 2048, 1024], BF16).ap()

    P = Prog(nc, same_eng_sync=cfg.get("same_eng_sync", True))
    if cfg.get("mix_in"):
        rows = cfg["mix_in"]
        P.dma("sp", io["mix_loc"][:, rows[0]:rows[1], :], io["mix_in"][:, rows[0]:rows[1], :], "mixin")
        P.flush()
    wspecs = [(io["w_out_p"], io["wo_scr"], D_MODEL, D_MODEL),
              (io["w_gu"], io["wgu_scr"], D_MODEL, 2 * FFN),
              (io["w_down"], io["wd_scr"], FFN, D_MODEL)]
    if "A" in phases:
        phase_a(P, nc, io, nsc=cfg.get("nsc", SEQ // 2048))
    w_in_r = ("W" in phases and "R" in phases and cfg.get("nrt", SEQ // 512) == SEQ // 512)
    if "R" in phases:
        (phase_r2 if cfg.get("r2", False) else phase_r)(P, nc, io, ntile=cfg.get("nrt", SEQ // 512), wspecs=wspecs if w_in_r else None,
                gather=(w_in_r and "X" in phases))
    if w_in_r:
        pass
    elif "W" in phases and "X" in phases:
        phase_w(P, nc, wspecs, pre=lambda: phase_x(P, nc, io, flush=False), engs=("dve", "act"))
    else:
        if "W" in phases:
            phase_w(P, nc, wspecs)
        if "X" in phases:
            phase_x(P, nc, io)
    if "D" in phases:
        phase_d(P, nc, io, ntile=cfg.get("ndt", 4))
    P.close()
    return nc


def host_inputs(inputs):
    f32 = np.float32
    x = np.asarray(inputs["x"], f32)
    w_in = np.asarray(inputs["w_in"], f32)[0]
    w_out = np.asarray(inputs["w_out"], f32)[0]
    w_gu = np.ascontiguousarray(np.asarray(inputs["w_gate_up"], f32)[0])
    w_down = np.ascontiguousarray(np.asarray(inputs["w_down"], f32)[0])
    table = np.asarray(inputs["rel_bias_table"], f32)
    bidx, amask = _attn_tables()
    ident = np.eye(128, dtype=f32)
    maps = []
    zoff = 3 * 1024
    for core in range(NCORES):
        b, g = core // 4, core % 4
        hs = list(range(4 * g, 4 * g + 4))
        acols = np.concatenate([np.arange(o + 64 * h, o + 64 * h + 64) for o in (0, 1024, 2048) for h in hs])
        rcols = np.concatenate(
            [np.arange(zoff + o + 64 * h, zoff + o + 64 * h + 64) for o in (0, 1024, 2048) for h in hs]
            + [np.arange(zoff + 3072, zoff + 3072 + 64 + 64 + 160)])
        perm = np.concatenate([np.concatenate([np.arange(256 * r, 256 * r + 256), np.arange(1024 + 256 * r, 1024 + 256 * r + 256)])
                               for r in range(4)])
        abias = table[bidx][:, :, :, hs]
        abias = np.ascontiguousarray(np.transpose(abias, (1, 0, 3, 2))).reshape(128, 3 * 4 * 256)
        m = {
            "xb": np.ascontiguousarray(x[b]),
            "xs": np.ascontiguousarray(x[b, 2048 * g:2048 * (g + 1)]),
            "norm1_g": np.asarray(inputs["norm1_g"], f32).reshape(1, D_MODEL),
            "norm2_g": np.asarray(inputs["norm2_g"], f32).reshape(1, D_MODEL),
            "final_g": np.asarray(inputs["final_g"], f32).reshape(1, D_MODEL),
            "w_in_a": np.ascontiguousarray(w_in[:, acols]),
            "w_in_r": np.ascontiguousarray(w_in[:, rcols]),
            "abias": abias.astype(f32),
            "amask": np.ascontiguousarray(np.transpose(amask, (1, 0, 2))).reshape(128, 768),
            "ident": ident,
            "w_out_p": np.ascontiguousarray(w_out[perm]),
            "w_gu": w_gu,
            "w_down": w_down,
        }
        m.update(host_rwkv(inputs, hs))
        maps.append(m)
    return maps


def host_rwkv(inputs, hs):
    f32 = np.float32
    mixv = np.asarray(inputs["rwkv_shift_mix"], f32)[0]
    vec = np.zeros((128, 64), f32)
    mats = np.zeros((128, 1024), f32)
    hc = np.concatenate([np.arange(64 * h, 64 * h + 64) for h in hs])
    for hl, h in enumerate(hs):
        sl = slice(64 * h, 64 * h + 64)
        vec[0:64, V_MIX_R + hl] = mixv[0:1024][sl]
        vec[0:64, V_MIX_K + hl] = mixv[1024:2048][sl]
        vec[0:64, V_MIX_V + hl] = mixv[2048:3072][sl]
        vec[0:64, V_W0 + hl] = np.asarray(inputs["rwkv_w0"], f32)[0][sl]
        vec[0:64, V_A0 + hl] = np.asarray(inputs["rwkv_a0"], f32)[0][sl]
        vec[0:64, V_KK + hl] = np.asarray(inputs["rwkv_k_k"], f32)[0][sl]
        vec[0:64, V_KA + hl] = np.asarray(inputs["rwkv_k_a"], f32)[0][sl]
        vec[0:64, V_RK + hl] = np.asarray(inputs["rwkv_r_k"], f32)[0][h]
        vec[0:64, V_LNW + hl] = np.asarray(inputs["rwkv_ln_w"], f32)[0][sl]
        vec[0:64, V_LNB + hl] = np.asarray(inputs["rwkv_ln_b"], f32)[0][sl]
    vec[0:64, V_MIX_W] = mixv[3072:3136]
    vec[0:64, V_MIX_A] = mixv[3136:3200]
    vec[0:128, V_MIX_G0] = mixv[3200:3328]
    vec[0:32, V_MIX_G1] = mixv[3328:3360]
    mats[0:64, 0:256] = np.asarray(inputs["rwkv_w_up"], f32)[0][:, hc]
    mats[0:64, 256:512] = np.asarray(inputs["rwkv_a_up"], f32)[0][:, hc]
    gup = np.asarray(inputs["rwkv_g_up"], f32)[0]
    mats[0:128, 512:768] = gup[0:128][:, hc]
    mats[0:32, 768:1024] = gup[128:160][:, hc]
    cst = np.zeros((64, 1024), f32)
    a = np.arange(64)
    cst[:, C_SU:C_SU + 64] = (a[:, None] < a[None, :])
    cst[:, C_IU:C_IU + 64] = (a[:, None] <= a[None, :])
    cst[:, C_SL:C_SL + 64] = (a[None, :] < a[:, None])
    cst[:, C_ID:C_ID + 64] = np.eye(64)
    cst[:, C_AVG:C_AVG + 64] = 1.0 / 64
    cst[:, C_ONE:C_ONE + 64] = 1.0
    rst = np.ones(512, f32)
    rst[0::64] = 0.0
    cst[:, C_RST:C_RST + 512] = rst[None, :]
    return {"rw_vec": vec, "rw_mats": mats, "rw_const": cst}


_NC_CACHE = {}


def kernel(**inputs):
    cfg = {"phases": "WARXD"}
    key = "full"
    if key not in _NC_CACHE:
        _NC_CACHE[key] = build(cfg)
    nc = _NC_CACHE[key]
    maps = host_inputs(inputs)
    res = run_bass_kernel_spmd(nc, maps, core_ids=list(range(NCORES)))
    out = np.empty((NBATCH, SEQ, D_MODEL), np.float32)
    for core in range(NCORES):
        b, g = core // 4, core % 4
        out[b, 2048 * g:2048 * (g + 1)] = res.results[core]["out"]
    return out
```

```python
from contextlib import ExitStack
import math
import numpy as np
import ml_dtypes
import concourse.bass as bass
import concourse.mybir as mybir
from concourse.bass_utils import run_bass_kernel_spmd

F32 = mybir.dt.float32
BF16 = mybir.dt.bfloat16
AF = mybir.ActivationFunctionType
ALU = mybir.AluOpType
AX = mybir.AxisListType

ENGS = ("sp", "act", "dve", "pool", "pe")

D_MODEL = 2048
SEQ = 8192
NBATCH = 2
HD = 64
FFN = 5632
NCORES = 8
PATTERNS = ((128, 1), (512, 4), (2048, 16))
RMS_EPS = 1e-6
GN_EPS = 64e-5
DECAY_SCALE = math.exp(-0.5)
NEG = -30000.0
RW_COLS = 1056
CH = 64


class Prog:
    def __init__(self, nc, same_eng_sync=True):
        self.nc = nc
        self.same_eng_sync = same_eng_sync
        self.stack = ExitStack()
        self.sems = {}
        self.cnt = {}
        self.ops = {e: [] for e in ENGS}
        self.waited = {e: {} for e in ENGS}
        self.bufs = {}
        self.pending = {e: ([], []) for e in ENGS}
        self.nops = 0
        self.psum_names = {"pT", "pp", "pS", "pO", "pb", "ptr", "pg"}

    def _excl(self, reads, writes):
        reads = list(reads)
        writes = list(writes)
        for k in reads:
            n = k[0] if isinstance(k, tuple) else k
            if n in self.psum_names and k not in writes:
                writes.append(k)
        return reads, writes

    def sem(self, key):
        if key not in self.sems:
            name = "s_" + "_".join(str(k) for k in (key if isinstance(key, tuple) else (key,)))
            self.sems[key] = self.stack.enter_context(self.nc.semaphore(name))
            self.cnt[key] = 0
        return self.sems[key]

    def _deps(self, reads, writes):
        deps = []
        for b in reads:
            st = self.bufs.get(b)
            if st is not None and st[0] is not None:
                deps.append(st[0])
        for b in writes:
            st = self.bufs.get(b)
            if st is not None:
                if st[0] is not None:
                    deps.append(st[0])
                deps.extend(st[1])
        return deps

    def _commit(self, tok, reads, writes):
        for b in reads:
            st = self.bufs.get(b)
            if st is None:
                self.bufs[b] = [None, [tok]]
            else:
                st[1].append(tok)
        for b in writes:
            self.bufs[b] = [tok, []]

    def _filter(self, eng, deps):
        w = self.waited[eng]
        best = {}
        for d in deps:
            if d is None:
                continue
            key, val = d
            if key == eng and (eng == "pe" or (not self.same_eng_sync and eng != "pool")):
                continue
            if w.get(key, 0) >= val:
                continue
            if best.get(key, 0) < val:
                best[key] = val
        for key, val in best.items():
            w[key] = val
        return list(best.items())

    def op(self, eng, fn, reads=(), writes=(), extra=(), inc=True):
        reads, writes = self._excl(reads, writes)
        deps = self._deps(reads, writes) + list(extra)
        waits = self._filter(eng, deps)
        self.sem(eng)
        self.nops += 1
        if not inc:
            self.ops[eng].append((waits, fn, None, 0))
            self.pending[eng][0].extend(reads)
            self.pending[eng][1].extend(writes)
            return None
        self.cnt[eng] += 1
        tok = (eng, self.cnt[eng])
        self.ops[eng].append((waits, fn, eng, 1))
        pr, pw = self.pending[eng]
        self._commit(tok, list(pr) + list(reads), list(pw) + list(writes))
        self.pending[eng] = ([], [])
        return tok

    def dma(self, q, out, in_, slot, reads=(), writes=(), extra=(), **kw):
        key = ("d", slot)
        self.sem(key)
        deps = self._deps(reads, writes) + list(extra)
        if self.cnt[key] > 0:
            deps.append((key, self.cnt[key]))
        waits = self._filter(q, deps)
        self.cnt[key] += 16
        tok = (key, self.cnt[key])
        self.ops[q].append((waits, (lambda e, o=out, i=in_, k=kw: e.dma_start(out=o, in_=i, **k)), key, 16))
        self._commit(tok, reads, writes)
        self.nops += 1
        return tok

    def dma_fn(self, q, fn, slot, reads=(), writes=(), extra=()):
        return self.custom(q, fn, ("d", slot), 16, reads=reads, writes=writes, extra=extra)

    def custom(self, eng, fn, key, amt, reads=(), writes=(), extra=()):
        self.sem(key)
        deps = self._deps(reads, writes) + list(extra)
        if self.cnt[key] > 0:
            deps.append((key, self.cnt[key]))
        waits = self._filter(eng, deps)
        self.cnt[key] += amt
        tok = (key, self.cnt[key])
        self.ops[eng].append((waits, fn, key, amt))
        self._commit(tok, reads, writes)
        return tok

    def flush(self):
        for e in ENGS:
            assert not self.pending[e][0] and not self.pending[e][1], f"pending non-inc ops on {e}"
        final = [(k, v) for k, v in self.cnt.items() if v > 0]
        for e in ENGS:
            self.sem(e)
            waits = self._filter(e, final)
            if waits:
                self.ops[e].append((waits, None, None, 0))
        ops = self.ops
        sems = self.sems

        def emit(engine, ename):
            for waits, fn, key, amt in ops[ename]:
                for (k, v) in waits:
                    engine.wait_ge(sems[k], v)
                if fn is None:
                    continue
                ins = fn(engine)
                if key is not None:
                    ins.then_inc(sems[key], amt)

        with self.nc.Block() as block:
            @block.sync
            def _(e):
                emit(e, "sp")

            @block.scalar
            def _(e):
                emit(e, "act")

            @block.vector
            def _(e):
                emit(e, "dve")

            @block.gpsimd
            def _(e):
                emit(e, "pool")

            @block.tensor
            def _(e):
                emit(e, "pe")

        self.ops = {e: [] for e in ENGS}
        self.bufs = {}
        for e in ENGS:
            self.waited[e] = dict(self.cnt)

    def close(self):
        self.stack.close()


class Alloc:
    def __init__(self, nc):
        self.nc = nc
        self.st = ExitStack()

    def sb(self, name, shape, dt):
        return self.st.enter_context(self.nc.sbuf_tensor(name, list(shape), dt))

    def ps(self, name, shape, dt):
        return self.st.enter_context(self.nc.psum_tensor(name, list(shape), dt))

    def close(self):
        self.st.close()


def _t5_bucket_np(dist):
    exact = 16
    d_f = np.maximum(dist, 1).astype(np.float32)
    large = exact + (np.log(d_f / np.float32(exact)) / np.float32(math.log(2048 / exact))
                     * np.float32(32 - exact)).astype(np.int32)
    large = np.minimum(large, 31)
    return np.where(dist < exact, dist, large)


def _attn_tables():
    ki = np.arange(128)[:, None]
    qi = np.arange(128)[None, :]
    bidx = np.zeros((3, 128, 256), np.int64)
    mask = np.zeros((3, 128, 256), np.float32)
    for p, (w, dil) in enumerate(PATTERNS):
        rel_prev = qi + 128 - ki
        rel_cur = qi - ki
        vp = ki >= qi
        vc = ki <= qi
        bidx[p, :, 0:128] = _t5_bucket_np(np.clip(rel_prev, 0, 128) * dil)
        bidx[p, :, 128:256] = _t5_bucket_np(np.clip(rel_cur, 0, 128) * dil)
        mask[p, :, 0:128] = np.where(vp, 0.0, NEG)
        mask[p, :, 128:256] = np.where(vc, 0.0, NEG)
    return bidx, mask


def wtile_index(N, kg, nb):
    return kg * (N // 512) + nb


def phase_w(P, nc, specs, pre=None, engs=("dve", "pool")):
    A = Alloc(nc)
    if pre is not None:
        pre()
    NS = 5
    stg = [A.sb(f"w_stg{i}", [128, 8, 512], F32) for i in range(NS)]
    ob = [A.sb(f"w_ob{i}", [128, 8, 512], BF16) for i in range(NS)]
    items = []
    for (src, dst, K, N) in specs:
        nkc = K // 128
        for kg in range((nkc + 15) // 16):
            nk = min(16, nkc - kg * 16)
            for nb in range(N // 512):
                tix = wtile_index(N, kg, nb)
                for h0 in range(0, nk, 8):
                    hk = min(8, nk - h0)
                    r0 = (kg * 16 + h0) * 128
                    src_ap = src[r0:r0 + hk * 128, nb * 512:(nb + 1) * 512].rearrange("(kc p) c -> p kc c", p=128)
                    dst_ap = dst[tix].rearrange("p (kc c) -> p kc c", c=512)[:, h0:h0 + hk, :]
                    items.append((src_ap, dst_ap, hk))

    def load(i):
        src_ap, _, hk = items[i]
        s = i % NS
        P.dma("sp", stg[s][:, 0:hk, :], src_ap, ("wl", s), writes=[("wstg", s)])

    for i in range(min(NS - 1, len(items))):
        load(i)
    for i, (src_ap, dst_ap, hk) in enumerate(items):
        s = i % NS
        eng = engs[i % len(engs)]
        if eng == "act":
            P.op("act", lambda e, s=s, hk=hk: e.activation(out=ob[s][:, 0:hk, :], in_=stg[s][:, 0:hk, :], func=AF.Copy),
                 reads=[("wstg", s)], writes=[("wob", s)])
        else:
            P.op(eng, lambda e, s=s, hk=hk: e.tensor_copy(out=ob[s][:, 0:hk, :], in_=stg[s][:, 0:hk, :]),
                 reads=[("wstg", s)], writes=[("wob", s)])
        if i + NS - 1 < len(items):
            load(i + NS - 1)
        P.dma("sp", dst_ap, ob[s][:, 0:hk, :], ("ws", s), reads=[("wob", s)])
    P.flush()
    A.close()


def norm_front(P, xin, xkey, gbc, junk, ssq, ub, ubkey, tagk):
    P.op("act", lambda e: e.activation(out=junk[:], in_=xin, func=AF.Square, accum_out=ssq[:, 0:1]),
         reads=[xkey], writes=["junk", ("ssq", tagk)])
    P.op("act", lambda e: e.activation(out=ssq[:, 1:2], in_=ssq[:, 0:1], func=AF.Sqrt, scale=1.0 / D_MODEL, bias=RMS_EPS),
         reads=[("ssq", tagk)], writes=[("ssq", tagk)])
    P.op("dve", lambda e: e.reciprocal(out=ssq[:, 2:3], in_=ssq[:, 1:2]), reads=[("ssq", tagk)], writes=[("ssq", tagk)])
    P.op("dve", lambda e: e.scalar_tensor_tensor(out=ub[:], in0=xin, scalar=ssq[:, 2:3], in1=gbc[:], op0=ALU.mult, op1=ALU.mult),
         reads=[xkey, ("ssq", tagk), "gbc"], writes=[ubkey])


def norm_back(P, ub, ubkey, pT, uT_dst, uTkey, ident, evac_eng="dve"):
    pk = [("pT", 0), ("pT", 1)]
    for kc in range(16):
        P.op("pe", lambda e, kc=kc: e.transpose(out=pT[:, kc, :], in_=ub[:, kc * 128:(kc + 1) * 128], identity=ident[:]),
             reads=[ubkey, "ident"], writes=pk, inc=(kc == 15))
    for half in range(2):
        hsl = slice(half * 8, half * 8 + 8)
        eng = evac_eng if half == 0 else ("act" if evac_eng == "dve" else "dve")
        if eng == "act":
            P.op("act", lambda e, hsl=hsl: e.activation(out=uT_dst[:, hsl, :], in_=pT[:, hsl, :], func=AF.Copy), reads=[pk[half]], writes=[uTkey])
        else:
            P.op(eng, lambda e, hsl=hsl: e.tensor_copy(out=uT_dst[:, hsl, :], in_=pT[:, hsl, :]), reads=[pk[half]], writes=[uTkey])


def norm_transpose_block(P, xin, xkey, gbc, junk, ssq, ub, ubkey, pT, pTkey, uT_dst, uTkey, ident, tagk,
                         evac_eng="dve"):
    norm_front(P, xin, xkey, gbc, junk, ssq, ub, ubkey, tagk)
    norm_back(P, ub, ubkey, pT, uT_dst, uTkey, ident, evac_eng)


def phase_a(P, nc, io, nsc=SEQ // 2048):
    A = Alloc(nc)
    x = io["xb"]
    wA = A.sb("wA", [128, 16, 768], BF16)
    wst = [A.sb(f"a_wst{i}", [128, 768], F32) for i in range(2)]
    xb = [A.sb(f"a_xb{i}", [128, 2048], F32) for i in range(2)]
    gbc = A.sb("a_gbc", [128, 2048], F32)
    junk = A.sb("a_junk", [128, 2048], BF16)
    ub = [A.sb(f"a_ub{i}", [128, 2048], BF16) for i in range(2)]
    ssq = [A.sb(f"a_ssq{i}", [128, 4], F32) for i in range(2)]
    uT = [A.sb(f"a_uT{i}", [128, 16, 512], BF16) for i in range(2)]
    ident = A.sb("a_ident", [128, 128], BF16)
    identf = A.sb("a_identf", [128, 128], F32)
    ones_f = A.sb("a_ones", [128, 64], F32)
    QT = [A.sb(f"a_QT{i}", [128, 2048], BF16) for i in range(2)]
    KT = [A.sb(f"a_KT{i}", [128, 4096], BF16) for i in range(2)]
    VT = [A.sb(f"a_VT{i}", [128, 4096], BF16) for i in range(2)]
    RING = (4, 8, 32)
    Vtok = [A.sb(f"a_Vtok{p}", [128, RING[p], 4, 65], BF16) for p in range(3)]
    acc = [A.sb(f"a_acc{h}", [65, 2048], F32) for h in range(2)]
    PT = [A.sb(f"a_PT{i}", [128, 512], BF16) for i in range(3)]
    biasT = A.sb("a_biasT", [128, 3, 4, 256], BF16)
    maskf = A.sb("a_maskf", [128, 3, 256], F32)
    recs = [A.sb(f"a_rec{i}", [64, 512], F32) for i in range(2)]
    mo = [A.sb(f"a_mo{i}", [64, 2048], BF16) for i in range(2)]
    pT = [A.ps(f"a_pT{i}", [128, 16, 128], BF16) for i in range(1)]
    pproj = [A.ps(f"a_pp{i}", [128, 512], F32) for i in range(2)]
    pS = [A.ps(f"a_pS{i}", [128, 512], F32) for i in range(2)]
    pO = [A.ps(f"a_pO{i}", [128, 512], F32) for i in range(2)]

    P.dma("sp", identf[:], io["ident"], "c0", writes=["identf"])
    P.op("dve", lambda e: e.tensor_copy(out=ident[:], in_=identf[:]), reads=["identf"], writes=["ident"])
    P.op("pool", lambda e: e.memset(ones_f[:], 1.0), writes=["ones"])
    P.dma("sp", gbc[:], io["norm1_g"].partition_broadcast(128), "c1", writes=["gbc"])
    P.dma("sp", maskf[:], io["amask"].rearrange("k (p q) -> k p q", p=3), "c2", writes=["maskf"])
    for p in range(3):
        P.dma("sp", xb[0][:, 0:1024], io["abias"][:, p * 1024:(p + 1) * 1024], ("xl", 0), writes=[("xb", 0)])
        P.op("dve", lambda e, p=p: e.tensor_tensor(
            out=biasT[:, p, :, :], in0=xb[0][:, 0:1024].rearrange("k (h q) -> k h q", h=4),
            in1=maskf[:, p:p + 1, :].to_broadcast([128, 4, 256]), op=ALU.add),
            reads=[("xb", 0), "maskf"], writes=["biasT"])
    for kc in range(16):
        s = kc % 2
        P.dma("sp", wst[s][:], io["w_in_a"][kc * 128:(kc + 1) * 128, :], ("awl", s), writes=[("wst", s)])
        P.op("pool" if kc % 2 else "dve", lambda e, s=s, kc=kc: e.tensor_copy(out=wA[:, kc, :], in_=wst[s][:]),
             reads=[("wst", s)], writes=["wA"])
    for p in range(3):
        P.op("pool", lambda e, p=p: e.memset(Vtok[p][:, :, :, 64:65], 1.0), writes=[("Vones", p)])

    ctr = {"o": 0, "t": 0, "s": 0, "m": 0, "pp": 0}

    def nfront(t, tb):
        blk = t * 4 + tb
        xs = blk % 2
        P.dma("sp", xb[xs][:], x[blk * 128:(blk + 1) * 128, :], ("xl", xs), writes=[("xb", xs)])
        norm_front(P, xb[xs][:], ("xb", xs), gbc, junk, ssq[xs], ub[xs], ("ub", xs), xs)

    def nback(t, tb):
        us = t % 2
        xs = (t * 4 + tb) % 2
        norm_back(P, ub[xs], ("ub", xs), pT[0], uT[us][:, :, tb * 128:(tb + 1) * 128], ("uT", us), ident)

    def ut_store(t):
        us = t % 2
        P.dma("sp", io["uT_scr"][t].rearrange("p (kc t) -> p kc t", t=512), uT[us][:], ("uts", us), reads=[("uT", us)])

    def proj_group(t, cb):
        us = t % 2
        ring0 = (t % 8) * 512
        loc0 = (t % 4) * 512
        pp = pproj[ctr["pp"] % 2]
        ppk = ("pp", ctr["pp"] % 2)
        ctr["pp"] += 1
        for kc in range(16):
            P.op("pe", lambda e, pp=pp, cb=cb, kc=kc, us=us: e.matmul(
                pp[:], lhsT=wA[:, kc, cb * 128:(cb + 1) * 128], rhs=uT[us][:, kc, :], start=(kc == 0), stop=(kc == 15)),
                reads=["wA", ("uT", us)], writes=[ppk], inc=(kc == 15))
        hp = cb % 2
        if cb < 2:
            P.op("act", lambda e, pp=pp, hp=hp, loc0=loc0: e.activation(out=QT[hp][:, loc0:loc0 + 512], in_=pp[:], func=AF.Copy, scale=0.125),
                 reads=[ppk], writes=[("QT", hp)])
        elif cb < 4:
            P.op("dve", lambda e, pp=pp, hp=hp, ring0=ring0: e.tensor_copy(out=KT[hp][:, ring0:ring0 + 512], in_=pp[:]),
                 reads=[ppk], writes=[("KT", hp)])
        else:
            P.op("act", lambda e, pp=pp, hp=hp, ring0=ring0: e.activation(out=VT[hp][:, ring0:ring0 + 512], in_=pp[:], func=AF.Copy),
                 reads=[ppk], writes=[("VT", hp)])

    for sc in range(nsc):
        if sc == 0:
            nfront(0, 0)
            for tb in range(4):
                if tb + 1 < 4:
                    nfront(0, tb + 1)
                nback(0, tb)
            ut_store(0)
        for tl in range(4):
            t = sc * 4 + tl
            nxt = t + 1 if t + 1 < nsc * 4 else None
            sched = ["f0", "g0", "b0", "f1", "g1", "b1", "f2", "g2", "b2", "f3", "g3", "b3", "g4", "g5"]
            for it in sched:
                if it[0] == "g":
                    proj_group(t, int(it[1]))
                elif nxt is not None:
                    (nfront if it[0] == "f" else nback)(nxt, int(it[1]))
            if nxt is not None:
                ut_store(nxt)
        rbase = (sc % 2) * 2048
        for hp in range(2):
            for p, (win, dil) in enumerate(PATTERNS):
                pdist = dil
                pend = []
                po = pok = None
                for bi in range(16):
                    grp, gi = bi // 2, bi % 2
                    if gi == 0:
                        po = pO[ctr["o"] % 2]
                        pok = ("pO", ctr["o"] % 2)
                        ctr["o"] += 1
                    if dil == 1:
                        nl, r = bi, 0
                    elif dil == 4:
                        nl, r = bi // 4, bi % 4
                    else:
                        nl, r = 0, bi
                    gblk = sc * 16 + bi
                    has_prev = (sc * 2048 + nl * win) >= win
                    cur_slot = gblk % RING[p]
                    prev_slot = (gblk - pdist) % RING[p]
                    c0 = rbase + nl * win + r
                    q0 = nl * win + r
                    pr0 = (c0 - win) % 4096
                    cur_sl = slice(c0, c0 + 127 * dil + 1, dil)
                    prev_sl = slice(pr0, pr0 + 127 * dil + 1, dil)
                    q_sl = slice(q0, q0 + 127 * dil + 1, dil)
                    tb_ = ctr["t"] % 2
                    tj = tb_ * 8
                    ctr["t"] += 1
                    P.op("pe", lambda e, hp=hp, cur_sl=cur_sl, tj=tj: e.transpose(out=pT[0][:, tj, :], in_=VT[hp][:, cur_sl], identity=ident[:]),
                         reads=[("VT", hp), "ident"], writes=[("pT", tb_)])
                    P.op("dve", lambda e, p=p, cur_slot=cur_slot, hp=hp, tj=tj: e.tensor_copy(
                        out=Vtok[p][:, cur_slot, 2 * hp:2 * hp + 2, 0:64], in_=pT[0][:, tj, :].rearrange("k (h d) -> k h d", h=2)),
                        reads=[("pT", tb_)], writes=[("Vtok", p, cur_slot, hp)])
                    ps_ = pS[ctr["s"] % 2]
                    psk = ("pS", ctr["s"] % 2)
                    pt_ = PT[ctr["s"] % 3]
                    ptk = ("PT", ctr["s"] % 3)
                    ctr["s"] += 1
                    parts = ([0] if has_prev else []) + [1]
                    nmm = len(parts) * 2
                    mi = 0
                    for hl in range(2):
                        for part in parts:
                            ksl = prev_sl if part == 0 else cur_sl
                            col = (hl * 2 + part) * 128
                            mi += 1
                            P.op("pe", lambda e, ps_=ps_, hp=hp, hl=hl, ksl=ksl, q_sl=q_sl, col=col, first=(mi == 1), last=(mi == nmm): e.matmul(
                                ps_[:, col:col + 128], lhsT=KT[hp][hl * 64:(hl + 1) * 64, ksl], rhs=QT[hp][hl * 64:(hl + 1) * 64, q_sl],
                                start=first, stop=last, skip_group_check=True),
                                reads=[("KT", hp), ("QT", hp)], writes=[psk], inc=(mi == nmm))
                        if hl == 0:
                            P.op("pe", lambda e, ps_=ps_, p=p, hp=hp: e.matmul(
                                ps_[:], lhsT=ident[:], rhs=biasT[:, p, 2 * hp:2 * hp + 2, :].rearrange("k h q -> k (h q)"),
                                start=False, stop=False, skip_group_check=True),
                                reads=["ident", "biasT"], writes=[psk], inc=False)
                    if has_prev:
                        P.op("act", lambda e, ps_=ps_, pt_=pt_: e.activation(out=pt_[:], in_=ps_[:], func=AF.Exp),
                             reads=[psk], writes=[ptk])
                    else:
                        P.op("act", lambda e, ps_=ps_, pt_=pt_: e.activation(
                            out=pt_[:].rearrange("k (h c) -> k h c", h=2)[:, :, 128:256],
                            in_=ps_[:].rearrange("k (h c) -> k h c", h=2)[:, :, 128:256], func=AF.Exp),
                            reads=[psk], writes=[ptk])

                    def pv(gi=gi, parts=parts, prev_slot=prev_slot, cur_slot=cur_slot, pt_=pt_, ptk=ptk, po=po, pok=pok,
                           nl=nl, r=r, bi=bi, p=p, hp=hp, dil=dil, win=win):
                        for hl in range(2):
                            h = 2 * hp + hl
                            for pi_, part in enumerate(parts):
                                slot = prev_slot if part == 0 else cur_slot
                                col = (hl * 2 + part) * 128
                                last = (gi == 1 and hl == 1 and pi_ == len(parts) - 1)
                                oc = (hl * 2 + gi) * 128
                                P.op("pe", lambda e, slot=slot, h=h, col=col, oc=oc, pi_=pi_, np_=len(parts): e.matmul(
                                    po[0:65, oc:oc + 128], lhsT=Vtok[p][:, slot, h, :], rhs=pt_[:, col:col + 128],
                                    start=(pi_ == 0), stop=(pi_ == np_ - 1)),
                                    reads=[("Vtok", p, slot, hp), ("Vones", p), ptk], writes=[pok], inc=last)
                        if gi == 1:
                            if dil == 1:
                                nl0, r0 = nl - 1, 0
                            else:
                                nl0, r0 = nl, r - 1
                            for hl in range(2):
                                src = po[0:65, hl * 256:(hl + 1) * 256].rearrange("d (j i) -> d j i", j=2)
                                if dil == 1:
                                    dst = acc[hl][:, nl0 * 128:nl0 * 128 + 256].rearrange("d (j i) -> d j i", j=2)
                                else:
                                    dst = acc[hl][:, nl0 * win:nl0 * win + 128 * dil].rearrange("d (i r) -> d r i", r=dil)[:, r0:r0 + 2, :]
                                if p == 0:
                                    P.op("act", lambda e, dst=dst, src=src: e.activation(out=dst, in_=src, func=AF.Copy), reads=[pok], writes=[("acc", hl)])
                                else:
                                    P.op("dve", lambda e, dst=dst, src=src: e.tensor_tensor(out=dst, in0=src, in1=dst, op=ALU.add),
                                         reads=[pok, ("acc", hl)], writes=[("acc", hl)])
                    if pend:
                        pend.pop()()
                    pend.append(pv)
                if pend:
                    pend.pop()()
            for hl in range(2):
                h = 2 * hp + hl
                m = mo[ctr["m"] % 2]
                mk = ("mo", ctr["m"] % 2)
                ctr["m"] += 1
                for pc in range(4):
                    po = pO[ctr["o"] % 2]
                    pok = ("pO", ctr["o"] % 2)
                    ctr["o"] += 1
                    P.op("pe", lambda e, po=po, hl=hl, pc=pc: e.matmul(po[0:64, :], lhsT=ones_f[64:65, 0:64], rhs=acc[hl][64:65, pc * 512:(pc + 1) * 512],
                                                                    start=True, stop=True),
                         reads=[("acc", hl), "ones"], writes=[pok])
                    rc = recs[pc % 2]
                    rck = ("rec", pc % 2)
                    P.op("dve", lambda e, po=po, rc=rc: e.reciprocal(out=rc[:], in_=po[0:64, :]), reads=[pok], writes=[rck])
                    P.op("pool", lambda e, m=m, hl=hl, pc=pc, rc=rc: e.tensor_tensor(out=m[:, pc * 512:(pc + 1) * 512], in0=acc[hl][0:64, pc * 512:(pc + 1) * 512],
                                                                           in1=rc[:], op=ALU.mult),
                         reads=[("acc", hl), rck], writes=[mk])
                P.dma("sp", io["mix_loc"][2 * sc:2 * sc + 2, h * 64:(h + 1) * 64, :].rearrange("j r t -> r j t"),
                      m[:].rearrange("r (j t) -> r j t", j=2), ("mos", mk[1]), reads=[mk])
    P.flush()
    A.close()


def phase_d(P, nc, io, ntile=4):
    A = Alloc(nc)
    xs = io["xs"]
    mix_gath = io["mix_gath"]
    out = io["out"]
    RS = 4
    ring = [A.sb(f"d_wr{i}", [128, 16, 512], BF16) for i in range(RS)]
    aT16 = A.sb("d_aT16", [128, 16, 512], BF16)
    mT16 = A.sb("d_mT16", [128, 16, 512], BF16)
    actT = A.sb("d_actT", [128, 44, 512], BF16)
    h = A.sb("d_h", [128, 4, 2048], F32)
    g2 = A.sb("d_g2", [128, 2048], F32)
    gF = A.sb("d_gF", [128, 2048], F32)
    junk = A.sb("d_junk", [128, 2048], BF16)
    ub = [A.sb(f"d_ub{i}", [128, 2048], BF16) for i in range(2)]
    ssq = [A.sb(f"d_ssq{i}", [128, 4], F32) for i in range(4)]
    sg = [A.sb(f"d_sg{i}", [128, 512], F32) for i in range(3)]
    ident = A.sb("d_ident", [128, 128], BF16)
    identf = A.sb("d_identf", [128, 128], F32)
    pb = [A.ps(f"d_pb{i}", [128, 512], F32) for i in range(6)]
    pT = A.ps("d_pT", [128, 16, 128], BF16)

    P.dma("pool", identf[:], io["ident"], "c0", writes=["identf"])
    P.op("dve", lambda e: e.tensor_copy(out=ident[:], in_=identf[:]), reads=["identf"], writes=["ident"])
    P.dma("pool", g2[:], io["norm2_g"].partition_broadcast(128), "c1", writes=["gbc"])
    P.dma("pool", gF[:], io["final_g"].partition_broadcast(128), "c2", writes=["gF"])

    seq = []
    for tt in range(ntile):
        for db in range(4):
            seq.append((io["wo_scr"], wtile_index(2048, 0, db), 16))
        for j in range(11):
            seq.append((io["wgu_scr"], wtile_index(11264, 0, j), 16))
            seq.append((io["wgu_scr"], wtile_index(11264, 0, 11 + j), 16))
        for db in range(4):
            for kg in range(3):
                seq.append((io["wd_scr"], wtile_index(2048, kg, db), 16 if kg < 2 else 12))
    st = {"issued": 0, "used": 0}

    def issue_one():
        n = st["issued"]
        if n >= len(seq):
            return
        scr, tix, nk = seq[n]
        s = n % RS
        P.dma("sp", ring[s][:, 0:nk, :], scr[tix].rearrange("p (kc c) -> p kc c", c=512)[:, 0:nk, :], ("wr", s), writes=[("ring", s)])
        st["issued"] += 1

    def next_w():
        n = st["used"]
        while st["issued"] < min(len(seq), n + RS - 1):
            issue_one()
        st["used"] += 1
        return ring[n % RS], ("ring", n % RS)

    bctr = {"b": 0}

    def bank():
        i = bctr["b"] % 6
        bctr["b"] += 1
        return pb[i], ("pb", i)

    def mix_load(tt):
        t0 = tt * 512

        def _mix_load(e, t0=t0):
            q = e.partition_id() % 4
            tl0 = (t0 % 1024)
            src = mix_gath.rearrange("j (cc p) t -> p j cc t", p=128)[:, bass.ds(q * 2 + t0 // 1024, 1), :, tl0:tl0 + 512]
            return e.dma_start(out=mT16[:].unsqueeze(1), in_=src)
        P.dma_fn("pool", _mix_load, "mixl", writes=["mT16"])

    def x_load(tt, tb):
        r0 = tt * 512 + tb * 128
        P.dma("pool", h[:, tb, :], xs[r0:r0 + 128, :], ("xl", tb), writes=[("h", tb)])

    for tt in range(ntile):
        t0 = tt * 512
        if tt == 0:
            mix_load(0)
            for tb in range(4):
                x_load(0, tb)
        for db in range(4):
            w, wk = next_w()
            for tb in range(4):
                b_, bk = bank()
                for cc in range(16):
                    P.op("pe", lambda e, b_=b_, w=w, cc=cc, tb=tb: e.matmul(
                        b_[:], lhsT=mT16[:, cc, tb * 128:(tb + 1) * 128], rhs=w[:, cc, :], start=(cc == 0), stop=(cc == 15)),
                        reads=["mT16", wk], writes=[bk], inc=(cc == 15))
                hs = h[:, tb, db * 512:(db + 1) * 512]
                P.op("dve", lambda e, hs=hs, b_=b_: e.tensor_tensor(out=hs, in0=b_[:], in1=hs, op=ALU.add),
                     reads=[bk, ("h", tb)], writes=[("h", tb)])
        if tt + 1 < ntile:
            mix_load(tt + 1)
        for tb in range(4):
            norm_transpose_block(P, h[:, tb, :], ("h", tb), g2, junk, ssq[tb], ub[tb % 2], ("ub", tb % 2),
                                 pT, "pT", aT16[:, :, tb * 128:(tb + 1) * 128], "aT16", ident, tb,
                                 evac_eng="act" if tb % 2 else "dve")
        for j in range(11):
            wg, wgk = next_w()
            wu, wuk = next_w()
            for fs in range(4):
                bg, bgk = bank()
                bu, buk = bank()
                for kc in range(16):
                    P.op("pe", lambda e, bg=bg, wg=wg, kc=kc, fs=fs: e.matmul(
                        bg[:], lhsT=wg[:, kc, fs * 128:(fs + 1) * 128], rhs=aT16[:, kc, :], start=(kc == 0), stop=(kc == 15)),
                        reads=["aT16", wgk], writes=[bgk], inc=(kc == 15))
                for kc in range(16):
                    P.op("pe", lambda e, bu=bu, wu=wu, kc=kc, fs=fs: e.matmul(
                        bu[:], lhsT=wu[:, kc, fs * 128:(fs + 1) * 128], rhs=aT16[:, kc, :], start=(kc == 0), stop=(kc == 15)),
                        reads=["aT16", wuk], writes=[buk], inc=(kc == 15))
                fi = j * 4 + fs
                s_ = sg[fi % 3]
                sk = ("sg", fi % 3)
                P.op("act", lambda e, s_=s_, bg=bg: e.activation(out=s_[:], in_=bg[:], func=AF.Silu), reads=[bgk], writes=[sk])
                P.op("dve", lambda e, s_=s_, bu=bu, fi=fi: e.tensor_tensor(out=actT[:, fi, :], in0=bu[:], in1=s_[:], op=ALU.mult),
                     reads=[buk, sk], writes=[("actT", fi)])
        for db in range(4):
            accb = [bank() for _ in range(4)]
            for kg in range(3):
                w, wk = next_w()
                nk = 16 if kg < 2 else 12
                for tb in range(4):
                    b_, bk = accb[tb]
                    for kl in range(nk):
                        fc = kg * 16 + kl
                        first = (fc == 0)
                        last = (fc == 43)
                        P.op("pe", lambda e, b_=b_, w=w, kl=kl, fc=fc, tb=tb, first=first, last=last: e.matmul(
                            b_[:], lhsT=actT[:, fc, tb * 128:(tb + 1) * 128], rhs=w[:, kl, :], start=first, stop=last),
                            reads=[("actT", fc), wk], writes=[bk], inc=(kl == nk - 1))
            for tb in range(4):
                b_, bk = accb[tb]
                hs = h[:, tb, db * 512:(db + 1) * 512]
                P.op("dve", lambda e, hs=hs, b_=b_: e.tensor_tensor(out=hs, in0=b_[:], in1=hs, op=ALU.add),
                     reads=[bk, ("h", tb)], writes=[("h", tb)])
        for tb in range(4):
            sq = ssq[tb]
            hs = h[:, tb, :]
            P.op("act", lambda e, hs=hs, sq=sq: e.activation(out=junk[:], in_=hs, func=AF.Square, accum_out=sq[:, 0:1]),
                 reads=[("h", tb)], writes=["junk", ("ssq", tb)])
            P.op("act", lambda e, sq=sq: e.activation(out=sq[:, 1:2], in_=sq[:, 0:1], func=AF.Sqrt, scale=1.0 / D_MODEL, bias=RMS_EPS),
                 reads=[("ssq", tb)], writes=[("ssq", tb)])
            P.op("dve", lambda e, sq=sq: e.reciprocal(out=sq[:, 2:3], in_=sq[:, 1:2]), reads=[("ssq", tb)], writes=[("ssq", tb)])
            P.op("dve", lambda e, hs=hs, sq=sq: e.scalar_tensor_tensor(out=hs, in0=hs, scalar=sq[:, 2:3], in1=gF[:], op0=ALU.mult, op1=ALU.mult),
                 reads=[("h", tb), ("ssq", tb), "gF"], writes=[("h", tb)])
            P.dma("pool", out[t0 + tb * 128:t0 + (tb + 1) * 128, :], h[:, tb, :], ("outs", tb), reads=[("h", tb)])
            if tt + 1 < ntile:
                x_load(tt + 1, tb)
    P.flush()
    A.close()


V_MIX_R, V_MIX_K, V_MIX_V = 0, 4, 8
V_MIX_W, V_MIX_A, V_MIX_G0, V_MIX_G1 = 12, 13, 14, 15
V_W0, V_A0, V_KK, V_KA, V_RK, V_LNW, V_LNB, V_OMKA = 16, 20, 24, 28, 36, 40, 44, 48
C_SU, C_IU, C_SL, C_ID, C_AVG, C_ONE, C_RST = 0, 64, 128, 192, 256, 320, 384


R_STOP = 99
R_GS = 2


def phase_r(P, nc, io, ntile=SEQ // 512, wspecs=None, gather=False):
    A = Alloc(nc)
    GS = R_GS
    DS = DECAY_SCALE
    wR = A.sb("r_wR", [128, 16, RW_COLS], BF16)
    wstg = A.sb("r_wstg", [128, 2, 2, 512], F32)
    wob = A.sb("r_wob", [128, 2, 2, 512], BF16)
    wst = [wstg[:].rearrange("p a b c -> p (a b c)")[:, 0:RW_COLS]]
    uT = A.sb("r_uT", [128, 16, 512], BF16)
    vec = A.sb("r_vec", [128, 64], F32)
    cst = A.sb("r_cst", [64, 1024], F32)
    wupb = A.sb("r_wupb", [64, 256], BF16)
    aupb = A.sb("r_aupb", [64, 256], BF16)
    gupb0 = A.sb("r_gupb0", [128, 256], BF16)
    gupb1 = A.sb("r_gupb1", [32, 256], BF16)
    Ib = A.sb("r_Ib", [64, 64], BF16)
    mask4 = A.sb("r_mask4", [64, 2, 2, 64], F32)
    zf = [A.sb(f"r_zf{i}", [128, 516], F32) for i in range(2)]
    tmpd = [A.sb(f"r_tmpd{i}", [128, 512], F32) for i in range(2)]
    cr = A.sb("r_cr", [128, 16], F32)
    zs_rkv = A.sb("r_zs", [64, 12, 512], F32)
    zs_w = A.sb("r_zsw", [64, 512], F32)
    zs_a = A.sb("r_zsa", [64, 512], F32)
    zs_g0 = A.sb("r_zsg0", [128, 512], F32)
    zs_g1 = A.sb("r_zsg1", [32, 512], F32)
    lob_w = A.sb("r_lobw", [64, 512], BF16)
    lob_a = A.sb("r_loba", [64, 512], BF16)
    gsb0 = A.sb("r_gsb0", [128, 512], BF16)
    gsb1 = A.sb("r_gsb1", [32, 512], BF16)
    tn = ["cs", "ei", "ev", "ex", "er", "sq", "n", "kk", "km", "bv", "rk"]
    T = {n: A.sb(f"r_t_{n}", [64, 512], F32) for n in tn}
    T["yc"] = T["rk"]
    T["sw"] = zs_w
    T["as"] = zs_a
    RA = A.sb("r_RA", [64, 4, 8, 2, 64], BF16)
    KB = A.sb("r_KB", [64, 4, 8, 2, 64], BF16)
    TR = A.sb("r_TR", [64, 4, 8, 4, 64], BF16)
    PC = A.sb("r_PC", [64, 4, 8], F32)
    bon = A.sb("r_bon", [64, 4, 512], F32)
    gT = A.sb("r_gT", [64, 4, 512], F32)
    yT = A.sb("r_yT", [64, 4, 512], F32)
    mo = A.sb("r_mo", [64, 4, 512], BF16)
    TK = A.sb("r_TK", [64, 4, 4, 64], BF16)
    MS = A.sb("r_MS", [64, 4, 2, 2, 64], BF16)
    Lb = [A.sb(f"r_L{i}", [64, 4, 64], BF16) for i in range(2)]
    Nb = [A.sb(f"r_N{i}", [64, 4, 64], BF16) for i in range(2)]
    Pb = [A.sb(f"r_P{i}", [64, 4, 64], BF16) for i in range(2)]
    ILb = A.sb("r_IL", [64, 4, 64], BF16)
    WT = A.sb("r_WT", [64, 4, 64], BF16)
    Zs = A.sb("r_Zs", [64, 4, 64], BF16)
    Ut = A.sb("r_Ut", [64, 4, 64], F32)
    Ub = A.sb("r_Ub", [64, 4, 64], BF16)
    Hf = A.sb("r_Hf", [64, 4, 64], F32)
    Hd = A.sb("r_Hd", [64, 4, 64], F32)
    Hb = A.sb("r_Hb", [64, 4, 64], BF16)
    pproj = [A.ps(f"r_pp{i}", [128, 512], F32) for i in range(2)]
    ptrs = [A.ps(f"r_ptr{i}", [64, 2, 4, 64], BF16) for i in range(2)]
    pgen = [A.ps(f"r_pg{i}", [64, 512], F32) for i in range(4)]
    paux = pgen[0]
    banks = [(pgen[i], ("pg", i)) for i in range(4)] + [(pproj[i][0:64, :], ("pp", i)) for i in range(2)]
    bctr = {"b": 0}

    def bank():
        b_ = banks[bctr["b"] % len(banks)]
        bctr["b"] += 1
        return b_

    P.dma("sp", vec[:], io["rw_vec"], "c0", writes=["vec"])
    P.dma("sp", cst[:], io["rw_const"], "c1", writes=["cst"])
    P.dma("sp", tmpd[0][:], io["rw_mats"][:, 0:512], "c2", writes=[("tmpd", 0)])
    P.dma("sp", tmpd[1][:], io["rw_mats"][:, 512:1024], "c3", writes=[("tmpd", 1)])
    P.op("dve", lambda e: e.tensor_copy(out=wupb[:], in_=tmpd[0][0:64, 0:256]), reads=[("tmpd", 0)], writes=["wupb"])
    P.op("dve", lambda e: e.tensor_copy(out=aupb[:], in_=tmpd[0][0:64, 256:512]), reads=[("tmpd", 0)], writes=["aupb"])
    P.op("dve", lambda e: e.tensor_copy(out=gupb0[:], in_=tmpd[1][:, 0:256]), reads=[("tmpd", 1)], writes=["gupb0"])
    P.op("dve", lambda e: e.tensor_copy(out=gupb1[:], in_=tmpd[1][0:32, 256:512]), reads=[("tmpd", 1)], writes=["gupb1"])
    P.op("dve", lambda e: e.tensor_copy(out=Ib[:], in_=cst[:, C_ID:C_ID + 64]), reads=["cst"], writes=["Ib"])
    for a in range(2):
        P.op("dve", lambda e, a=a: e.tensor_copy(out=mask4[:, a, 0, :], in_=cst[:, C_SU:C_SU + 64]), reads=["cst"], writes=["mask4"])
        P.op("dve", lambda e, a=a: e.tensor_copy(out=mask4[:, a, 1, :], in_=cst[:, C_IU:C_IU + 64]), reads=["cst"], writes=["mask4"])
    P.op("dve", lambda e: e.tensor_scalar(out=vec[:, V_OMKA:V_OMKA + 4], in0=vec[:, V_KA:V_KA + 4], scalar1=-1.0, scalar2=1.0, op0=ALU.mult, op1=ALU.add),
         reads=["vec"], writes=["vec"])
    for kc in range(16):
        s = 0
        P.dma("sp", wst[s], io["w_in_r"][kc * 128:(kc + 1) * 128, :], ("rwl", s), writes=[("wstg", 0), ("wstg", 1)])
        P.op("pool" if kc % 2 else "dve", lambda e, s=s, kc=kc: e.tensor_copy(out=wR[:, kc, :], in_=wst[s]),
             reads=[("wstg", 0), ("wstg", 1)], writes=["wR"])
    P.op("pool", lambda e: e.memset(cr[:], 0.0), writes=["cr"])
    P.op("pool", lambda e: e.memset(Hf[:], 0.0), writes=[("Hf", 0), ("Hf", 1)])
    P.op("pool", lambda e: e.memset(Hb[:], 0.0), writes=[("Hb", 0), ("Hb", 1)])

    groups = []
    for qi, mv in enumerate((V_MIX_R, V_MIX_K, V_MIX_V)):
        for h in range(4):
            groups.append((qi * 256 + h * 64, 64, zs_rkv[:, qi * 4 + h, :], mv + h, ("zs", qi, h)))
    groups.append((768, 64, zs_w[:], V_MIX_W, "zsw"))
    groups.append((832, 64, zs_a[:], V_MIX_A, "zsa"))
    groups.append((896, 128, zs_g0[:], V_MIX_G0, "zsg0"))
    groups.append((1024, 32, zs_g1[:], V_MIX_G1, "zsg1"))

    ctr = {"pp": 0, "z": 0}

    def vcol(c, m=64):
        return vec[0:m, c:c + 1]

    def make_tile_fns():
        def proj_group(gi):
            (c0, M, dst, mcol, zkey) = groups[gi]
            pp = pproj[ctr["pp"] % 2]
            ppk = ("pp", ctr["pp"] % 2)
            ctr["pp"] += 1
            for kc in range(16):
                P.op("pe", lambda e, pp=pp, M=M, c0=c0, kc=kc: e.matmul(pp[0:M, :], lhsT=wR[:, kc, c0:c0 + M], rhs=uT[:, kc, :],
                                                                     start=(kc == 0), stop=(kc == 15)),
                     reads=["wR", "uT"], writes=[ppk], inc=(kc == 15))
            zi = ctr["z"] % 2
            ctr["z"] += 1
            z = zf[zi]
            zk = ("zf", zi)
            td = tmpd[zi]
            tdk = ("tmpd", zi)
            P.op("act", lambda e, z=z, pp=pp, M=M: e.activation(out=z[0:M, 1:513], in_=pp[0:M, :], func=AF.Copy), reads=[ppk], writes=[zk])
            P.op("pool", lambda e, z=z, M=M, gi=gi: e.tensor_copy(out=z[0:M, 0:1], in_=cr[0:M, gi:gi + 1]), reads=["cr", zk], writes=[zk])
            P.op("dve", lambda e, z=z, td=td, pp=pp, M=M: e.tensor_tensor(out=td[0:M, :], in0=z[0:M, 0:512], in1=pp[0:M, :], op=ALU.subtract),
                 reads=[zk, ppk], writes=[tdk])
            P.op("dve", lambda e, td=td, pp=pp, M=M, dst=dst, mcol=mcol: e.scalar_tensor_tensor(
                out=dst, in0=td[0:M, :], scalar=vec[0:M, mcol:mcol + 1], in1=pp[0:M, :], op0=ALU.mult, op1=ALU.add),
                reads=[tdk, ppk, "vec"], writes=[zkey])
            P.op("pool", lambda e, z=z, M=M, gi=gi: e.tensor_copy(out=cr[0:M, gi:gi + 1], in_=z[0:M, 512:513]), reads=[zk], writes=["cr"])
        def lora_acts():
            P.op("act", lambda e: e.activation(out=lob_w[:], in_=zs_w[:], func=AF.Tanh), reads=["zsw"], writes=["lobw"])
            P.op("pool", lambda e: e.tensor_copy(out=lob_a[:], in_=zs_a[:]), reads=["zsa"], writes=["loba"])
            P.op("act", lambda e: e.activation(out=gsb0[:], in_=zs_g0[:], func=AF.Sigmoid), reads=["zsg0"], writes=["gsb0"])
            P.op("act", lambda e: e.activation(out=gsb1[:], in_=zs_g1[:], func=AF.Sigmoid), reads=["zsg1"], writes=["gsb1"])
        def preproc(h):
            r_ = zs_rkv[:, h, :]
            k_ = zs_rkv[:, 4 + h, :]
            v_ = zs_rkv[:, 8 + h, :]
            rk_, kk_, vk_ = ("zs", 0, h), ("zs", 1, h), ("zs", 2, h)
            hs = slice(h * 64, (h + 1) * 64)
            P.op("pe", lambda e, hs=hs: e.matmul(paux[:], lhsT=wupb[:, hs], rhs=lob_w[:], start=True, stop=True),
                 reads=["wupb", "lobw"], writes=[("pg", 0)])
            P.op("act", lambda e, h=h: e.activation(out=T["sw"][:], in_=paux[:], func=AF.Sigmoid, bias=vcol(V_W0 + h)),
                 reads=[("pg", 0), "vec"], writes=["zsw"])
            P.op("dve", lambda e: e.tensor_tensor_scan(out=T["cs"][:], data0=cst[:, C_RST:C_RST + 512], data1=T["sw"][:], initial=0.0,
                                                       op0=ALU.mult, op1=ALU.add), reads=["zsw", "cst"], writes=["t_cs"])
            P.op("act", lambda e: e.activation(out=T["ei"][:], in_=T["cs"][:], func=AF.Exp, scale=-DS), reads=["t_cs"], writes=["t_ei"])
            P.op("act", lambda e: e.activation(out=T["ev"][:], in_=T["cs"][:], func=AF.Exp, scale=DS), reads=["t_cs"], writes=["t_ev"])
            P.op("pool", lambda e: e.tensor_tensor(out=T["ex"][:], in0=T["cs"][:], in1=T["sw"][:], op=ALU.subtract),
                 reads=["t_cs", "zsw"], writes=["t_ex"])
            P.op("act", lambda e: e.activation(out=T["ex"][:], in_=T["ex"][:], func=AF.Exp, scale=-DS), reads=["t_ex"], writes=["t_ex"])
            P.op("pool", lambda e, h=h: e.tensor_copy(out=PC[:, h, :], in_=T["ei"][:, 63:512:64]), reads=["t_ei"], writes=["PC"])
            P.op("dve", lambda e: e.tensor_tensor(out=T["er"][:].rearrange("j (c s) -> j c s", s=64),
                                                  in0=T["ev"][:].rearrange("j (c s) -> j c s", s=64),
                                                  in1=T["ei"][:].rearrange("j (c s) -> j c s", s=64)[:, :, 63:64].to_broadcast([64, 8, 64]),
                                                  op=ALU.mult), reads=["t_ev", "t_ei"], writes=["t_er"])
            P.op("pe", lambda e, hs=hs: e.matmul(paux[:], lhsT=aupb[:, hs], rhs=lob_a[:], start=True, stop=True),
                 reads=["aupb", "loba"], writes=[("pg", 0)])
            P.op("act", lambda e, h=h: e.activation(out=T["as"][:], in_=paux[:], func=AF.Sigmoid, bias=vcol(V_A0 + h)),
                 reads=[("pg", 0), "vec"], writes=["zsa"])
            P.op("act", lambda e, k_=k_, h=h: e.activation(out=T["sq"][:], in_=k_, func=AF.Square, scale=vcol(V_KK + h)),
                 reads=[kk_, "vec"], writes=["t_sq"])
            P.op("pe", lambda e: e.matmul(paux[:], lhsT=cst[:, C_ONE:C_ONE + 64], rhs=T["sq"][:], start=True, stop=True),
                 reads=["cst", "t_sq"], writes=[("pg", 0)])
            P.op("act", lambda e: e.activation(out=T["n"][:], in_=paux[:], func=AF.Sqrt), reads=[("pg", 0)], writes=["t_n"])
            P.op("dve", lambda e: e.tensor_scalar(out=T["n"][:], in0=T["n"][:], scalar1=1e-12, scalar2=None, op0=ALU.max), reads=["t_n"], writes=["t_n"])
            P.op("dve", lambda e: e.reciprocal(out=T["n"][:], in_=T["n"][:]), reads=["t_n"], writes=["t_n"])
            P.op("dve", lambda e, k_=k_, h=h: e.scalar_tensor_tensor(out=T["kk"][:], in0=k_, scalar=vcol(V_KK + h), in1=T["n"][:],
                                                                   op0=ALU.mult, op1=ALU.mult), reads=[kk_, "vec", "t_n"], writes=["t_kk"])
            P.op("pool", lambda e, h=h: e.tensor_scalar(out=T["km"][:], in0=T["as"][:], scalar1=vcol(V_KA + h), scalar2=vcol(V_OMKA + h),
                                                     op0=ALU.mult, op1=ALU.add), reads=["zsa", "vec"], writes=["t_km"])
            P.op("pool", lambda e, k_=k_: e.tensor_tensor(out=T["km"][:], in0=T["km"][:], in1=k_, op=ALU.mult), reads=["t_km", kk_], writes=["t_km"])
            P.op("pool", lambda e: e.tensor_tensor(out=T["bv"][:], in0=T["kk"][:], in1=T["as"][:], op=ALU.mult), reads=["t_kk", "zsa"], writes=["t_bv"])

            def c3(ap):
                return ap.rearrange("j (c s) -> j c s", s=64)
            P.op("dve", lambda e, h=h: e.scalar_tensor_tensor(out=RA[:, h, :, 0, :], in0=c3(T["kk"][:]), scalar=-1.0, in1=c3(T["ex"][:]),
                                                            op0=ALU.mult, op1=ALU.mult), reads=["t_kk", "t_ex"], writes=[("RA", h)])
            P.op("pool", lambda e, h=h, r_=r_: e.tensor_tensor(out=RA[:, h, :, 1, :], in0=c3(r_), in1=c3(T["ei"][:]), op=ALU.mult),
                 reads=[rk_, "t_ei"], writes=[("RA", h)])
            P.op("dve", lambda e, h=h: e.tensor_tensor(out=KB[:, h, :, 0, :], in0=c3(T["km"][:]), in1=c3(T["ev"][:]), op=ALU.mult),
                 reads=["t_km", "t_ev"], writes=[("KB", h)])
            P.op("pool", lambda e, h=h: e.tensor_tensor(out=KB[:, h, :, 1, :], in0=c3(T["bv"][:]), in1=c3(T["ev"][:]), op=ALU.mult),
                 reads=["t_bv", "t_ev"], writes=[("KB", h)])
            P.op("dve", lambda e, h=h: e.tensor_tensor(out=TR[:, h, :, 0, :], in0=c3(T["km"][:]), in1=c3(T["er"][:]), op=ALU.mult),
                 reads=["t_km", "t_er"], writes=[("TR", h)])
            P.op("pool", lambda e, h=h: e.tensor_tensor(out=TR[:, h, :, 1, :], in0=c3(T["bv"][:]), in1=c3(T["er"][:]), op=ALU.mult),
                 reads=["t_bv", "t_er"], writes=[("TR", h)])
            P.op("act", lambda e, h=h, v_=v_: e.activation(out=TR[:, h, :, 2, :], in_=c3(v_), func=AF.Copy), reads=[vk_], writes=[("TR", h)])
            P.op("pool", lambda e, h=h: e.tensor_copy(out=TR[:, h, :, 3, :], in_=RA[:, h, :, 0, :]), reads=[("RA", h)], writes=[("TR", h)])
            P.op("dve", lambda e, h=h, r_=r_: e.scalar_tensor_tensor(out=T["rk"][:], in0=r_, scalar=vcol(V_RK + h), in1=T["km"][:],
                                                                   op0=ALU.mult, op1=ALU.mult), reads=[rk_, "vec", "t_km"], writes=["t_rk"])
            P.op("pe", lambda e: e.matmul(paux[:], lhsT=cst[:, C_ONE:C_ONE + 64], rhs=T["rk"][:], start=True, stop=True),
                 reads=["cst", "t_rk"], writes=[("pg", 0)])
            P.op("dve", lambda e, h=h, v_=v_: e.tensor_tensor(out=bon[:, h, :], in0=paux[:], in1=v_, op=ALU.mult), reads=[("pg", 0), vk_], writes=[("bon", h)])
            P.op("pe", lambda e, hs=hs: e.matmul(paux[:], lhsT=gupb0[:, hs], rhs=gsb0[:], start=True, stop=False),
                 reads=["gupb0", "gsb0"], writes=[("pg", 0)], inc=False)
            P.op("pe", lambda e, hs=hs: e.matmul(paux[:], lhsT=gupb1[:, hs], rhs=gsb1[:], start=False, stop=True),
                 reads=["gupb1", "gsb1"], writes=[("pg", 0)])
            P.op("act", lambda e, h=h: e.activation(out=gT[:, h, :], in_=paux[:], func=AF.Copy), reads=[("pg", 0)], writes=[("gT", h)])

        def chunks():
            def chunk_group(c, g):
                hh = tuple(range(GS * g, GS * g + GS))
                hs2 = slice(GS * g, GS * g + GS)
                K_ = lambda n, *a: (n, g) + a
                if GS == 2:
                    pt, ptk = ptrs[g], ("ptr", g)
                else:
                    pt, ptk = ptrs[g // 2][:, g % 2:g % 2 + 1], ("ptr", g // 2)
                for hi, h in enumerate(hh):
                    for q in range(4):
                        P.op("pe", lambda e, hi=hi, h=h, q=q: e.transpose(out=pt[:, hi, q, :], in_=TR[:, h, c, q, :], identity=Ib[:]),
                             reads=[("TR", h), "Ib"], writes=[ptk], inc=(hi == GS - 1 and q == 3))
                P.op("act", lambda e: e.activation(out=TK[:, hs2], in_=pt[:], func=AF.Copy), reads=[ptk], writes=[K_("TK")])
                yield
                b1, b1k = bank()
                v1 = b1[:, 0:GS * 256].rearrange("s (h a x) -> s h a x", h=GS, a=2)
                for hi, h in enumerate(hh):
                    rhs = RA[:, h, c, :, :].rearrange("j a t -> j (a t)")
                    P.op("pe", lambda e, hi=hi, h=h, rhs=rhs: e.matmul(v1[:, hi, 0, :], lhsT=KB[:, h, c, 1, :], rhs=rhs, start=True, stop=True),
                         reads=[("KB", h), ("RA", h)], writes=[b1k], inc=False)
                    P.op("pe", lambda e, hi=hi, h=h, rhs=rhs: e.matmul(v1[:, hi, 1, :], lhsT=KB[:, h, c, 0, :], rhs=rhs, start=True, stop=True),
                         reads=[("KB", h), ("RA", h)], writes=[b1k], inc=(hi == GS - 1))
                b2, b2k = bank()
                v2 = b2[:, 0:GS * 64].rearrange("s (h t) -> s h t", h=GS)
                for hi, h in enumerate(hh):
                    P.op("pe", lambda e, hi=hi, h=h: e.matmul(v2[:, hi, :], lhsT=RA[:, h, c, 0, :], rhs=KB[:, h, c, 1, :], start=True, stop=True),
                         reads=[("KB", h), ("RA", h)], writes=[b2k], inc=(hi == GS - 1))
                P.op("dve", lambda e: e.tensor_tensor(
                    out=MS[:, hs2].rearrange("s h a b t -> s h (a b t)"), in0=v1.rearrange("s h a x -> s h (a x)"),
                    in1=mask4[:].rearrange("s a b t -> s (a b t)").unsqueeze(1).to_broadcast([64, GS, 256]), op=ALU.mult),
                    reads=[b1k, "mask4"], writes=[K_("MS")])
                P.op("dve", lambda e: e.tensor_tensor(out=Lb[0][:, hs2], in0=v2, in1=cst[:, C_SL:C_SL + 64].unsqueeze(1).to_broadcast([64, GS, 64]),
                                                      op=ALU.mult), reads=[b2k, "cst"], writes=[K_("L", 0)])
                P.op("pool", lambda e: e.tensor_tensor(out=Pb[0][:, hs2], in0=MS[:, hs2, 0, 0, :], in1=cst[:, C_ID:C_ID + 64].unsqueeze(1).to_broadcast([64, GS, 64]),
                                                       op=ALU.add), reads=[K_("MS"), "cst"], writes=[K_("P", 0)])
                yield
                for k in range(5):
                    li, lo = k % 2, (k + 1) % 2
                    Nk = (lambda h: MS[:, h, 0, 0, :]) if k == 0 else (lambda h, li=li: Nb[li][:, h, :])
                    nkey = K_("MS") if k == 0 else K_("N", li)
                    bl, blk = bank()
                    vl = bl[:, 0:GS * 64].rearrange("s (h t) -> s h t", h=GS)
                    for hi, h in enumerate(hh):
                        P.op("pe", lambda e, hi=hi, h=h, Nk=Nk, li=li, vl=vl: e.matmul(vl[:, hi, :], lhsT=Nk(h), rhs=Lb[li][:, h, :], start=True, stop=True),
                             reads=[nkey, K_("L", li)], writes=[blk], inc=(hi == GS - 1))
                    if k < 4:
                        bn, bnk = bank()
                        vn = bn[:, 0:GS * 64].rearrange("s (h t) -> s h t", h=GS)
                        for hi, h in enumerate(hh):
                            P.op("pe", lambda e, hi=hi, h=h, Nk=Nk, li=li, vn=vn: e.matmul(vn[:, hi, :], lhsT=Lb[li][:, h, :], rhs=Nk(h), start=True, stop=True),
                                 reads=[nkey, K_("L", li)], writes=[bnk], inc=(hi == GS - 1))
                    P.op("act", lambda e, lo=lo, vl=vl: e.activation(out=Lb[lo][:, hs2], in_=vl, func=AF.Copy), reads=[blk], writes=[K_("L", lo)])
                    if k < 4:
                        P.op("dve", lambda e, lo=lo, vn=vn: e.tensor_copy(out=Nb[lo][:, hs2], in_=vn), reads=[bnk], writes=[K_("N", lo)])
                    yield
                    bp, bpk = bank()
                    vp = bp[:, 0:GS * 64].rearrange("s (h t) -> s h t", h=GS)
                    for hi, h in enumerate(hh):
                        P.op("pe", lambda e, hi=hi, h=h, li=li, lo=lo, vp=vp: e.matmul(vp[:, hi, :], lhsT=Lb[lo][:, h, :], rhs=Pb[li][:, h, :], start=True, stop=False),
                             reads=[K_("L", lo), K_("P", li)], writes=[bpk], inc=False)
                        P.op("pe", lambda e, hi=hi, h=h, li=li, vp=vp: e.matmul(vp[:, hi, :], lhsT=Ib[:], rhs=Pb[li][:, h, :], start=False, stop=True),
                             reads=["Ib", K_("P", li)], writes=[bpk], inc=(hi == GS - 1))
                    if k % 2 == 0:
                        P.op("dve", lambda e, lo=lo, vp=vp: e.tensor_copy(out=Pb[lo][:, hs2], in_=vp), reads=[bpk], writes=[K_("P", lo)])
                    else:
                        P.op("act", lambda e, lo=lo, vp=vp: e.activation(out=Pb[lo][:, hs2], in_=vp, func=AF.Copy), reads=[bpk], writes=[K_("P", lo)])
                    yield
                TT = Pb[1]
                ttk = K_("P", 1)
                bw, bwk = bank()
                vw = bw[:, 0:GS * 64].rearrange("s (h t) -> s h t", h=GS)
                for hi, h in enumerate(hh):
                    P.op("pe", lambda e, hi=hi, h=h: e.matmul(vw[:, hi, :], lhsT=TK[:, h, 3, :], rhs=TT[:, h, :], start=True, stop=True),
                         reads=[K_("TK"), ttk], writes=[bwk], inc=(hi == GS - 1))
                bz, bzk = bank()
                vz = bz[:, 0:GS * 64].rearrange("s (h t) -> s h t", h=GS)
                for hi, h in enumerate(hh):
                    P.op("pe", lambda e, hi=hi, h=h: e.matmul(vz[:, hi, :], lhsT=MS[:, h, 1, 0, :], rhs=TK[:, h, 2, :], start=True, stop=True),
                         reads=[K_("MS"), K_("TK")], writes=[bzk], inc=(hi == GS - 1))
                P.op("act", lambda e: e.activation(out=WT[:, hs2], in_=vw, func=AF.Copy), reads=[bwk], writes=[K_("WT")])
                P.op("dve", lambda e: e.tensor_copy(out=Zs[:, hs2], in_=vz), reads=[bzk], writes=[K_("Zs")])
                yield
                bu, buk = bank()
                vu = bu[:, 0:GS * 64].rearrange("s (h t) -> s h t", h=GS)
                for hi, h in enumerate(hh):
                    P.op("pe", lambda e, hi=hi, h=h: e.matmul(vu[:, hi, :], lhsT=TT[:, h, :], rhs=Zs[:, h, :], start=True, stop=True),
                         reads=[ttk, K_("Zs")], writes=[buk], inc=(hi == GS - 1))
                P.op("act", lambda e: e.activation(out=Ut[:, hs2], in_=vu, func=AF.Copy), reads=[buk], writes=[K_("Ut")])
                P.op("pool", lambda e: e.tensor_tensor(out=Hd[:, hs2], in0=Hf[:, hs2], in1=PC[:, hs2, c:c + 1].to_broadcast([64, GS, 64]), op=ALU.mult),
                     reads=[K_("Hf"), "PC"], writes=[K_("Hd")])
                yield
                b5, b5k = bank()
                v5 = b5[:, 0:GS * 64].rearrange("s (h t) -> s h t", h=GS)
                for hi, h in enumerate(hh):
                    P.op("pe", lambda e, hi=hi, h=h: e.matmul(v5[:, hi, :], lhsT=WT[:, h, :], rhs=Hb[:, h, :], start=True, stop=True),
                         reads=[K_("WT"), K_("Hb")], writes=[b5k], inc=(hi == GS - 1))
                P.op("dve", lambda e: e.tensor_tensor(out=Ub[:, hs2], in0=v5, in1=Ut[:, hs2], op=ALU.add), reads=[b5k, K_("Ut")], writes=[K_("Ub")])
                yield
                by, byk = bank()
                vy = by[:, 0:GS * 64].rearrange("s (h t) -> s h t", h=GS)
                for hi, h in enumerate(hh):
                    P.op("pe", lambda e, hi=hi, h=h: e.matmul(vy[:, hi, :], lhsT=Hb[:, h, :], rhs=RA[:, h, c, 1, :], start=True, stop=False),
                         reads=[K_("Hb"), ("RA", h)], writes=[byk], inc=False)
                    P.op("pe", lambda e, hi=hi, h=h: e.matmul(vy[:, hi, :], lhsT=Ub[:, h, :], rhs=MS[:, h, 0, 1, :], start=False, stop=False),
                         reads=[K_("Ub"), K_("MS")], writes=[byk], inc=False)
                    P.op("pe", lambda e, hi=hi, h=h: e.matmul(vy[:, hi, :], lhsT=TK[:, h, 2, :], rhs=MS[:, h, 1, 1, :], start=False, stop=True),
                         reads=[K_("TK"), K_("MS")], writes=[byk], inc=(hi == GS - 1))
                bh, bhk = bank()
                vh = bh[:, 0:GS * 64].rearrange("s (h t) -> s h t", h=GS)
                for hi, h in enumerate(hh):
                    P.op("pe", lambda e, hi=hi, h=h: e.matmul(vh[:, hi, :], lhsT=TK[:, h, 1, :], rhs=Ub[:, h, :], start=True, stop=False),
                         reads=[K_("TK"), K_("Ub")], writes=[bhk], inc=False)
                    P.op("pe", lambda e, hi=hi, h=h: e.matmul(vh[:, hi, :], lhsT=TK[:, h, 0, :], rhs=TK[:, h, 2, :], start=False, stop=True),
                         reads=[K_("TK")], writes=[bhk], inc=(hi == GS - 1))
                P.op("dve", lambda e: e.tensor_tensor(out=Hb[:, hs2], in0=vh, in1=Hd[:, hs2], op=ALU.add), reads=[bhk, K_("Hd")], writes=[K_("Hb")])
                P.op("dve", lambda e: e.tensor_tensor(out=Hf[:, hs2], in0=vh, in1=Hd[:, hs2], op=ALU.add), reads=[bhk, K_("Hd")], writes=[K_("Hf")])
                P.op("act", lambda e: e.activation(out=yT[:, hs2, c * 64:(c + 1) * 64], in_=vy, func=AF.Copy), reads=[byk], writes=[K_("yT")])
                yield

            for c in range(8):
                gens = [chunk_group(c, g) for g in range(4 // GS)]
                live = [True] * len(gens)
                rnd = 0
                quota = per_chunk
                while any(live):
                    if quota > 0 and rnd % 5 == 1:
                        w_steps(1)
                        quota -= 1
                    rnd += 1
                    for gi_, gn in enumerate(gens):
                        if live[gi_]:
                            try:
                                next(gn)
                            except StopIteration:
                                live[gi_] = False
                w_steps(quota)

        def output_head(h):
            y_ = yT[:, h, :]
            P.op("pe", lambda e, y_=y_: e.matmul(paux[:], lhsT=cst[:, C_AVG:C_AVG + 64], rhs=y_, start=True, stop=True),
                 reads=["cst", ("yT", 0), ("yT", 1)], writes=[("pg", 0)])
            P.op("dve", lambda e, y_=y_: e.tensor_tensor(out=T["yc"][:], in0=y_, in1=paux[:], op=ALU.subtract), reads=[("yT", 0), ("yT", 1), ("pg", 0)], writes=["t_rk"])
            P.op("act", lambda e: e.activation(out=T["sq"][:], in_=T["yc"][:], func=AF.Square), reads=["t_rk"], writes=["t_sq"])
            P.op("pe", lambda e: e.matmul(paux[:], lhsT=cst[:, C_AVG:C_AVG + 64], rhs=T["sq"][:], start=True, stop=True),
                 reads=["cst", "t_sq"], writes=[("pg", 0)])
            P.op("act", lambda e: e.activation(out=T["n"][:], in_=paux[:], func=AF.Sqrt, bias=GN_EPS), reads=[("pg", 0)], writes=["t_n"])
            P.op("dve", lambda e: e.reciprocal(out=T["n"][:], in_=T["n"][:]), reads=["t_n"], writes=["t_n"])
            P.op("pool", lambda e: e.tensor_tensor(out=T["yc"][:], in0=T["yc"][:], in1=T["n"][:], op=ALU.mult), reads=["t_rk", "t_n"], writes=["t_rk"])
            P.op("pool", lambda e, h=h: e.tensor_scalar(out=T["yc"][:], in0=T["yc"][:], scalar1=vcol(V_LNW + h), scalar2=vcol(V_LNB + h),
                                                     op0=ALU.mult, op1=ALU.add), reads=["t_rk", "vec"], writes=["t_rk"])
            P.op("pool", lambda e, h=h: e.tensor_tensor(out=T["yc"][:], in0=T["yc"][:], in1=bon[:, h, :], op=ALU.add), reads=["t_rk", ("bon", h)], writes=["t_rk"])
            P.op("dve", lambda e, h=h: e.tensor_tensor(out=mo[:, h, :], in0=T["yc"][:], in1=gT[:, h, :], op=ALU.mult),
                 reads=["t_rk", ("gT", h)], writes=["mo"])
        def store_mo(t):
            P.dma("sp", io["mix_loc"][t // 2, 256:512, (t % 2) * 512:(t % 2) * 512 + 512].rearrange("(h i) t -> i h t", h=4), mo[:], "rmo",
                  reads=["mo"], writes=[("mixslab", t // 2)])
        return proj_group, lora_acts, preproc, chunks, output_head, store_mo

    proj_group, lora_acts, preproc, chunks, output_head, store_mo = make_tile_fns()

    def gather_slab(j):
        P.custom("pool", lambda e, j=j: e.collective_compute(
            "AllGather", ALU.bypass, replica_groups=[[0, 1, 2, 3], [4, 5, 6, 7]],
            ins=[io["mix_loc"][j].opt()], outs=[io["mix_gath"][j].opt()]),
            key="cc", amt=1, reads=[("mixslab", j)], writes=[])

    witems = []
    for (src, dst, K, N) in (wspecs or []):
        nkc = K // 128
        for kg in range((nkc + 15) // 16):
            nk = min(16, nkc - kg * 16)
            for nb in range(N // 512):
                tix = wtile_index(N, kg, nb)
                for h0 in range(0, nk, 2):
                    r0 = (kg * 16 + h0) * 128
                    witems.append((src[r0:r0 + 256, nb * 512:(nb + 1) * 512].rearrange("(kc p) c -> p kc c", p=128),
                                   dst[tix].rearrange("p (kc c) -> p kc c", c=512)[:, h0:h0 + 2, :]))
    wst_ = {"i": 0}

    def w_load(i):
        if i < len(witems):
            P.dma("sp", wstg[:, i % 2], witems[i][0], ("wl", i % 2), writes=[("wstg", i % 2)])

    def w_steps(n):
        for _ in range(n):
            i = wst_["i"]
            if i >= len(witems):
                return
            if i == 0:
                w_load(0)
            w_load(i + 1)
            sl = i % 2
            P.op("pool", lambda e, sl=sl: e.tensor_copy(out=wob[:, sl], in_=wstg[:, sl]), reads=[("wstg", sl)], writes=[("wob", sl)])
            P.dma("sp", witems[i][1], wob[:, sl], ("ws", sl), reads=[("wob", sl)])
            wst_["i"] += 1
    per_chunk = -(-len(witems) // max(1, (ntile * 8 - 8))) if witems else 0
    for t in range(ntile):
        P.dma("sp", uT[:], io["uT_scr"][t].rearrange("p (kc t) -> p kc t", t=512), "utl", writes=["uT"])
        for gi in (12, 13, 14, 15):
            proj_group(gi)
        lora_acts()
        for h in range(4):
            if t > 0:
                output_head(h)
            for gi in (h, 4 + h, 8 + h):
                proj_group(gi)
            if h >= 1:
                preproc(h - 1)
        if t > 0:
            store_mo(t - 1)
        preproc(3)
        if gather and t > 0 and (t - 1) % 2 == 1:
            gather_slab((t - 1) // 2)
        chunks()
    for h in range(4):
        output_head(h)
    store_mo(ntile - 1)
    if gather:
        gather_slab((ntile - 1) // 2)
    w_steps(len(witems))
    P.flush()
    A.close()


def phase_r2(P, nc, io, ntile=SEQ // 512, wspecs=None, gather=False):
    A = Alloc(nc)
    UW = 256
    NCU = UW // 64
    nunit = ntile * 2
    P0 = P
    DS = DECAY_SCALE
    wR = A.sb("r_wR", [128, 16, RW_COLS], BF16)
    wstg = A.sb("r_wstg", [128, 2, 2, 512], F32)
    wob = A.sb("r_wob", [128, 2, 2, 512], BF16)
    wst = [wstg[:].rearrange("p a b c -> p (a b c)")[:, 0:RW_COLS]]
    uT2 = [A.sb("r_uT", [128, 16, 512], BF16)] * 2
    vec = A.sb("r_vec", [128, 64], F32)
    cst = A.sb("r_cst", [64, 1024], F32)
    wupb = A.sb("r_wupb", [64, 256], BF16)
    aupb = A.sb("r_aupb", [64, 256], BF16)
    gupb0 = A.sb("r_gupb0", [128, 256], BF16)
    gupb1 = A.sb("r_gupb1", [32, 256], BF16)
    Ib = A.sb("r_Ib", [64, 64], BF16)
    mask4 = A.sb("r_mask4", [64, 2, 2, 64], F32)
    zf = [A.sb(f"r_zf{i}", [128, 516], F32) for i in range(2)]
    tmpd = [A.sb(f"r_tmpd{i}", [128, 512], F32) for i in range(2)]
    cr = A.sb("r_cr", [128, 16], F32)
    zs_rkv2 = [A.sb("r_zs%d" % i, [64, 12, UW], F32) for i in range(2)]
    zs_w2 = [A.sb("r_zsw%d" % i, [64, UW], F32) for i in range(2)]
    zs_a2 = [A.sb("r_zsa%d" % i, [64, UW], F32) for i in range(2)]
    zs_g02 = [A.sb("r_zsg0%d" % i, [128, UW], F32) for i in range(2)]
    zs_g12 = [A.sb("r_zsg1%d" % i, [32, UW], F32) for i in range(2)]
    lob_w2 = [A.sb("r_lobw%d" % i, [64, UW], BF16) for i in range(2)]
    lob_a2 = [A.sb("r_loba%d" % i, [64, UW], BF16) for i in range(2)]
    gsb02 = [A.sb("r_gsb0%d" % i, [128, UW], BF16) for i in range(2)]
    gsb12 = [A.sb("r_gsb1%d" % i, [32, UW], BF16) for i in range(2)]
    tn = ["cs", "ei", "ev", "ex", "er", "sq", "n", "kk", "km", "bv", "rk"]
    T0 = {n: A.sb(f"r_t_{n}", [64, UW], F32) for n in tn}
    To = {n: A.sb(f"r_o_{n}", [64, UW], F32) for n in ("sq", "n", "yc")}
    RA2 = [A.sb("r_RA%d" % i, [64, 4, NCU, 2, 64], BF16) for i in range(2)]
    KB2 = [A.sb("r_KB%d" % i, [64, 4, NCU, 2, 64], BF16) for i in range(2)]
    TR2 = [A.sb("r_TR%d" % i, [64, 4, NCU, 4, 64], BF16) for i in range(2)]
    PC2 = [A.sb("r_PC%d" % i, [64, 4, NCU], F32) for i in range(2)]
    bon2 = [A.sb("r_bon%d" % i, [64, 4, UW], F32) for i in range(2)]
    gT2 = [A.sb("r_gT%d" % i, [64, 4, UW], F32) for i in range(2)]
    yT2 = [A.sb("r_yT%d" % i, [64, 4, UW], F32) for i in range(2)]
    mo2 = [A.sb("r_mo%d" % i, [64, 4, UW], BF16) for i in range(2)]
    TK = A.sb("r_TK", [64, 4, 4, 64], BF16)
    MS = A.sb("r_MS", [64, 4, 2, 2, 64], BF16)
    Lb = [A.sb(f"r_L{i}", [64, 4, 64], BF16) for i in range(2)]
    Nb = [A.sb(f"r_N{i}", [64, 4, 64], BF16) for i in range(2)]
    Pb = [A.sb(f"r_P{i}", [64, 4, 64], BF16) for i in range(2)]
    ILb = A.sb("r_IL", [64, 4, 64], BF16)
    WT = A.sb("r_WT", [64, 4, 64], BF16)
    Zs = A.sb("r_Zs", [64, 4, 64], BF16)
    Ut = A.sb("r_Ut", [64, 4, 64], F32)
    Ub = A.sb("r_Ub", [64, 4, 64], BF16)
    Hf = A.sb("r_Hf", [64, 4, 64], F32)
    Hd = A.sb("r_Hd", [64, 4, 64], F32)
    Hb = A.sb("r_Hb", [64, 4, 64], BF16)
    pproj = [A.ps(f"r_pp{i}", [128, 512], F32) for i in range(2)]
    ptr1 = A.ps("r_ptr", [64, 2, 2, 4, 64], BF16)
    ptrs = [ptr1[:, 0], ptr1[:, 1]]
    pgen = [A.ps(f"r_pg{i}", [64, 512], F32) for i in range(5)]
    paux = pgen[0]
    banks = [(pgen[i], ("pg", i)) for i in range(1, 5)]
    bctr = {"b": 0}

    def bank():
        b_ = banks[bctr["b"] % len(banks)]
        bctr["b"] += 1
        return b_

    P.dma("sp", vec[:], io["rw_vec"], "c0", writes=["vec"])
    P.dma("sp", cst[:], io["rw_const"], "c1", writes=["cst"])
    P.dma("sp", tmpd[0][:], io["rw_mats"][:, 0:512], "c2", writes=[("tmpd", 0)])
    P.dma("sp", tmpd[1][:], io["rw_mats"][:, 512:1024], "c3", writes=[("tmpd", 1)])
    P.op("dve", lambda e: e.tensor_copy(out=wupb[:], in_=tmpd[0][0:64, 0:256]), reads=[("tmpd", 0)], writes=["wupb"])
    P.op("dve", lambda e: e.tensor_copy(out=aupb[:], in_=tmpd[0][0:64, 256:512]), reads=[("tmpd", 0)], writes=["aupb"])
    P.op("dve", lambda e: e.tensor_copy(out=gupb0[:], in_=tmpd[1][:, 0:256]), reads=[("tmpd", 1)], writes=["gupb0"])
    P.op("dve", lambda e: e.tensor_copy(out=gupb1[:], in_=tmpd[1][0:32, 256:512]), reads=[("tmpd", 1)], writes=["gupb1"])
    P.op("dve", lambda e: e.tensor_copy(out=Ib[:], in_=cst[:, C_ID:C_ID + 64]), reads=["cst"], writes=["Ib"])
    for a in range(2):
        P.op("dve", lambda e, a=a: e.tensor_copy(out=mask4[:, a, 0, :], in_=cst[:, C_SU:C_SU + 64]), reads=["cst"], writes=["mask4"])
        P.op("dve", lambda e, a=a: e.tensor_copy(out=mask4[:, a, 1, :], in_=cst[:, C_IU:C_IU + 64]), reads=["cst"], writes=["mask4"])
    P.op("dve", lambda e: e.tensor_scalar(out=vec[:, V_OMKA:V_OMKA + 4], in0=vec[:, V_KA:V_KA + 4], scalar1=-1.0, scalar2=1.0, op0=ALU.mult, op1=ALU.add),
         reads=["vec"], writes=["vec"])
    for kc in range(16):
        s = 0
        P.dma("sp", wst[s], io["w_in_r"][kc * 128:(kc + 1) * 128, :], ("rwl", s), writes=[("wstg", 0), ("wstg", 1)])
        P.op("pool" if kc % 2 else "dve", lambda e, s=s, kc=kc: e.tensor_copy(out=wR[:, kc, :], in_=wst[s]),
             reads=[("wstg", 0), ("wstg", 1)], writes=["wR"])
    P.op("pool", lambda e: e.memset(cr[:], 0.0), writes=["cr"])
    P.op("pool", lambda e: e.memset(Hf[:], 0.0), writes=[("Hf", 0), ("Hf", 1)])
    P.op("pool", lambda e: e.memset(Hb[:], 0.0), writes=[("Hb", 0), ("Hb", 1)])

    def mk_groups(zs_rkv, zs_w, zs_a, zs_g0, zs_g1):
        groups = []
        for qi, mv in enumerate((V_MIX_R, V_MIX_K, V_MIX_V)):
            for h in range(4):
                groups.append((qi * 256 + h * 64, 64, zs_rkv[:, qi * 4 + h, :], mv + h, ("zs", qi, h)))
        groups.append((768, 64, zs_w[:], V_MIX_W, "zsw"))
        groups.append((832, 64, zs_a[:], V_MIX_A, "zsa"))
        groups.append((896, 128, zs_g0[:], V_MIX_G0, "zsg0"))
        groups.append((1024, 32, zs_g1[:], V_MIX_G1, "zsg1"))
        return groups

    ctr = {"pp": 0, "z": 0}

    def vcol(c, m=64):
        return vec[0:m, c:c + 1]

    PZ_NAMES = {"zs", "zsw", "zsa", "zsg0", "zsg1", "lobw", "loba", "gsb0", "gsb1", "RA", "KB", "TR", "PC", "bon", "gT", "yT", "mo"}

    class PX:
        def __init__(self, pz):
            self.pz = pz
            self.buf = None

        def _k(self, k):
            n = k[0] if isinstance(k, tuple) else k
            return (k, "pz", self.pz) if n in PZ_NAMES else k

        def op(self, eng, fn, reads=(), writes=(), **kw):
            rd, wr = [self._k(k) for k in reads], [self._k(k) for k in writes]
            if self.buf is not None:
                self.buf.append(lambda: P0.op(eng, fn, reads=rd, writes=wr, **kw))
                return None
            return P0.op(eng, fn, reads=rd, writes=wr, **kw)

        def dma(self, q, out, in_, slot, reads=(), writes=(), **kw):
            rd, wr = [self._k(k) for k in reads], [self._k(k) for k in writes]
            if self.buf is not None:
                self.buf.append(lambda: P0.dma(q, out, in_, slot, reads=rd, writes=wr, **kw))
                return None
            return P0.dma(q, out, in_, slot, reads=rd, writes=wr, **kw)

    def make_tile_fns(pz):
        P = PX(pz)
        zs_rkv, zs_w, zs_a, zs_g0, zs_g1 = zs_rkv2[pz], zs_w2[pz], zs_a2[pz], zs_g02[pz], zs_g12[pz]
        lob_w, lob_a, gsb0, gsb1 = lob_w2[pz], lob_a2[pz], gsb02[pz], gsb12[pz]
        RA, KB, TR, PC, bon, gT, yT, mo = RA2[pz], KB2[pz], TR2[pz], PC2[pz], bon2[pz], gT2[pz], yT2[pz], mo2[pz]
        T = dict(T0)
        T["sw"] = zs_w
        T["as"] = zs_a
        groups = mk_groups(zs_rkv, zs_w, zs_a, zs_g0, zs_g1)
        uts = {"ap": None}
        def proj_group(gi):
            (c0, M, dst, mcol, zkey) = groups[gi]
            pp = pproj[ctr["pp"] % 2]
            ppk = ("pp", ctr["pp"] % 2)
            ctr["pp"] += 1
            for kc in range(16):
                P.op("pe", lambda e, pp=pp, M=M, c0=c0, kc=kc, ua=uts["ap"]: e.matmul(pp[0:M, 0:UW], lhsT=wR[:, kc, c0:c0 + M], rhs=ua[:, kc, :],
                                                                     start=(kc == 0), stop=(kc == 15)),
                     reads=["wR", uts["key"]], writes=[ppk], inc=(kc == 15))
            zi = ctr["z"] % 2
            ctr["z"] += 1
            z = zf[zi]
            zk = ("zf", zi)
            td = tmpd[zi]
            tdk = ("tmpd", zi)
            P.op("act", lambda e, z=z, pp=pp, M=M: e.activation(out=z[0:M, 1:UW + 1], in_=pp[0:M, 0:UW], func=AF.Copy), reads=[ppk], writes=[zk])
            P.op("pool", lambda e, z=z, M=M, gi=gi: e.tensor_copy(out=z[0:M, 0:1], in_=cr[0:M, gi:gi + 1]), reads=["cr", zk], writes=[zk])
            P.op("dve", lambda e, z=z, td=td, pp=pp, M=M: e.tensor_tensor(out=td[0:M, 0:UW], in0=z[0:M, 0:UW], in1=pp[0:M, 0:UW], op=ALU.subtract),
                 reads=[zk, ppk], writes=[tdk])
            P.op("dve", lambda e, td=td, pp=pp, M=M, dst=dst, mcol=mcol: e.scalar_tensor_tensor(
                out=dst, in0=td[0:M, 0:UW], scalar=vec[0:M, mcol:mcol + 1], in1=pp[0:M, 0:UW], op0=ALU.mult, op1=ALU.add),
                reads=[tdk, ppk, "vec"], writes=[zkey])
            P.op("pool", lambda e, z=z, M=M, gi=gi: e.tensor_copy(out=cr[0:M, gi:gi + 1], in_=z[0:M, UW:UW + 1]), reads=[zk], writes=["cr"])
        def lora_acts():
            P.op("act", lambda e: e.activation(out=lob_w[:], in_=zs_w[:], func=AF.Tanh), reads=["zsw"], writes=["lobw"])
            P.op("pool", lambda e: e.tensor_copy(out=lob_a[:], in_=zs_a[:]), reads=["zsa"], writes=["loba"])
            P.op("act", lambda e: e.activation(out=gsb0[:], in_=zs_g0[:], func=AF.Sigmoid), reads=["zsg0"], writes=["gsb0"])
            P.op("act", lambda e: e.activation(out=gsb1[:], in_=zs_g1[:], func=AF.Sigmoid), reads=["zsg1"], writes=["gsb1"])
        def preproc(h):
            r_ = zs_rkv[:, h, :]
            k_ = zs_rkv[:, 4 + h, :]
            v_ = zs_rkv[:, 8 + h, :]
            rk_, kk_, vk_ = ("zs", 0, h), ("zs", 1, h), ("zs", 2, h)
            hs = slice(h * 64, (h + 1) * 64)
            P.op("pe", lambda e, hs=hs: e.matmul(paux[:, 0:UW], lhsT=wupb[:, hs], rhs=lob_w[:], start=True, stop=True),
                 reads=["wupb", "lobw"], writes=[("pg", 0)])
            P.op("act", lambda e, h=h: e.activation(out=T["sw"][:], in_=paux[:, 0:UW], func=AF.Sigmoid, bias=vcol(V_W0 + h)),
                 reads=[("pg", 0), "vec"], writes=["zsw"])
            P.op("dve", lambda e: e.tensor_tensor_scan(out=T["cs"][:], data0=cst[:, C_RST:C_RST + UW], data1=T["sw"][:], initial=0.0,
                                                       op0=ALU.mult, op1=ALU.add), reads=["zsw", "cst"], writes=["t_cs"])
            P.op("act", lambda e: e.activation(out=T["ei"][:], in_=T["cs"][:], func=AF.Exp, scale=-DS), reads=["t_cs"], writes=["t_ei"])
            P.op("act", lambda e: e.activation(out=T["ev"][:], in_=T["cs"][:], func=AF.Exp, scale=DS), reads=["t_cs"], writes=["t_ev"])
            P.op("pool", lambda e: e.tensor_tensor(out=T["ex"][:], in0=T["cs"][:], in1=T["sw"][:], op=ALU.subtract),
                 reads=["t_cs", "zsw"], writes=["t_ex"])
            P.op("act", lambda e: e.activation(out=T["ex"][:], in_=T["ex"][:], func=AF.Exp, scale=-DS), reads=["t_ex"], writes=["t_ex"])
            P.op("pool", lambda e, h=h: e.tensor_copy(out=PC[:, h, :], in_=T["ei"][:, 63:UW:64]), reads=["t_ei"], writes=["PC"])
            P.op("dve", lambda e: e.tensor_tensor(out=T["er"][:].rearrange("j (c s) -> j c s", s=64),
                                                  in0=T["ev"][:].rearrange("j (c s) -> j c s", s=64),
                                                  in1=T["ei"][:].rearrange("j (c s) -> j c s", s=64)[:, :, 63:64].to_broadcast([64, NCU, 64]),
                                                  op=ALU.mult), reads=["t_ev", "t_ei"], writes=["t_er"])
            P.op("pe", lambda e, hs=hs: e.matmul(paux[:, 0:UW], lhsT=aupb[:, hs], rhs=lob_a[:], start=True, stop=True),
                 reads=["aupb", "loba"], writes=[("pg", 0)])
            P.op("act", lambda e, h=h: e.activation(out=T["as"][:], in_=paux[:, 0:UW], func=AF.Sigmoid, bias=vcol(V_A0 + h)),
                 reads=[("pg", 0), "vec"], writes=["zsa"])
            P.op("act", lambda e, k_=k_, h=h: e.activation(out=T["sq"][:], in_=k_, func=AF.Square, scale=vcol(V_KK + h)),
                 reads=[kk_, "vec"], writes=["t_sq"])
            P.op("pe", lambda e: e.matmul(paux[:, 0:UW], lhsT=cst[:, C_ONE:C_ONE + 64], rhs=T["sq"][:], start=True, stop=True),
                 reads=["cst", "t_sq"], writes=[("pg", 0)])
            P.op("act", lambda e: e.activation(out=T["n"][:], in_=paux[:, 0:UW], func=AF.Sqrt), reads=[("pg", 0)], writes=["t_n"])
            P.op("dve", lambda e: e.tensor_scalar(out=T["n"][:], in0=T["n"][:], scalar1=1e-12, scalar2=None, op0=ALU.max), reads=["t_n"], writes=["t_n"])
            P.op("dve", lambda e: e.reciprocal(out=T["n"][:], in_=T["n"][:]), reads=["t_n"], writes=["t_n"])
            P.op("dve", lambda e, k_=k_, h=h: e.scalar_tensor_tensor(out=T["kk"][:], in0=k_, scalar=vcol(V_KK + h), in1=T["n"][:],
                                                                   op0=ALU.mult, op1=ALU.mult), reads=[kk_, "vec", "t_n"], writes=["t_kk"])
            P.op("pool", lambda e, h=h: e.tensor_scalar(out=T["km"][:], in0=T["as"][:], scalar1=vcol(V_KA + h), scalar2=vcol(V_OMKA + h),
                                                     op0=ALU.mult, op1=ALU.add), reads=["zsa", "vec"], writes=["t_km"])
            P.op("pool", lambda e, k_=k_: e.tensor_tensor(out=T["km"][:], in0=T["km"][:], in1=k_, op=ALU.mult), reads=["t_km", kk_], writes=["t_km"])
            P.op("pool", lambda e: e.tensor_tensor(out=T["bv"][:], in0=T["kk"][:], in1=T["as"][:], op=ALU.mult), reads=["t_kk", "zsa"], writes=["t_bv"])

            def c3(ap):
                return ap.rearrange("j (c s) -> j c s", s=64)
            P.op("dve", lambda e, h=h: e.scalar_tensor_tensor(out=RA[:, h, :, 0, :], in0=c3(T["kk"][:]), scalar=-1.0, in1=c3(T["ex"][:]),
                                                            op0=ALU.mult, op1=ALU.mult), reads=["t_kk", "t_ex"], writes=[("RA", h)])
            P.op("pool", lambda e, h=h, r_=r_: e.tensor_tensor(out=RA[:, h, :, 1, :], in0=c3(r_), in1=c3(T["ei"][:]), op=ALU.mult),
                 reads=[rk_, "t_ei"], writes=[("RA", h)])
            P.op("dve", lambda e, h=h: e.tensor_tensor(out=KB[:, h, :, 0, :], in0=c3(T["km"][:]), in1=c3(T["ev"][:]), op=ALU.mult),
                 reads=["t_km", "t_ev"], writes=[("KB", h)])
            P.op("pool", lambda e, h=h: e.tensor_tensor(out=KB[:, h, :, 1, :], in0=c3(T["bv"][:]), in1=c3(T["ev"][:]), op=ALU.mult),
                 reads=["t_bv", "t_ev"], writes=[("KB", h)])
            P.op("dve", lambda e, h=h: e.tensor_tensor(out=TR[:, h, :, 0, :], in0=c3(T["km"][:]), in1=c3(T["er"][:]), op=ALU.mult),
                 reads=["t_km", "t_er"], writes=[("TR", h)])
            P.op("pool", lambda e, h=h: e.tensor_tensor(out=TR[:, h, :, 1, :], in0=c3(T["bv"][:]), in1=c3(T["er"][:]), op=ALU.mult),
                 reads=["t_bv", "t_er"], writes=[("TR", h)])
            P.op("act", lambda e, h=h, v_=v_: e.activation(out=TR[:, h, :, 2, :], in_=c3(v_), func=AF.Copy), reads=[vk_], writes=[("TR", h)])
            P.op("pool", lambda e, h=h: e.tensor_copy(out=TR[:, h, :, 3, :], in_=RA[:, h, :, 0, :]), reads=[("RA", h)], writes=[("TR", h)])
            P.op("dve", lambda e, h=h, r_=r_: e.scalar_tensor_tensor(out=T["rk"][:], in0=r_, scalar=vcol(V_RK + h), in1=T["km"][:],
                                                                   op0=ALU.mult, op1=ALU.mult), reads=[rk_, "vec", "t_km"], writes=["t_rk"])
            P.op("pe", lambda e: e.matmul(paux[:, 0:UW], lhsT=cst[:, C_ONE:C_ONE + 64], rhs=T["rk"][:], start=True, stop=True),
                 reads=["cst", "t_rk"], writes=[("pg", 0)])
            P.op("dve", lambda e, h=h, v_=v_: e.tensor_tensor(out=bon[:, h, :], in0=paux[:, 0:UW], in1=v_, op=ALU.mult), reads=[("pg", 0), vk_], writes=[("bon", h)])
            P.op("pe", lambda e, hs=hs: e.matmul(paux[:, 0:UW], lhsT=gupb0[:, hs], rhs=gsb0[:], start=True, stop=False),
                 reads=["gupb0", "gsb0"], writes=[("pg", 0)], inc=False)
            P.op("pe", lambda e, hs=hs: e.matmul(paux[:, 0:UW], lhsT=gupb1[:, hs], rhs=gsb1[:], start=False, stop=True),
                 reads=["gupb1", "gsb1"], writes=[("pg", 0)])
            P.op("act", lambda e, h=h: e.activation(out=gT[:, h, :], in_=paux[:, 0:UW], func=AF.Copy), reads=[("pg", 0)], writes=[("gT", h)])

        def chunks():
            def chunk_group(c, g):
                hh = (2 * g, 2 * g + 1)
                hs2 = slice(2 * g, 2 * g + 2)
                K_ = lambda n, *a: (n, g) + a
                pt = ptrs[g]
                ptk = ("ptr", 0)
                for hi, h in enumerate(hh):
                    for q in range(4):
                        P.op("pe", lambda e, hi=hi, h=h, q=q: e.transpose(out=pt[:, hi, q, :], in_=TR[:, h, c, q, :], identity=Ib[:]),
                             reads=[("TR", h), "Ib"], writes=[ptk], inc=(hi == 1 and q == 3))
                P.op("act", lambda e: e.activation(out=TK[:, hs2], in_=pt[:], func=AF.Copy), reads=[ptk], writes=[K_("TK")])
                yield
                b1, b1k = bank()
                v1 = b1[:, 0:512].rearrange("s (h a x) -> s h a x", h=2, a=2)
                for hi, h in enumerate(hh):
                    rhs = RA[:, h, c, :, :].rearrange("j a t -> j (a t)")
                    P.op("pe", lambda e, hi=hi, h=h, rhs=rhs: e.matmul(v1[:, hi, 0, :], lhsT=KB[:, h, c, 1, :], rhs=rhs, start=True, stop=True),
                         reads=[("KB", h), ("RA", h)], writes=[b1k], inc=False)
                    P.op("pe", lambda e, hi=hi, h=h, rhs=rhs: e.matmul(v1[:, hi, 1, :], lhsT=KB[:, h, c, 0, :], rhs=rhs, start=True, stop=True),
                         reads=[("KB", h), ("RA", h)], writes=[b1k], inc=(hi == 1))
                b2, b2k = bank()
                v2 = b2[:, 0:128].rearrange("s (h t) -> s h t", h=2)
                for hi, h in enumerate(hh):
                    P.op("pe", lambda e, hi=hi, h=h: e.matmul(v2[:, hi, :], lhsT=RA[:, h, c, 0, :], rhs=KB[:, h, c, 1, :], start=True, stop=True),
                         reads=[("KB", h), ("RA", h)], writes=[b2k], inc=(hi == 1))
                P.op("dve", lambda e: e.tensor_tensor(
                    out=MS[:, hs2].rearrange("s h a b t -> s h (a b t)"), in0=v1.rearrange("s h a x -> s h (a x)"),
                    in1=mask4[:].rearrange("s a b t -> s (a b t)").unsqueeze(1).to_broadcast([64, 2, 256]), op=ALU.mult),
                    reads=[b1k, "mask4"], writes=[K_("MS")])
                P.op("dve", lambda e: e.tensor_tensor(out=Lb[0][:, hs2], in0=v2, in1=cst[:, C_SL:C_SL + 64].unsqueeze(1).to_broadcast([64, 2, 64]),
                                                      op=ALU.mult), reads=[b2k, "cst"], writes=[K_("L", 0)])
                P.op("pool", lambda e: e.tensor_tensor(out=Pb[0][:, hs2], in0=MS[:, hs2, 0, 0, :], in1=cst[:, C_ID:C_ID + 64].unsqueeze(1).to_broadcast([64, 2, 64]),
                                                       op=ALU.add), reads=[K_("MS"), "cst"], writes=[K_("P", 0)])
                yield
                for k in range(5):
                    li, lo = k % 2, (k + 1) % 2
                    Nk = (lambda h: MS[:, h, 0, 0, :]) if k == 0 else (lambda h, li=li: Nb[li][:, h, :])
                    nkey = K_("MS") if k == 0 else K_("N", li)
                    bl, blk = bank()
                    vl = bl[:, 0:128].rearrange("s (h t) -> s h t", h=2)
                    for hi, h in enumerate(hh):
                        P.op("pe", lambda e, hi=hi, h=h, Nk=Nk, li=li, vl=vl: e.matmul(vl[:, hi, :], lhsT=Nk(h), rhs=Lb[li][:, h, :], start=True, stop=True),
                             reads=[nkey, K_("L", li)], writes=[blk], inc=(hi == 1))
                    if k < 4:
                        bn, bnk = bank()
                        vn = bn[:, 0:128].rearrange("s (h t) -> s h t", h=2)
                        for hi, h in enumerate(hh):
                            P.op("pe", lambda e, hi=hi, h=h, Nk=Nk, li=li, vn=vn: e.matmul(vn[:, hi, :], lhsT=Lb[li][:, h, :], rhs=Nk(h), start=True, stop=True),
                                 reads=[nkey, K_("L", li)], writes=[bnk], inc=(hi == 1))
                    P.op("act", lambda e, lo=lo, vl=vl: e.activation(out=Lb[lo][:, hs2], in_=vl, func=AF.Copy), reads=[blk], writes=[K_("L", lo)])
                    if k < 4:
                        P.op("dve", lambda e, lo=lo, vn=vn: e.tensor_copy(out=Nb[lo][:, hs2], in_=vn), reads=[bnk], writes=[K_("N", lo)])
                    yield
                    bp, bpk = bank()
                    vp = bp[:, 0:128].rearrange("s (h t) -> s h t", h=2)
                    for hi, h in enumerate(hh):
                        P.op("pe", lambda e, hi=hi, h=h, li=li, lo=lo, vp=vp: e.matmul(vp[:, hi, :], lhsT=Lb[lo][:, h, :], rhs=Pb[li][:, h, :], start=True, stop=False),
                             reads=[K_("L", lo), K_("P", li)], writes=[bpk], inc=False)
                        P.op("pe", lambda e, hi=hi, h=h, li=li, vp=vp: e.matmul(vp[:, hi, :], lhsT=Ib[:], rhs=Pb[li][:, h, :], start=False, stop=True),
                             reads=["Ib", K_("P", li)], writes=[bpk], inc=(hi == 1))
                    if k % 2 == 0:
                        P.op("dve", lambda e, lo=lo, vp=vp: e.tensor_copy(out=Pb[lo][:, hs2], in_=vp), reads=[bpk], writes=[K_("P", lo)])
                    else:
                        P.op("act", lambda e, lo=lo, vp=vp: e.activation(out=Pb[lo][:, hs2], in_=vp, func=AF.Copy), reads=[bpk], writes=[K_("P", lo)])
                    yield
                TT = Pb[1]
                ttk = K_("P", 1)
                bw, bwk = bank()
                vw = bw[:, 0:128].rearrange("s (h t) -> s h t", h=2)
                for hi, h in enumerate(hh):
                    P.op("pe", lambda e, hi=hi, h=h: e.matmul(vw[:, hi, :], lhsT=TK[:, h, 3, :], rhs=TT[:, h, :], start=True, stop=True),
                         reads=[K_("TK"), ttk], writes=[bwk], inc=(hi == 1))
                bz, bzk = bank()
                vz = bz[:, 0:128].rearrange("s (h t) -> s h t", h=2)
                for hi, h in enumerate(hh):
                    P.op("pe", lambda e, hi=hi, h=h: e.matmul(vz[:, hi, :], lhsT=MS[:, h, 1, 0, :], rhs=TK[:, h, 2, :], start=True, stop=True),
                         reads=[K_("MS"), K_("TK")], writes=[bzk], inc=(hi == 1))
                P.op("act", lambda e: e.activation(out=WT[:, hs2], in_=vw, func=AF.Copy), reads=[bwk], writes=[K_("WT")])
                P.op("dve", lambda e: e.tensor_copy(out=Zs[:, hs2], in_=vz), reads=[bzk], writes=[K_("Zs")])
                yield
                bu, buk = bank()
                vu = bu[:, 0:128].rearrange("s (h t) -> s h t", h=2)
                for hi, h in enumerate(hh):
                    P.op("pe", lambda e, hi=hi, h=h: e.matmul(vu[:, hi, :], lhsT=TT[:, h, :], rhs=Zs[:, h, :], start=True, stop=True),
                         reads=[ttk, K_("Zs")], writes=[buk], inc=(hi == 1))
                P.op("act", lambda e: e.activation(out=Ut[:, hs2], in_=vu, func=AF.Copy), reads=[buk], writes=[K_("Ut")])
                P.op("pool", lambda e: e.tensor_tensor(out=Hd[:, hs2], in0=Hf[:, hs2], in1=PC[:, hs2, c:c + 1].to_broadcast([64, 2, 64]), op=ALU.mult),
                     reads=[K_("Hf"), "PC"], writes=[K_("Hd")])
                yield
                b5, b5k = bank()
                v5 = b5[:, 0:128].rearrange("s (h t) -> s h t", h=2)
                for hi, h in enumerate(hh):
                    P.op("pe", lambda e, hi=hi, h=h: e.matmul(v5[:, hi, :], lhsT=WT[:, h, :], rhs=Hb[:, h, :], start=True, stop=True),
                         reads=[K_("WT"), K_("Hb")], writes=[b5k], inc=(hi == 1))
                P.op("dve", lambda e: e.tensor_tensor(out=Ub[:, hs2], in0=v5, in1=Ut[:, hs2], op=ALU.add), reads=[b5k, K_("Ut")], writes=[K_("Ub")])
                yield
                by, byk = bank()
                vy = by[:, 0:128].rearrange("s (h t) -> s h t", h=2)
                for hi, h in enumerate(hh):
                    P.op("pe", lambda e, hi=hi, h=h: e.matmul(vy[:, hi, :], lhsT=Hb[:, h, :], rhs=RA[:, h, c, 1, :], start=True, stop=False),
                         reads=[K_("Hb"), ("RA", h)], writes=[byk], inc=False)
                    P.op("pe", lambda e, hi=hi, h=h: e.matmul(vy[:, hi, :], lhsT=Ub[:, h, :], rhs=MS[:, h, 0, 1, :], start=False, stop=False),
                         reads=[K_("Ub"), K_("MS")], writes=[byk], inc=False)
                    P.op("pe", lambda e, hi=hi, h=h: e.matmul(vy[:, hi, :], lhsT=TK[:, h, 2, :], rhs=MS[:, h, 1, 1, :], start=False, stop=True),
                         reads=[K_("TK"), K_("MS")], writes=[byk], inc=(hi == 1))
                bh, bhk = bank()
                vh = bh[:, 0:128].rearrange("s (h t) -> s h t", h=2)
                for hi, h in enumerate(hh):
                    P.op("pe", lambda e, hi=hi, h=h: e.matmul(vh[:, hi, :], lhsT=TK[:, h, 1, :], rhs=Ub[:, h, :], start=True, stop=False),
                         reads=[K_("TK"), K_("Ub")], writes=[bhk], inc=False)
                    P.op("pe", lambda e, hi=hi, h=h: e.matmul(vh[:, hi, :], lhsT=TK[:, h, 0, :], rhs=TK[:, h, 2, :], start=False, stop=True),
                         reads=[K_("TK")], writes=[bhk], inc=(hi == 1))
                P.op("dve", lambda e: e.tensor_tensor(out=Hb[:, hs2], in0=vh, in1=Hd[:, hs2], op=ALU.add), reads=[bhk, K_("Hd")], writes=[K_("Hb")])
                P.op("dve", lambda e: e.tensor_tensor(out=Hf[:, hs2], in0=vh, in1=Hd[:, hs2], op=ALU.add), reads=[bhk, K_("Hd")], writes=[K_("Hf")])
                P.op("act", lambda e: e.activation(out=yT[:, hs2, c * 64:(c + 1) * 64], in_=vy, func=AF.Copy), reads=[byk], writes=[K_("yT")])
                yield

            return chunk_group

        def output_head(h):
            y_ = yT[:, h, :]
            P.op("pe", lambda e, y_=y_: e.matmul(paux[:, 0:UW], lhsT=cst[:, C_AVG:C_AVG + 64], rhs=y_, start=True, stop=True),
                 reads=["cst", ("yT", 0), ("yT", 1)], writes=[("pg", 0)])
            P.op("dve", lambda e, y_=y_: e.tensor_tensor(out=To["yc"][:], in0=y_, in1=paux[:, 0:UW], op=ALU.subtract), reads=[("yT", 0), ("yT", 1), ("pg", 0)], writes=["o_yc"])
            P.op("act", lambda e: e.activation(out=To["sq"][:], in_=To["yc"][:], func=AF.Square), reads=["o_yc"], writes=["o_sq"])
            P.op("pe", lambda e: e.matmul(paux[:, 0:UW], lhsT=cst[:, C_AVG:C_AVG + 64], rhs=To["sq"][:], start=True, stop=True),
                 reads=["cst", "o_sq"], writes=[("pg", 0)])
            P.op("act", lambda e: e.activation(out=To["n"][:], in_=paux[:, 0:UW], func=AF.Sqrt, bias=GN_EPS), reads=[("pg", 0)], writes=["o_n"])
            P.op("dve", lambda e: e.reciprocal(out=To["n"][:], in_=To["n"][:]), reads=["o_n"], writes=["o_n"])
            P.op("pool", lambda e: e.tensor_tensor(out=To["yc"][:], in0=To["yc"][:], in1=To["n"][:], op=ALU.mult), reads=["o_yc", "o_n"], writes=["o_yc"])
            P.op("pool", lambda e, h=h: e.tensor_scalar(out=To["yc"][:], in0=To["yc"][:], scalar1=vcol(V_LNW + h), scalar2=vcol(V_LNB + h),
                                                     op0=ALU.mult, op1=ALU.add), reads=["o_yc", "vec"], writes=["o_yc"])
            P.op("pool", lambda e, h=h: e.tensor_tensor(out=To["yc"][:], in0=To["yc"][:], in1=bon[:, h, :], op=ALU.add), reads=["o_yc", ("bon", h)], writes=["o_yc"])
            P.op("dve", lambda e, h=h: e.tensor_tensor(out=mo[:, h, :], in0=To["yc"][:], in1=gT[:, h, :], op=ALU.mult),
                 reads=["o_yc", ("gT", h)], writes=["mo"])
        def store_mo(u):
            off = (u % 4) * UW
            P.dma("sp", io["mix_loc"][u // 4, 256:512, off:off + UW].rearrange("(h i) t -> i h t", h=4), mo[:], ("rmo", pz),
                  reads=["mo"], writes=[("mixslab", u // 4)])
        return proj_group, lora_acts, preproc, chunks, output_head, store_mo, uts, P

    F = [make_tile_fns(0), make_tile_fns(1)]

    def gather_slab(j):
        P0.custom("pool", lambda e, j=j: e.collective_compute(
            "AllGather", ALU.bypass, replica_groups=[[0, 1, 2, 3], [4, 5, 6, 7]],
            ins=[io["mix_loc"][j].opt()], outs=[io["mix_gath"][j].opt()]),
            key="cc", amt=1, reads=[("mixslab", j)], writes=[])

    witems = []
    for (src, dst, K, N) in (wspecs or []):
        nkc = K // 128
        for kg in range((nkc + 15) // 16):
            nk = min(16, nkc - kg * 16)
            for nb in range(N // 512):
                tix = wtile_index(N, kg, nb)
                for h0 in range(0, nk, 2):
                    r0 = (kg * 16 + h0) * 128
                    witems.append((src[r0:r0 + 256, nb * 512:(nb + 1) * 512].rearrange("(kc p) c -> p kc c", p=128),
                                   dst[tix].rearrange("p (kc c) -> p kc c", c=512)[:, h0:h0 + 2, :]))
    wst_ = {"i": 0}

    def w_load(i):
        if i < len(witems):
            P.dma("sp", wstg[:, i % 2], witems[i][0], ("wl", i % 2), writes=[("wstg", i % 2)])

    def w_steps(n):
        for _ in range(n):
            i = wst_["i"]
            if i >= len(witems):
                return
            if i == 0:
                w_load(0)
            w_load(i + 1)
            sl = i % 2
            P.op("pool", lambda e, sl=sl: e.tensor_copy(out=wob[:, sl], in_=wstg[:, sl]), reads=[("wstg", sl)], writes=[("wob", sl)])
            P.dma("sp", witems[i][1], wob[:, sl], ("ws", sl), reads=[("wob", sl)])
            wst_["i"] += 1
    per_chunk = -(-len(witems) // max(1, (nunit * NCU - NCU))) if witems else 0

    def load_uT(t):
        P0.dma("sp", uT2[0][:], io["uT_scr"][t].rearrange("p (kc t) -> p kc t", t=512), "utl", writes=["uT"])

    def stage1(u):
        proj_group, lora_acts, preproc, chunks, output_head, store_mo, uts, px = F[u % 2]
        t, half = u // 2, u % 2
        uts["ap"] = uT2[0][:, :, half * UW:(half + 1) * UW]
        uts["key"] = "uT"
        for gi in (12, 13, 14, 15):
            proj_group(gi)
            yield
        lora_acts()
        yield
        for h in range(4):
            for gi in (h, 4 + h, 8 + h):
                proj_group(gi)
                yield
            if h >= 1:
                yield from sliced(px, preproc, h - 1)
        yield from sliced(px, preproc, 3)
        if half == 1 and t + 1 < ntile:
            load_uT(t + 1)

    def sliced(px, fn, *a):
        px.buf = []
        fn(*a)
        ops, px.buf = px.buf, None
        for i, th in enumerate(ops):
            th()
            if i % 3 == 2:
                yield
        yield

    def stage3(u):
        proj_group, lora_acts, preproc, chunks, output_head, store_mo, uts, px = F[u % 2]
        for h in range(4):
            output_head(h)
            yield
        store_mo(u)
        if gather and u % 4 == 3:
            gather_slab(u // 4)
        yield

    def drain(g):
        for _ in g:
            pass

    load_uT(0)
    drain(stage1(0))
    for u in range(nunit):
        chunk_group = F[u % 2][3]()
        side = []
        if u + 1 < nunit:
            side.append(stage1(u + 1))
        if u > 0:
            side.append(stage3(u - 1))
        for c in range(NCU):
            w_steps(per_chunk)
            gens = [chunk_group(c, 0), chunk_group(c, 1)] + side
            live = [True] * len(gens)
            nchunk_live = 2
            while nchunk_live > 0:
                for gi_, gn in enumerate(gens):
                    if live[gi_]:
                        try:
                            next(gn)
                        except StopIteration:
                            live[gi_] = False
                            if gi_ < 2:
                                nchunk_live -= 1
            side = [g for gi_, g in enumerate(gens) if gi_ >= 2 and live[gi_]]
        for g in side:
            drain(g)
    drain(stage3(nunit - 1))
    w_steps(len(witems))
    P.flush()
    A.close()


def phase_x(P, nc, io, flush=True):
    groups = [[0, 1, 2, 3], [4, 5, 6, 7]]
    for j in range(8):
        P.custom("pool", lambda e, j=j: e.collective_compute(
            "AllGather", ALU.bypass, replica_groups=groups,
            ins=[io["mix_loc"][j].opt()], outs=[io["mix_gath"][j].opt()]),
            key="cc", amt=1, reads=[], writes=[])
    if flush:
        P.flush()


def build(cfg):
    nc = bass.Bass("TRN2", target_bir_lowering=False)
    phases = cfg.get("phases", "WARXD")
    io = {}

    def ein(name, shape, dt=F32):
        io[name] = nc.dram_tensor(name, list(shape), dt, kind="ExternalInput").ap()

    ein("xb", [SEQ, D_MODEL])
    ein("xs", [2048, D_MODEL])
    ein("norm1_g", [1, D_MODEL])
    ein("norm2_g", [1, D_MODEL])
    ein("final_g", [1, D_MODEL])
    ein("w_in_a", [D_MODEL, 768])
    ein("w_in_r", [D_MODEL, RW_COLS])
    ein("abias", [128, 3 * 4 * 256])
    ein("amask", [128, 3 * 256])
    ein("ident", [128, 128])
    ein("w_out_p", [D_MODEL, D_MODEL])
    ein("w_gu", [D_MODEL, 2 * FFN])
    ein("w_down", [FFN, D_MODEL])
    ein("rw_vec", [128, 64])
    ein("rw_mats", [128, 1024])
    ein("rw_const", [64, 1024])
    io["out"] = nc.dram_tensor("out", [2048, D_MODEL], F32, kind="ExternalOutput").ap()
    if cfg.get("mix_in"):
        ein("mix_in", [8, 512, 1024], BF16)
    io["uT_scr"] = nc.dram_tensor("uT_scr", [SEQ // 512, 128, 16 * 512], BF16).ap()
    io["wo_scr"] = nc.dram_tensor("wo_scr", [4, 128, 16 * 512], BF16).ap()
    io["wgu_scr"] = nc.dram_tensor("wgu_scr", [22, 128, 16 * 512], BF16).ap()
    io["wd_scr"] = nc.dram_tensor("wd_scr", [12, 128, 16 * 512], BF16).ap()
    if cfg.get("dump_mix"):
        io["mix_loc"] = nc.dram_tensor("mix_loc", [8, 512, 1024], BF16, kind="ExternalOutput").ap()
    else:
        io["mix_loc"] = nc.dram_tensor("mix_loc", [8, 512, 1024], BF16).ap()
    io["mix_gath"] = nc.dram_tensor("mix_gath", [8, 2048, 1024], BF16).ap()

    P = Prog(nc, same_eng_sync=cfg.get("same_eng_sync", True))
    if cfg.get("mix_in"):
        rows = cfg["mix_in"]
        P.dma("sp", io["mix_loc"][:, rows[0]:rows[1], :], io["mix_in"][:, rows[0]:rows[1], :], "mixin")
        P.flush()
    wspecs = [(io["w_out_p"], io["wo_scr"], D_MODEL, D_MODEL),
              (io["w_gu"], io["wgu_scr"], D_MODEL, 2 * FFN),
              (io["w_down"], io["wd_scr"], FFN, D_MODEL)]
    if "A" in phases:
        phase_a(P, nc, io, nsc=cfg.get("nsc", SEQ // 2048))
    w_in_r = ("W" in phases and "R" in phases and cfg.get("nrt", SEQ // 512) == SEQ // 512)
    if "R" in phases:
        (phase_r2 if cfg.get("r2", False) else phase_r)(P, nc, io, ntile=cfg.get("nrt", SEQ // 512), wspecs=wspecs if w_in_r else None,
                gather=(w_in_r and "X" in phases))
    if w_in_r:
        pass
    elif "W" in phases and "X" in phases:
        phase_w(P, nc, wspecs, pre=lambda: phase_x(P, nc, io, flush=False), engs=("dve", "act"))
    else:
        if "W" in phases:
            phase_w(P, nc, wspecs)
        if "X" in phases:
            phase_x(P, nc, io)
    if "D" in phases:
        phase_d(P, nc, io, ntile=cfg.get("ndt", 4))
    P.close()
    return nc


def host_inputs(inputs):
    f32 = np.float32
    x = np.asarray(inputs["x"], f32)
    w_in = np.asarray(inputs["w_in"], f32)[0]
    w_out = np.asarray(inputs["w_out"], f32)[0]
    w_gu = np.ascontiguousarray(np.asarray(inputs["w_gate_up"], f32)[0])
    w_down = np.ascontiguousarray(np.asarray(inputs["w_down"], f32)[0])
    table = np.asarray(inputs["rel_bias_table"], f32)
    bidx, amask = _attn_tables()
    ident = np.eye(128, dtype=f32)
    maps = []
    zoff = 3 * 1024
    for core in range(NCORES):
        b, g = core // 4, core % 4
        hs = list(range(4 * g, 4 * g + 4))
        acols = np.concatenate([np.arange(o + 64 * h, o + 64 * h + 64) for o in (0, 1024, 2048) for h in hs])
        rcols = np.concatenate(
            [np.arange(zoff + o + 64 * h, zoff + o + 64 * h + 64) for o in (0, 1024, 2048) for h in hs]
            + [np.arange(zoff + 3072, zoff + 3072 + 64 + 64 + 160)])
        perm = np.concatenate([np.concatenate([np.arange(256 * r, 256 * r + 256), np.arange(1024 + 256 * r, 1024 + 256 * r + 256)])
                               for r in range(4)])
        abias = table[bidx][:, :, :, hs]
        abias = np.ascontiguousarray(np.transpose(abias, (1, 0, 3, 2))).reshape(128, 3 * 4 * 256)
        m = {
            "xb": np.ascontiguousarray(x[b]),
            "xs": np.ascontiguousarray(x[b, 2048 * g:2048 * (g + 1)]),
            "norm1_g": np.asarray(inputs["norm1_g"], f32).reshape(1, D_MODEL),
            "norm2_g": np.asarray(inputs["norm2_g"], f32).reshape(1, D_MODEL),
            "final_g": np.asarray(inputs["final_g"], f32).reshape(1, D_MODEL),
            "w_in_a": np.ascontiguousarray(w_in[:, acols]),
            "w_in_r": np.ascontiguousarray(w_in[:, rcols]),
            "abias": abias.astype(f32),
            "amask": np.ascontiguousarray(np.transpose(amask, (1, 0, 2))).reshape(128, 768),
            "ident": ident,
            "w_out_p": np.ascontiguousarray(w_out[perm]),
            "w_gu": w_gu,
            "w_down": w_down,
        }
        m.update(host_rwkv(inputs, hs))
        maps.append(m)
    return maps


def host_rwkv(inputs, hs):
    f32 = np.float32
    mixv = np.asarray(inputs["rwkv_shift_mix"], f32)[0]
    vec = np.zeros((128, 64), f32)
    mats = np.zeros((128, 1024), f32)
    hc = np.concatenate([np.arange(64 * h, 64 * h + 64) for h in hs])
    for hl, h in enumerate(hs):
        sl = slice(64 * h, 64 * h + 64)
        vec[0:64, V_MIX_R + hl] = mixv[0:1024][sl]
        vec[0:64, V_MIX_K + hl] = mixv[1024:2048][sl]
        vec[0:64, V_MIX_V + hl] = mixv[2048:3072][sl]
        vec[0:64, V_W0 + hl] = np.asarray(inputs["rwkv_w0"], f32)[0][sl]
        vec[0:64, V_A0 + hl] = np.asarray(inputs["rwkv_a0"], f32)[0][sl]
        vec[0:64, V_KK + hl] = np.asarray(inputs["rwkv_k_k"], f32)[0][sl]
        vec[0:64, V_KA + hl] = np.asarray(inputs["rwkv_k_a"], f32)[0][sl]
        vec[0:64, V_RK + hl] = np.asarray(inputs["rwkv_r_k"], f32)[0][h]
        vec[0:64, V_LNW + hl] = np.asarray(inputs["rwkv_ln_w"], f32)[0][sl]
        vec[0:64, V_LNB + hl] = np.asarray(inputs["rwkv_ln_b"], f32)[0][sl]
    vec[0:64, V_MIX_W] = mixv[3072:3136]
    vec[0:64, V_MIX_A] = mixv[3136:3200]
    vec[0:128, V_MIX_G0] = mixv[3200:3328]
    vec[0:32, V_MIX_G1] = mixv[3328:3360]
    mats[0:64, 0:256] = np.asarray(inputs["rwkv_w_up"], f32)[0][:, hc]
    mats[0:64, 256:512] = np.asarray(inputs["rwkv_a_up"], f32)[0][:, hc]
    gup = np.asarray(inputs["rwkv_g_up"], f32)[0]
    mats[0:128, 512:768] = gup[0:128][:, hc]
    mats[0:32, 768:1024] = gup[128:160][:, hc]
    cst = np.zeros((64, 1024), f32)
    a = np.arange(64)
    cst[:, C_SU:C_SU + 64] = (a[:, None] < a[None, :])
    cst[:, C_IU:C_IU + 64] = (a[:, None] <= a[None, :])
    cst[:, C_SL:C_SL + 64] = (a[None, :] < a[:, None])
    cst[:, C_ID:C_ID + 64] = np.eye(64)
    cst[:, C_AVG:C_AVG + 64] = 1.0 / 64
    cst[:, C_ONE:C_ONE + 64] = 1.0
    rst = np.ones(512, f32)
    rst[0::64] = 0.0
    cst[:, C_RST:C_RST + 512] = rst[None, :]
    return {"rw_vec": vec, "rw_mats": mats, "rw_const": cst}


_NC_CACHE = {}


def kernel(**inputs):
    cfg = {"phases": "WARXD"}
    key = "full"
    if key not in _NC_CACHE:
        _NC_CACHE[key] = build(cfg)
    nc = _NC_CACHE[key]
    maps = host_inputs(inputs)
    res = run_bass_kernel_spmd(nc, maps, core_ids=list(range(NCORES)))
    out = np.empty((NBATCH, SEQ, D_MODEL), np.float32)
    for core in range(NCORES):
        b, g = core // 4, core % 4
        out[b, 2048 * g:2048 * (g + 1)] = res.results[core]["out"]
    return out
```

```python
from contextlib import ExitStack
import math
import numpy as np
import ml_dtypes
import concourse.bass as bass
import concourse.mybir as mybir
from concourse.bass_utils import run_bass_kernel_spmd

F32 = mybir.dt.float32
BF16 = mybir.dt.bfloat16
AF = mybir.ActivationFunctionType
ALU = mybir.AluOpType
AX = mybir.AxisListType

ENGS = ("sp", "act", "dve", "pool", "pe")

D_MODEL = 2048
SEQ = 8192
NBATCH = 2
HD = 64
FFN = 5632
NCORES = 8
PATTERNS = ((128, 1), (512, 4), (2048, 16))
RMS_EPS = 1e-6
GN_EPS = 64e-5
DECAY_SCALE = math.exp(-0.5)
NEG = -30000.0
RW_COLS = 1056
CH = 64


class Prog:
    def __init__(self, nc, same_eng_sync=True):
        self.nc = nc
        self.same_eng_sync = same_eng_sync
        self.stack = ExitStack()
        self.sems = {}
        self.cnt = {}
        self.ops = {e: [] for e in ENGS}
        self.waited = {e: {} for e in ENGS}
        self.bufs = {}
        self.pending = {e: ([], []) for e in ENGS}
        self.nops = 0
        self.psum_names = {"pT", "pp", "pS", "pO", "pb", "ptr", "pg"}

    def _excl(self, reads, writes):
        reads = list(reads)
        writes = list(writes)
        for k in reads:
            n = k[0] if isinstance(k, tuple) else k
            if n in self.psum_names and k not in writes:
                writes.append(k)
        return reads, writes

    def sem(self, key):
        if key not in self.sems:
            name = "s_" + "_".join(str(k) for k in (key if isinstance(key, tuple) else (key,)))
            self.sems[key] = self.stack.enter_context(self.nc.semaphore(name))
            self.cnt[key] = 0
        return self.sems[key]

    def _deps(self, reads, writes):
        deps = []
        for b in reads:
            st = self.bufs.get(b)
            if st is not None and st[0] is not None:
                deps.append((st[0], True))
        for b in writes:
            st = self.bufs.get(b)
            if st is not None:
                if st[0] is not None:
                    deps.append((st[0], False))
                deps.extend((t, False) for t in st[1])
        return deps

    def _commit(self, tok, reads, writes):
        for b in reads:
            st = self.bufs.get(b)
            if st is None:
                self.bufs[b] = [None, [tok]]
            else:
                st[1].append(tok)
        for b in writes:
            self.bufs[b] = [tok, []]

    def _filter(self, eng, deps):
        w = self.waited[eng]
        best = {}
        for d in deps:
            if d is None:
                continue
            if len(d) == 2 and isinstance(d[1], bool):
                (key, val), raw = d
            else:
                (key, val), raw = d, True
            if key == eng and (eng == "pe" or not raw or (not self.same_eng_sync and eng != "pool")):
                continue
            if w.get(key, 0) >= val:
                continue
            if best.get(key, 0) < val:
                best[key] = val
        for key, val in best.items():
            w[key] = val
        return list(best.items())

    def op(self, eng, fn, reads=(), writes=(), extra=(), inc=True):
        reads, writes = self._excl(reads, writes)
        deps = self._deps(reads, writes) + list(extra)
        waits = self._filter(eng, deps)
        self.sem(eng)
        self.nops += 1
        if not inc:
            self.ops[eng].append((waits, fn, None, 0))
            self.pending[eng][0].extend(reads)
            self.pending[eng][1].extend(writes)
            return None
        self.cnt[eng] += 1
        tok = (eng, self.cnt[eng])
        self.ops[eng].append((waits, fn, eng, 1))
        pr, pw = self.pending[eng]
        self._commit(tok, list(pr) + list(reads), list(pw) + list(writes))
        self.pending[eng] = ([], [])
        return tok

    def dma(self, q, out, in_, slot, reads=(), writes=(), extra=(), **kw):
        key = ("d", slot)
        self.sem(key)
        deps = self._deps(reads, writes) + list(extra)
        if self.cnt[key] > 0:
            deps.append((key, self.cnt[key]))
        waits = self._filter(q, deps)
        self.cnt[key] += 16
        tok = (key, self.cnt[key])
        self.ops[q].append((waits, (lambda e, o=out, i=in_, k=kw: e.dma_start(out=o, in_=i, **k)), key, 16))
        self._commit(tok, reads, writes)
        self.nops += 1
        return tok

    def dma_fn(self, q, fn, slot, reads=(), writes=(), extra=()):
        return self.custom(q, fn, ("d", slot), 16, reads=reads, writes=writes, extra=extra)

    def custom(self, eng, fn, key, amt, reads=(), writes=(), extra=()):
        self.sem(key)
        deps = self._deps(reads, writes) + list(extra)
        if self.cnt[key] > 0:
            deps.append((key, self.cnt[key]))
        waits = self._filter(eng, deps)
        self.cnt[key] += amt
        tok = (key, self.cnt[key])
        self.ops[eng].append((waits, fn, key, amt))
        self._commit(tok, reads, writes)
        return tok

    def flush(self):
        for e in ENGS:
            assert not self.pending[e][0] and not self.pending[e][1], f"pending non-inc ops on {e}"
        final = [(k, v) for k, v in self.cnt.items() if v > 0]
        for e in ENGS:
            self.sem(e)
            waits = self._filter(e, final)
            if waits:
                self.ops[e].append((waits, None, None, 0))
        ops = self.ops
        sems = self.sems

        def emit(engine, ename):
            for waits, fn, key, amt in ops[ename]:
                for (k, v) in waits:
                    engine.wait_ge(sems[k], v)
                if fn is None:
                    continue
                ins = fn(engine)
                if key is not None:
                    ins.then_inc(sems[key], amt)

        with self.nc.Block() as block:
            @block.sync
            def _(e):
                emit(e, "sp")

            @block.scalar
            def _(e):
                emit(e, "act")

            @block.vector
            def _(e):
                emit(e, "dve")

            @block.gpsimd
            def _(e):
                emit(e, "pool")

            @block.tensor
            def _(e):
                emit(e, "pe")

        self.ops = {e: [] for e in ENGS}
        self.bufs = {}
        for e in ENGS:
            self.waited[e] = dict(self.cnt)

    def close(self):
        self.stack.close()


class Alloc:
    def __init__(self, nc):
        self.nc = nc
        self.st = ExitStack()

    def sb(self, name, shape, dt):
        return self.st.enter_context(self.nc.sbuf_tensor(name, list(shape), dt))

    def ps(self, name, shape, dt):
        return self.st.enter_context(self.nc.psum_tensor(name, list(shape), dt))

    def close(self):
        self.st.close()


def _t5_bucket_np(dist):
    exact = 16
    d_f = np.maximum(dist, 1).astype(np.float32)
    large = exact + (np.log(d_f / np.float32(exact)) / np.float32(math.log(2048 / exact))
                     * np.float32(32 - exact)).astype(np.int32)
    large = np.minimum(large, 31)
    return np.where(dist < exact, dist, large)


def _attn_tables():
    ki = np.arange(128)[:, None]
    qi = np.arange(128)[None, :]
    bidx = np.zeros((3, 128, 256), np.int64)
    mask = np.zeros((3, 128, 256), np.float32)
    for p, (w, dil) in enumerate(PATTERNS):
        rel_prev = qi + 128 - ki
        rel_cur = qi - ki
        vp = ki >= qi
        vc = ki <= qi
        bidx[p, :, 0:128] = _t5_bucket_np(np.clip(rel_prev, 0, 128) * dil)
        bidx[p, :, 128:256] = _t5_bucket_np(np.clip(rel_cur, 0, 128) * dil)
        mask[p, :, 0:128] = np.where(vp, 0.0, NEG)
        mask[p, :, 128:256] = np.where(vc, 0.0, NEG)
    return bidx, mask


def wtile_index(N, kg, nb):
    return kg * (N // 512) + nb


def phase_w(P, nc, specs, pre=None, engs=("dve", "pool")):
    A = Alloc(nc)
    if pre is not None:
        pre()
    NS = 5
    stg = [A.sb(f"w_stg{i}", [128, 8, 512], F32) for i in range(NS)]
    ob = [A.sb(f"w_ob{i}", [128, 8, 512], BF16) for i in range(NS)]
    items = []
    for (src, dst, K, N) in specs:
        nkc = K // 128
        for kg in range((nkc + 15) // 16):
            nk = min(16, nkc - kg * 16)
            for nb in range(N // 512):
                tix = wtile_index(N, kg, nb)
                for h0 in range(0, nk, 8):
                    hk = min(8, nk - h0)
                    r0 = (kg * 16 + h0) * 128
                    src_ap = src[r0:r0 + hk * 128, nb * 512:(nb + 1) * 512].rearrange("(kc p) c -> p kc c", p=128)
                    dst_ap = dst[tix].rearrange("p (kc c) -> p kc c", c=512)[:, h0:h0 + hk, :]
                    items.append((src_ap, dst_ap, hk))

    def load(i):
        src_ap, _, hk = items[i]
        s = i % NS
        P.dma("sp", stg[s][:, 0:hk, :], src_ap, ("wl", s), writes=[("wstg", s)])

    for i in range(min(NS - 1, len(items))):
        load(i)
    for i, (src_ap, dst_ap, hk) in enumerate(items):
        s = i % NS
        eng = engs[i % len(engs)]
        if eng == "act":
            P.op("act", lambda e, s=s, hk=hk: e.activation(out=ob[s][:, 0:hk, :], in_=stg[s][:, 0:hk, :], func=AF.Copy),
                 reads=[("wstg", s)], writes=[("wob", s)])
        else:
            P.op(eng, lambda e, s=s, hk=hk: e.tensor_copy(out=ob[s][:, 0:hk, :], in_=stg[s][:, 0:hk, :]),
                 reads=[("wstg", s)], writes=[("wob", s)])
        if i + NS - 1 < len(items):
            load(i + NS - 1)
        P.dma("sp", dst_ap, ob[s][:, 0:hk, :], ("ws", s), reads=[("wob", s)])
    P.flush()
    A.close()


def norm_front(P, xin, xkey, gbc, junk, ssq, ub, ubkey, tagk):
    P.op("act", lambda e: e.activation(out=junk[:], in_=xin, func=AF.Square, accum_out=ssq[:, 0:1]),
         reads=[xkey], writes=["junk", ("ssq", tagk)])
    P.op("act", lambda e: e.activation(out=ssq[:, 1:2], in_=ssq[:, 0:1], func=AF.Sqrt, scale=1.0 / D_MODEL, bias=RMS_EPS),
         reads=[("ssq", tagk)], writes=[("ssq", tagk)])
    P.op("dve", lambda e: e.reciprocal(out=ssq[:, 2:3], in_=ssq[:, 1:2]), reads=[("ssq", tagk)], writes=[("ssq", tagk)])
    P.op("dve", lambda e: e.scalar_tensor_tensor(out=ub[:], in0=xin, scalar=ssq[:, 2:3], in1=gbc[:], op0=ALU.mult, op1=ALU.mult),
         reads=[xkey, ("ssq", tagk), "gbc"], writes=[ubkey])


def norm_back(P, ub, ubkey, pT, uT_dst, uTkey, ident, evac_eng="dve"):
    pk = [("pT", 0), ("pT", 1)]
    for kc in range(16):
        P.op("pe", lambda e, kc=kc: e.transpose(out=pT[:, kc, :], in_=ub[:, kc * 128:(kc + 1) * 128], identity=ident[:]),
             reads=[ubkey, "ident"], writes=pk, inc=(kc == 15))
    for half in range(2):
        hsl = slice(half * 8, half * 8 + 8)
        eng = evac_eng if half == 0 else ("act" if evac_eng == "dve" else "dve")
        if eng == "act":
            P.op("act", lambda e, hsl=hsl: e.activation(out=uT_dst[:, hsl, :], in_=pT[:, hsl, :], func=AF.Copy), reads=[pk[half]], writes=[uTkey])
        else:
            P.op(eng, lambda e, hsl=hsl: e.tensor_copy(out=uT_dst[:, hsl, :], in_=pT[:, hsl, :]), reads=[pk[half]], writes=[uTkey])


def norm_transpose_block(P, xin, xkey, gbc, junk, ssq, ub, ubkey, pT, pTkey, uT_dst, uTkey, ident, tagk,
                         evac_eng="dve"):
    norm_front(P, xin, xkey, gbc, junk, ssq, ub, ubkey, tagk)
    norm_back(P, ub, ubkey, pT, uT_dst, uTkey, ident, evac_eng)


def phase_a(P, nc, io, nsc=SEQ // 2048):
    A = Alloc(nc)
    x = io["xb"]
    wA = A.sb("wA", [128, 16, 768], BF16)
    wst = [A.sb(f"a_wst{i}", [128, 768], F32) for i in range(2)]
    xb = [A.sb(f"a_xb{i}", [128, 2048], F32) for i in range(2)]
    gbc = A.sb("a_gbc", [128, 2048], F32)
    junk = A.sb("a_junk", [128, 2048], BF16)
    ub = [A.sb(f"a_ub{i}", [128, 2048], BF16) for i in range(2)]
    ssq = [A.sb(f"a_ssq{i}", [128, 4], F32) for i in range(2)]
    uT = [A.sb(f"a_uT{i}", [128, 16, 512], BF16) for i in range(2)]
    ident = A.sb("a_ident", [128, 128], BF16)
    identf = A.sb("a_identf", [128, 128], F32)
    ones_f = A.sb("a_ones", [128, 64], F32)
    QT = [A.sb(f"a_QT{i}", [128, 2048], BF16) for i in range(2)]
    KT = [A.sb(f"a_KT{i}", [128, 4096], BF16) for i in range(2)]
    VT = [A.sb(f"a_VT{i}", [128, 4096], BF16) for i in range(2)]
    RING = (4, 8, 32)
    Vtok = [A.sb(f"a_Vtok{p}", [128, RING[p], 4, 65], BF16) for p in range(3)]
    acc = [A.sb(f"a_acc{h}", [65, 2048], F32) for h in range(2)]
    PT = [A.sb(f"a_PT{i}", [128, 512], BF16) for i in range(3)]
    biasT = A.sb("a_biasT", [128, 3, 4, 256], BF16)
    maskf = A.sb("a_maskf", [128, 3, 256], F32)
    recs = [A.sb(f"a_rec{i}", [64, 512], F32) for i in range(2)]
    mo = [A.sb(f"a_mo{i}", [64, 2048], BF16) for i in range(2)]
    pT = [A.ps(f"a_pT{i}", [128, 16, 128], BF16) for i in range(1)]
    pproj = [A.ps(f"a_pp{i}", [128, 512], F32) for i in range(2)]
    pS = [A.ps(f"a_pS{i}", [128, 512], F32) for i in range(2)]
    pO = [A.ps(f"a_pO{i}", [128, 512], F32) for i in range(2)]

    P.dma("sp", identf[:], io["ident"], "c0", writes=["identf"])
    P.op("dve", lambda e: e.tensor_copy(out=ident[:], in_=identf[:]), reads=["identf"], writes=["ident"])
    P.op("pool", lambda e: e.memset(ones_f[:], 1.0), writes=["ones"])
    P.dma("sp", gbc[:], io["norm1_g"].partition_broadcast(128), "c1", writes=["gbc"])
    P.dma("sp", maskf[:], io["amask"].rearrange("k (p q) -> k p q", p=3), "c2", writes=["maskf"])
    for p in range(3):
        P.dma("sp", xb[0][:, 0:1024], io["abias"][:, p * 1024:(p + 1) * 1024], ("xl", 0), writes=[("xb", 0)])
        P.op("dve", lambda e, p=p: e.tensor_tensor(
            out=biasT[:, p, :, :], in0=xb[0][:, 0:1024].rearrange("k (h q) -> k h q", h=4),
            in1=maskf[:, p:p + 1, :].to_broadcast([128, 4, 256]), op=ALU.add),
            reads=[("xb", 0), "maskf"], writes=["biasT"])
    for kc in range(16):
        s = kc % 2
        P.dma("sp", wst[s][:], io["w_in_a"][kc * 128:(kc + 1) * 128, :], ("awl", s), writes=[("wst", s)])
        P.op("pool" if kc % 2 else "dve", lambda e, s=s, kc=kc: e.tensor_copy(out=wA[:, kc, :], in_=wst[s][:]),
             reads=[("wst", s)], writes=["wA"])
    for p in range(3):
        P.op("pool", lambda e, p=p: e.memset(Vtok[p][:, :, :, 64:65], 1.0), writes=[("Vones", p)])

    ctr = {"o": 0, "t": 0, "s": 0, "m": 0, "pp": 0}

    def nfront(t, tb):
        blk = t * 4 + tb
        xs = blk % 2
        P.dma("sp", xb[xs][:], x[blk * 128:(blk + 1) * 128, :], ("xl", xs), writes=[("xb", xs)])
        norm_front(P, xb[xs][:], ("xb", xs), gbc, junk, ssq[xs], ub[xs], ("ub", xs), xs)

    def nback(t, tb):
        us = t % 2
        xs = (t * 4 + tb) % 2
        norm_back(P, ub[xs], ("ub", xs), pT[0], uT[us][:, :, tb * 128:(tb + 1) * 128], ("uT", us), ident)

    def ut_store(t):
        us = t % 2
        P.dma("sp", io["uT_scr"][t].rearrange("p (kc t) -> p kc t", t=512), uT[us][:], ("uts", us), reads=[("uT", us)])

    def proj_group(t, cb):
        us = t % 2
        ring0 = (t % 8) * 512
        loc0 = (t % 4) * 512
        pp = pproj[ctr["pp"] % 2]
        ppk = ("pp", ctr["pp"] % 2)
        ctr["pp"] += 1
        for kc in range(16):
            P.op("pe", lambda e, pp=pp, cb=cb, kc=kc, us=us: e.matmul(
                pp[:], lhsT=wA[:, kc, cb * 128:(cb + 1) * 128], rhs=uT[us][:, kc, :], start=(kc == 0), stop=(kc == 15)),
                reads=["wA", ("uT", us)], writes=[ppk], inc=(kc == 15))
        hp = cb % 2
        if cb < 2:
            P.op("act", lambda e, pp=pp, hp=hp, loc0=loc0: e.activation(out=QT[hp][:, loc0:loc0 + 512], in_=pp[:], func=AF.Copy, scale=0.125),
                 reads=[ppk], writes=[("QT", hp)])
        elif cb < 4:
            P.op("dve", lambda e, pp=pp, hp=hp, ring0=ring0: e.tensor_copy(out=KT[hp][:, ring0:ring0 + 512], in_=pp[:]),
                 reads=[ppk], writes=[("KT", hp)])
        else:
            P.op("act", lambda e, pp=pp, hp=hp, ring0=ring0: e.activation(out=VT[hp][:, ring0:ring0 + 512], in_=pp[:], func=AF.Copy),
                 reads=[ppk], writes=[("VT", hp)])

    for sc in range(nsc):
        if sc == 0:
            nfront(0, 0)
            for tb in range(4):
                if tb + 1 < 4:
                    nfront(0, tb + 1)
                nback(0, tb)
            ut_store(0)
        for tl in range(4):
            t = sc * 4 + tl
            nxt = t + 1 if t + 1 < nsc * 4 else None
            sched = ["f0", "g0", "b0", "f1", "g1", "b1", "f2", "g2", "b2", "f3", "g3", "b3", "g4", "g5"]
            for it in sched:
                if it[0] == "g":
                    proj_group(t, int(it[1]))
                elif nxt is not None:
                    (nfront if it[0] == "f" else nback)(nxt, int(it[1]))
            if nxt is not None:
                ut_store(nxt)
        rbase = (sc % 2) * 2048
        for hp in range(2):
            for p, (win, dil) in enumerate(PATTERNS):
                pdist = dil
                pend = []
                po = pok = None
                for bi in range(16):
                    grp, gi = bi // 2, bi % 2
                    if gi == 0:
                        po = pO[ctr["o"] % 2]
                        pok = ("pO", ctr["o"] % 2)
                        ctr["o"] += 1
                    if dil == 1:
                        nl, r = bi, 0
                    elif dil == 4:
                        nl, r = bi // 4, bi % 4
                    else:
                        nl, r = 0, bi
                    gblk = sc * 16 + bi
                    has_prev = (sc * 2048 + nl * win) >= win
                    cur_slot = gblk % RING[p]
                    prev_slot = (gblk - pdist) % RING[p]
                    c0 = rbase + nl * win + r
                    q0 = nl * win + r
                    pr0 = (c0 - win) % 4096
                    cur_sl = slice(c0, c0 + 127 * dil + 1, dil)
                    prev_sl = slice(pr0, pr0 + 127 * dil + 1, dil)
                    q_sl = slice(q0, q0 + 127 * dil + 1, dil)
                    tb_ = ctr["t"] % 2
                    tj = tb_ * 8
                    ctr["t"] += 1
                    P.op("pe", lambda e, hp=hp, cur_sl=cur_sl, tj=tj: e.transpose(out=pT[0][:, tj, :], in_=VT[hp][:, cur_sl], identity=ident[:]),
                         reads=[("VT", hp), "ident"], writes=[("pT", tb_)])
                    P.op("dve", lambda e, p=p, cur_slot=cur_slot, hp=hp, tj=tj: e.tensor_copy(
                        out=Vtok[p][:, cur_slot, 2 * hp:2 * hp + 2, 0:64], in_=pT[0][:, tj, :].rearrange("k (h d) -> k h d", h=2)),
                        reads=[("pT", tb_)], writes=[("Vtok", p, cur_slot, hp)])
                    ps_ = pS[ctr["s"] % 2]
                    psk = ("pS", ctr["s"] % 2)
                    pt_ = PT[ctr["s"] % 3]
                    ptk = ("PT", ctr["s"] % 3)
                    ctr["s"] += 1
                    parts = ([0] if has_prev else []) + [1]
                    nmm = len(parts) * 2
                    mi = 0
                    for hl in range(2):
                        for part in parts:
                            ksl = prev_sl if part == 0 else cur_sl
                            col = (hl * 2 + part) * 128
                            mi += 1
                            P.op("pe", lambda e, ps_=ps_, hp=hp, hl=hl, ksl=ksl, q_sl=q_sl, col=col, first=(mi == 1), last=(mi == nmm): e.matmul(
                                ps_[:, col:col + 128], lhsT=KT[hp][hl * 64:(hl + 1) * 64, ksl], rhs=QT[hp][hl * 64:(hl + 1) * 64, q_sl],
                                start=first, stop=last, skip_group_check=True),
                                reads=[("KT", hp), ("QT", hp)], writes=[psk], inc=(mi == nmm))
                        if hl == 0:
                            P.op("pe", lambda e, ps_=ps_, p=p, hp=hp: e.matmul(
                                ps_[:], lhsT=ident[:], rhs=biasT[:, p, 2 * hp:2 * hp + 2, :].rearrange("k h q -> k (h q)"),
                                start=False, stop=False, skip_group_check=True),
                                reads=["ident", "biasT"], writes=[psk], inc=False)
                    if has_prev:
                        P.op("act", lambda e, ps_=ps_, pt_=pt_: e.activation(out=pt_[:], in_=ps_[:], func=AF.Exp),
                             reads=[psk], writes=[ptk])
                    else:
                        P.op("act", lambda e, ps_=ps_, pt_=pt_: e.activation(
                            out=pt_[:].rearrange("k (h c) -> k h c", h=2)[:, :, 128:256],
                            in_=ps_[:].rearrange("k (h c) -> k h c", h=2)[:, :, 128:256], func=AF.Exp),
                            reads=[psk], writes=[ptk])

                    def pv(gi=gi, parts=parts, prev_slot=prev_slot, cur_slot=cur_slot, pt_=pt_, ptk=ptk, po=po, pok=pok,
                           nl=nl, r=r, bi=bi, p=p, hp=hp, dil=dil, win=win):
                        for hl in range(2):
                            h = 2 * hp + hl
                            for pi_, part in enumerate(parts):
                                slot = prev_slot if part == 0 else cur_slot
                                col = (hl * 2 + part) * 128
                                last = (gi == 1 and hl == 1 and pi_ == len(parts) - 1)
                                oc = (hl * 2 + gi) * 128
                                P.op("pe", lambda e, slot=slot, h=h, col=col, oc=oc, pi_=pi_, np_=len(parts): e.matmul(
                                    po[0:65, oc:oc + 128], lhsT=Vtok[p][:, slot, h, :], rhs=pt_[:, col:col + 128],
                                    start=(pi_ == 0), stop=(pi_ == np_ - 1)),
                                    reads=[("Vtok", p, slot, hp), ("Vones", p), ptk], writes=[pok], inc=last)
                        if gi == 1:
                            if dil == 1:
                                nl0, r0 = nl - 1, 0
                            else:
                                nl0, r0 = nl, r - 1
                            for hl in range(2):
                                src = po[0:65, hl * 256:(hl + 1) * 256].rearrange("d (j i) -> d j i", j=2)
                                if dil == 1:
                                    dst = acc[hl][:, nl0 * 128:nl0 * 128 + 256].rearrange("d (j i) -> d j i", j=2)
                                else:
                                    dst = acc[hl][:, nl0 * win:nl0 * win + 128 * dil].rearrange("d (i r) -> d r i", r=dil)[:, r0:r0 + 2, :]
                                if p == 0:
                                    P.op("dve", lambda e, dst=dst, src=src: e.tensor_copy(out=dst, in_=src), reads=[pok], writes=[("acc", hl)])
                                else:
                                    P.op("dve", lambda e, dst=dst, src=src: e.tensor_tensor(out=dst, in0=src, in1=dst, op=ALU.add),
                                         reads=[pok, ("acc", hl)], writes=[("acc", hl)])
                    if pend:
                        pend.pop()()
                    pend.append(pv)
                if pend:
                    pend.pop()()
            for hl in range(2):
                h = 2 * hp + hl
                m = mo[ctr["m"] % 2]
                mk = ("mo", ctr["m"] % 2)
                ctr["m"] += 1
                for pc in range(4):
                    po = pO[ctr["o"] % 2]
                    pok = ("pO", ctr["o"] % 2)
                    ctr["o"] += 1
                    P.op("pe", lambda e, po=po, hl=hl, pc=pc: e.matmul(po[0:64, :], lhsT=ones_f[64:65, 0:64], rhs=acc[hl][64:65, pc * 512:(pc + 1) * 512],
                                                                    start=True, stop=True),
                         reads=[("acc", hl), "ones"], writes=[pok])
                    rc = recs[pc % 2]
                    rck = ("rec", pc % 2)
                    P.op("dve", lambda e, po=po, rc=rc: e.reciprocal(out=rc[:], in_=po[0:64, :]), reads=[pok], writes=[rck])
                    P.op("pool", lambda e, m=m, hl=hl, pc=pc, rc=rc: e.tensor_tensor(out=m[:, pc * 512:(pc + 1) * 512], in0=acc[hl][0:64, pc * 512:(pc + 1) * 512],
                                                                           in1=rc[:], op=ALU.mult),
                         reads=[("acc", hl), rck], writes=[mk])
                P.dma("sp", io["mix_loc"][2 * sc:2 * sc + 2, h * 64:(h + 1) * 64, :].rearrange("j r t -> r j t"),
                      m[:].rearrange("r (j t) -> r j t", j=2), ("mos", mk[1]), reads=[mk])
    P.flush()
    A.close()


def phase_d(P, nc, io, ntile=4):
    A = Alloc(nc)
    xs = io["xs"]
    mix_gath = io["mix_gath"]
    out = io["out"]
    RS = 4
    ring = [A.sb(f"d_wr{i}", [128, 16, 512], BF16) for i in range(RS)]
    aT16 = A.sb("d_aT16", [128, 16, 512], BF16)
    mT16 = A.sb("d_mT16", [128, 16, 512], BF16)
    actT = A.sb("d_actT", [128, 44, 512], BF16)
    h = A.sb("d_h", [128, 4, 2048], F32)
    g2 = A.sb("d_g2", [128, 2048], F32)
    gF = A.sb("d_gF", [128, 2048], F32)
    junk = A.sb("d_junk", [128, 2048], BF16)
    ub = [A.sb(f"d_ub{i}", [128, 2048], BF16) for i in range(2)]
    ssq = [A.sb(f"d_ssq{i}", [128, 4], F32) for i in range(4)]
    sg = [A.sb(f"d_sg{i}", [128, 512], F32) for i in range(3)]
    ident = A.sb("d_ident", [128, 128], BF16)
    identf = A.sb("d_identf", [128, 128], F32)
    pb = [A.ps(f"d_pb{i}", [128, 512], F32) for i in range(6)]
    pT = A.ps("d_pT", [128, 16, 128], BF16)

    P.dma("pool", identf[:], io["ident"], "c0", writes=["identf"])
    P.op("dve", lambda e: e.tensor_copy(out=ident[:], in_=identf[:]), reads=["identf"], writes=["ident"])
    P.dma("pool", g2[:], io["norm2_g"].partition_broadcast(128), "c1", writes=["gbc"])
    P.dma("pool", gF[:], io["final_g"].partition_broadcast(128), "c2", writes=["gF"])

    seq = []
    for tt in range(ntile):
        for db in range(4):
            seq.append((io["wo_scr"], wtile_index(2048, 0, db), 16))
        for j in range(11):
            seq.append((io["wgu_scr"], wtile_index(11264, 0, j), 16))
            seq.append((io["wgu_scr"], wtile_index(11264, 0, 11 + j), 16))
        for db in range(4):
            for kg in range(3):
                seq.append((io["wd_scr"], wtile_index(2048, kg, db), 16 if kg < 2 else 12))
    st = {"issued": 0, "used": 0}

    def issue_one():
        n = st["issued"]
        if n >= len(seq):
            return
        scr, tix, nk = seq[n]
        s = n % RS
        P.dma("sp", ring[s][:, 0:nk, :], scr[tix].rearrange("p (kc c) -> p kc c", c=512)[:, 0:nk, :], ("wr", s), writes=[("ring", s)])
        st["issued"] += 1

    def next_w():
        n = st["used"]
        while st["issued"] < min(len(seq), n + RS - 1):
            issue_one()
        st["used"] += 1
        return ring[n % RS], ("ring", n % RS)

    bctr = {"b": 0}

    def bank():
        i = bctr["b"] % 6
        bctr["b"] += 1
        return pb[i], ("pb", i)

    def mix_load(tt):
        t0 = tt * 512

        def _mix_load(e, t0=t0):
            q = e.partition_id() % 4
            tl0 = (t0 % 1024)
            src = mix_gath.rearrange("j (cc p) t -> p j cc t", p=128)[:, bass.ds(q * 2 + t0 // 1024, 1), :, tl0:tl0 + 512]
            return e.dma_start(out=mT16[:].unsqueeze(1), in_=src)
        P.dma_fn("pool", _mix_load, "mixl", writes=["mT16"])

    def x_load(tt, tb):
        r0 = tt * 512 + tb * 128
        P.dma("pool", h[:, tb, :], xs[r0:r0 + 128, :], ("xl", tb), writes=[("h", tb)])

    for tt in range(ntile):
        t0 = tt * 512
        if tt == 0:
            mix_load(0)
            for tb in range(4):
                x_load(0, tb)
        for db in range(4):
            w, wk = next_w()
            for tb in range(4):
                b_, bk = bank()
                for cc in range(16):
                    P.op("pe", lambda e, b_=b_, w=w, cc=cc, tb=tb: e.matmul(
                        b_[:], lhsT=mT16[:, cc, tb * 128:(tb + 1) * 128], rhs=w[:, cc, :], start=(cc == 0), stop=(cc == 15)),
                        reads=["mT16", wk], writes=[bk], inc=(cc == 15))
                hs = h[:, tb, db * 512:(db + 1) * 512]
                P.op("dve", lambda e, hs=hs, b_=b_: e.tensor_tensor(out=hs, in0=b_[:], in1=hs, op=ALU.add),
                     reads=[bk, ("h", tb)], writes=[("h", tb)])
        if tt + 1 < ntile:
            mix_load(tt + 1)
        for tb in range(4):
            norm_transpose_block(P, h[:, tb, :], ("h", tb), g2, junk, ssq[tb], ub[tb % 2], ("ub", tb % 2),
                                 pT, "pT", aT16[:, :, tb * 128:(tb + 1) * 128], "aT16", ident, tb,
                                 evac_eng="act" if tb % 2 else "dve")
        for j in range(11):
            wg, wgk = next_w()
            wu, wuk = next_w()
            for fs in range(4):
                bg, bgk = bank()
                bu, buk = bank()
                for kc in range(16):
                    P.op("pe", lambda e, bg=bg, wg=wg, kc=kc, fs=fs: e.matmul(
                        bg[:], lhsT=wg[:, kc, fs * 128:(fs + 1) * 128], rhs=aT16[:, kc, :], start=(kc == 0), stop=(kc == 15)),
                        reads=["aT16", wgk], writes=[bgk], inc=(kc == 15))
                for kc in range(16):
                    P.op("pe", lambda e, bu=bu, wu=wu, kc=kc, fs=fs: e.matmul(
                        bu[:], lhsT=wu[:, kc, fs * 128:(fs + 1) * 128], rhs=aT16[:, kc, :], start=(kc == 0), stop=(kc == 15)),
                        reads=["aT16", wuk], writes=[buk], inc=(kc == 15))
                fi = j * 4 + fs
                s_ = sg[fi % 3]
                sk = ("sg", fi % 3)
                P.op("act", lambda e, s_=s_, bg=bg: e.activation(out=s_[:], in_=bg[:], func=AF.Silu), reads=[bgk], writes=[sk])
                P.op("dve", lambda e, s_=s_, bu=bu, fi=fi: e.tensor_tensor(out=actT[:, fi, :], in0=bu[:], in1=s_[:], op=ALU.mult),
                     reads=[buk, sk], writes=[("actT", fi)])
        for db in range(4):
            accb = [bank() for _ in range(4)]
            for kg in range(3):
                w, wk = next_w()
                nk = 16 if kg < 2 else 12
                for tb in range(4):
                    b_, bk = accb[tb]
                    for kl in range(nk):
                        fc = kg * 16 + kl
                        first = (fc == 0)
                        last = (fc == 43)
                        P.op("pe", lambda e, b_=b_, w=w, kl=kl, fc=fc, tb=tb, first=first, last=last: e.matmul(
                            b_[:], lhsT=actT[:, fc, tb * 128:(tb + 1) * 128], rhs=w[:, kl, :], start=first, stop=last),
                            reads=[("actT", fc), wk], writes=[bk], inc=(kl == nk - 1))
            for tb in range(4):
                b_, bk = accb[tb]
                hs = h[:, tb, db * 512:(db + 1) * 512]
                P.op("dve", lambda e, hs=hs, b_=b_: e.tensor_tensor(out=hs, in0=b_[:], in1=hs, op=ALU.add),
                     reads=[bk, ("h", tb)], writes=[("h", tb)])
        for tb in range(4):
            sq = ssq[tb]
            hs = h[:, tb, :]
            P.op("act", lambda e, hs=hs, sq=sq: e.activation(out=junk[:], in_=hs, func=AF.Square, accum_out=sq[:, 0:1]),
                 reads=[("h", tb)], writes=["junk", ("ssq", tb)])
            P.op("act", lambda e, sq=sq: e.activation(out=sq[:, 1:2], in_=sq[:, 0:1], func=AF.Sqrt, scale=1.0 / D_MODEL, bias=RMS_EPS),
                 reads=[("ssq", tb)], writes=[("ssq", tb)])
            P.op("dve", lambda e, sq=sq: e.reciprocal(out=sq[:, 2:3], in_=sq[:, 1:2]), reads=[("ssq", tb)], writes=[("ssq", tb)])
            P.op("dve", lambda e, hs=hs, sq=sq: e.scalar_tensor_tensor(out=hs, in0=hs, scalar=sq[:, 2:3], in1=gF[:], op0=ALU.mult, op1=ALU.mult),
                 reads=[("h", tb), ("ssq", tb), "gF"], writes=[("h", tb)])
            P.dma("pool", out[t0 + tb * 128:t0 + (tb + 1) * 128, :], h[:, tb, :], ("outs", tb), reads=[("h", tb)])
            if tt + 1 < ntile:
                x_load(tt + 1, tb)
    P.flush()
    A.close()


V_MIX_R, V_MIX_K, V_MIX_V = 0, 4, 8
V_MIX_W, V_MIX_A, V_MIX_G0, V_MIX_G1 = 12, 13, 14, 15
V_W0, V_A0, V_KK, V_KA, V_RK, V_LNW, V_LNB, V_OMKA = 16, 20, 24, 28, 36, 40, 44, 48
C_SU, C_IU, C_SL, C_ID, C_AVG, C_ONE, C_RST = 0, 64, 128, 192, 256, 320, 384


R_STOP = 99


def phase_r(P, nc, io, ntile=SEQ // 512, wspecs=None, gather=False):
    A = Alloc(nc)
    DS = DECAY_SCALE
    wR = A.sb("r_wR", [128, 16, RW_COLS], BF16)
    wstg = A.sb("r_wstg", [128, 2, 2, 512], F32)
    wob = A.sb("r_wob", [128, 2, 2, 512], BF16)
    wst = [wstg[:].rearrange("p a b c -> p (a b c)")[:, 0:RW_COLS]]
    uT = A.sb("r_uT", [128, 16, 512], BF16)
    vec = A.sb("r_vec", [128, 64], F32)
    cst = A.sb("r_cst", [64, 1024], F32)
    wupb = A.sb("r_wupb", [64, 256], BF16)
    aupb = A.sb("r_aupb", [64, 256], BF16)
    gupb0 = A.sb("r_gupb0", [128, 256], BF16)
    gupb1 = A.sb("r_gupb1", [32, 256], BF16)
    Ib = A.sb("r_Ib", [64, 64], BF16)
    mask4 = A.sb("r_mask4", [64, 2, 2, 64], F32)
    zf = [A.sb(f"r_zf{i}", [128, 516], F32) for i in range(2)]
    tmpd = [A.sb(f"r_tmpd{i}", [128, 512], F32) for i in range(2)]
    cr = A.sb("r_cr", [128, 16], F32)
    zs_rkv = A.sb("r_zs", [64, 12, 512], F32)
    zs_w = A.sb("r_zsw", [64, 512], F32)
    zs_a = A.sb("r_zsa", [64, 512], F32)
    zs_g0 = A.sb("r_zsg0", [128, 512], F32)
    zs_g1 = A.sb("r_zsg1", [32, 512], F32)
    lob_w = A.sb("r_lobw", [64, 512], BF16)
    lob_a = A.sb("r_loba", [64, 512], BF16)
    gsb0 = A.sb("r_gsb0", [128, 512], BF16)
    gsb1 = A.sb("r_gsb1", [32, 512], BF16)
    tn = ["cs", "ei", "ev", "ex", "er", "sq", "n", "kk", "km", "bv", "rk"]
    T = {n: A.sb(f"r_t_{n}", [64, 512], F32) for n in tn}
    T["yc"] = T["rk"]
    T["sw"] = zs_w
    T["as"] = zs_a
    RA = A.sb("r_RA", [64, 4, 8, 2, 64], BF16)
    KB = A.sb("r_KB", [64, 4, 8, 2, 64], BF16)
    TR = A.sb("r_TR", [64, 4, 8, 4, 64], BF16)
    PC = A.sb("r_PC", [64, 4, 8], F32)
    bon = A.sb("r_bon", [64, 4, 512], F32)
    gT = A.sb("r_gT", [64, 4, 512], F32)
    yT = A.sb("r_yT", [64, 4, 512], F32)
    mo = A.sb("r_mo", [64, 4, 512], BF16)
    TK = A.sb("r_TK", [64, 4, 4, 64], BF16)
    MS = A.sb("r_MS", [64, 4, 2, 2, 64], BF16)
    Lb = [A.sb(f"r_L{i}", [64, 4, 64], BF16) for i in range(2)]
    Nb = [A.sb(f"r_N{i}", [64, 4, 64], BF16) for i in range(2)]
    Pb = [A.sb(f"r_P{i}", [64, 4, 64], BF16) for i in range(2)]
    ILb = A.sb("r_IL", [64, 4, 64], BF16)
    WT = A.sb("r_WT", [64, 4, 64], BF16)
    Zs = A.sb("r_Zs", [64, 4, 64], BF16)
    Ut = A.sb("r_Ut", [64, 4, 64], F32)
    Ub = A.sb("r_Ub", [64, 4, 64], BF16)
    Hf = A.sb("r_Hf", [64, 4, 64], F32)
    Hd = A.sb("r_Hd", [64, 4, 64], F32)
    Hb = A.sb("r_Hb", [64, 4, 64], BF16)
    pproj = [A.ps(f"r_pp{i}", [128, 512], F32) for i in range(2)]
    ptrs = [A.ps(f"r_ptr{i}", [64, 2, 4, 64], BF16) for i in range(2)]
    pgen = [A.ps(f"r_pg{i}", [64, 512], F32) for i in range(4)]
    paux = pgen[0]
    banks = [(pgen[i], ("pg", i)) for i in range(4)] + [(pproj[i][0:64, :], ("pp", i)) for i in range(2)]
    bctr = {"b": 0}

    def bank():
        b_ = banks[bctr["b"] % len(banks)]
        bctr["b"] += 1
        return b_

    P.dma("sp", vec[:], io["rw_vec"], "c0", writes=["vec"])
    P.dma("sp", cst[:], io["rw_const"], "c1", writes=["cst"])
    P.dma("sp", tmpd[0][:], io["rw_mats"][:, 0:512], "c2", writes=[("tmpd", 0)])
    P.dma("sp", tmpd[1][:], io["rw_mats"][:, 512:1024], "c3", writes=[("tmpd", 1)])
    P.op("dve", lambda e: e.tensor_copy(out=wupb[:], in_=tmpd[0][0:64, 0:256]), reads=[("tmpd", 0)], writes=["wupb"])
    P.op("dve", lambda e: e.tensor_copy(out=aupb[:], in_=tmpd[0][0:64, 256:512]), reads=[("tmpd", 0)], writes=["aupb"])
    P.op("dve", lambda e: e.tensor_copy(out=gupb0[:], in_=tmpd[1][:, 0:256]), reads=[("tmpd", 1)], writes=["gupb0"])
    P.op("dve", lambda e: e.tensor_copy(out=gupb1[:], in_=tmpd[1][0:32, 256:512]), reads=[("tmpd", 1)], writes=["gupb1"])
    P.op("dve", lambda e: e.tensor_copy(out=Ib[:], in_=cst[:, C_ID:C_ID + 64]), reads=["cst"], writes=["Ib"])
    for a in range(2):
        P.op("dve", lambda e, a=a: e.tensor_copy(out=mask4[:, a, 0, :], in_=cst[:, C_SU:C_SU + 64]), reads=["cst"], writes=["mask4"])
        P.op("dve", lambda e, a=a: e.tensor_copy(out=mask4[:, a, 1, :], in_=cst[:, C_IU:C_IU + 64]), reads=["cst"], writes=["mask4"])
    P.op("dve", lambda e: e.tensor_scalar(out=vec[:, V_OMKA:V_OMKA + 4], in0=vec[:, V_KA:V_KA + 4], scalar1=-1.0, scalar2=1.0, op0=ALU.mult, op1=ALU.add),
         reads=["vec"], writes=["vec"])
    for kc in range(16):
        s = 0
        P.dma("sp", wst[s], io["w_in_r"][kc * 128:(kc + 1) * 128, :], ("rwl", s), writes=[("wstg", 0), ("wstg", 1)])
        P.op("pool" if kc % 2 else "dve", lambda e, s=s, kc=kc: e.tensor_copy(out=wR[:, kc, :], in_=wst[s]),
             reads=[("wstg", 0), ("wstg", 1)], writes=["wR"])
    P.op("pool", lambda e: e.memset(cr[:], 0.0), writes=["cr"])
    P.op("pool", lambda e: e.memset(Hf[:], 0.0), writes=[("Hf", 0), ("Hf", 1)])
    P.op("pool", lambda e: e.memset(Hb[:], 0.0), writes=[("Hb", 0), ("Hb", 1)])

    groups = []
    for qi, mv in enumerate((V_MIX_R, V_MIX_K, V_MIX_V)):
        for h in range(4):
            groups.append((qi * 256 + h * 64, 64, zs_rkv[:, qi * 4 + h, :], mv + h, ("zs", qi, h)))
    groups.append((768, 64, zs_w[:], V_MIX_W, "zsw"))
    groups.append((832, 64, zs_a[:], V_MIX_A, "zsa"))
    groups.append((896, 128, zs_g0[:], V_MIX_G0, "zsg0"))
    groups.append((1024, 32, zs_g1[:], V_MIX_G1, "zsg1"))

    ctr = {"pp": 0, "z": 0}

    def vcol(c, m=64):
        return vec[0:m, c:c + 1]

    def make_tile_fns():
        def proj_group(gi):
            (c0, M, dst, mcol, zkey) = groups[gi]
            pp = pproj[ctr["pp"] % 2]
            ppk = ("pp", ctr["pp"] % 2)
            ctr["pp"] += 1
            for kc in range(16):
                P.op("pe", lambda e, pp=pp, M=M, c0=c0, kc=kc: e.matmul(pp[0:M, :], lhsT=wR[:, kc, c0:c0 + M], rhs=uT[:, kc, :],
                                                                     start=(kc == 0), stop=(kc == 15)),
                     reads=["wR", "uT"], writes=[ppk], inc=(kc == 15))
            zi = ctr["z"] % 2
            ctr["z"] += 1
            z = zf[zi]
            zk = ("zf", zi)
            td = tmpd[zi]
            tdk = ("tmpd", zi)
            P.op("act", lambda e, z=z, pp=pp, M=M: e.activation(out=z[0:M, 1:513], in_=pp[0:M, :], func=AF.Copy), reads=[ppk], writes=[zk])
            P.op("pool", lambda e, z=z, M=M, gi=gi: e.tensor_copy(out=z[0:M, 0:1], in_=cr[0:M, gi:gi + 1]), reads=["cr", zk], writes=[zk])
            P.op("dve", lambda e, z=z, td=td, pp=pp, M=M: e.tensor_tensor(out=td[0:M, :], in0=z[0:M, 0:512], in1=pp[0:M, :], op=ALU.subtract),
                 reads=[zk, ppk], writes=[tdk])
            P.op("dve", lambda e, td=td, pp=pp, M=M, dst=dst, mcol=mcol: e.scalar_tensor_tensor(
                out=dst, in0=td[0:M, :], scalar=vec[0:M, mcol:mcol + 1], in1=pp[0:M, :], op0=ALU.mult, op1=ALU.add),
                reads=[tdk, ppk, "vec"], writes=[zkey])
            P.op("pool", lambda e, z=z, M=M, gi=gi: e.tensor_copy(out=cr[0:M, gi:gi + 1], in_=z[0:M, 512:513]), reads=[zk], writes=["cr"])
        def lora_acts():
            P.op("act", lambda e: e.activation(out=lob_w[:], in_=zs_w[:], func=AF.Tanh), reads=["zsw"], writes=["lobw"])
            P.op("pool", lambda e: e.tensor_copy(out=lob_a[:], in_=zs_a[:]), reads=["zsa"], writes=["loba"])
            P.op("act", lambda e: e.activation(out=gsb0[:], in_=zs_g0[:], func=AF.Sigmoid), reads=["zsg0"], writes=["gsb0"])
            P.op("act", lambda e: e.activation(out=gsb1[:], in_=zs_g1[:], func=AF.Sigmoid), reads=["zsg1"], writes=["gsb1"])
        def preproc(h):
            r_ = zs_rkv[:, h, :]
            k_ = zs_rkv[:, 4 + h, :]
            v_ = zs_rkv[:, 8 + h, :]
            rk_, kk_, vk_ = ("zs", 0, h), ("zs", 1, h), ("zs", 2, h)
            hs = slice(h * 64, (h + 1) * 64)
            P.op("pe", lambda e, hs=hs: e.matmul(paux[:], lhsT=wupb[:, hs], rhs=lob_w[:], start=True, stop=True),
                 reads=["wupb", "lobw"], writes=[("pg", 0)])
            P.op("act", lambda e, h=h: e.activation(out=T["sw"][:], in_=paux[:], func=AF.Sigmoid, bias=vcol(V_W0 + h)),
                 reads=[("pg", 0), "vec"], writes=["zsw"])
            P.op("dve", lambda e: e.tensor_tensor_scan(out=T["cs"][:], data0=cst[:, C_RST:C_RST + 512], data1=T["sw"][:], initial=0.0,
                                                       op0=ALU.mult, op1=ALU.add), reads=["zsw", "cst"], writes=["t_cs"])
            P.op("act", lambda e: e.activation(out=T["ei"][:], in_=T["cs"][:], func=AF.Exp, scale=-DS), reads=["t_cs"], writes=["t_ei"])
            P.op("act", lambda e: e.activation(out=T["ev"][:], in_=T["cs"][:], func=AF.Exp, scale=DS), reads=["t_cs"], writes=["t_ev"])
            P.op("pool", lambda e: e.tensor_tensor(out=T["ex"][:], in0=T["cs"][:], in1=T["sw"][:], op=ALU.subtract),
                 reads=["t_cs", "zsw"], writes=["t_ex"])
            P.op("act", lambda e: e.activation(out=T["ex"][:], in_=T["ex"][:], func=AF.Exp, scale=-DS), reads=["t_ex"], writes=["t_ex"])
            P.op("pool", lambda e, h=h: e.tensor_copy(out=PC[:, h, :], in_=T["ei"][:, 63:512:64]), reads=["t_ei"], writes=["PC"])
            P.op("dve", lambda e: e.tensor_tensor(out=T["er"][:].rearrange("j (c s) -> j c s", s=64),
                                                  in0=T["ev"][:].rearrange("j (c s) -> j c s", s=64),
                                                  in1=T["ei"][:].rearrange("j (c s) -> j c s", s=64)[:, :, 63:64].to_broadcast([64, 8, 64]),
                                                  op=ALU.mult), reads=["t_ev", "t_ei"], writes=["t_er"])
            P.op("pe", lambda e, hs=hs: e.matmul(paux[:], lhsT=aupb[:, hs], rhs=lob_a[:], start=True, stop=True),
                 reads=["aupb", "loba"], writes=[("pg", 0)])
            P.op("act", lambda e, h=h: e.activation(out=T["as"][:], in_=paux[:], func=AF.Sigmoid, bias=vcol(V_A0 + h)),
                 reads=[("pg", 0), "vec"], writes=["zsa"])
            P.op("act", lambda e, k_=k_, h=h: e.activation(out=T["sq"][:], in_=k_, func=AF.Square, scale=vcol(V_KK + h)),
                 reads=[kk_, "vec"], writes=["t_sq"])
            P.op("pe", lambda e: e.matmul(paux[:], lhsT=cst[:, C_ONE:C_ONE + 64], rhs=T["sq"][:], start=True, stop=True),
                 reads=["cst", "t_sq"], writes=[("pg", 0)])
            P.op("act", lambda e: e.activation(out=T["n"][:], in_=paux[:], func=AF.Sqrt), reads=[("pg", 0)], writes=["t_n"])
            P.op("dve", lambda e: e.tensor_scalar(out=T["n"][:], in0=T["n"][:], scalar1=1e-12, scalar2=None, op0=ALU.max), reads=["t_n"], writes=["t_n"])
            P.op("dve", lambda e: e.reciprocal(out=T["n"][:], in_=T["n"][:]), reads=["t_n"], writes=["t_n"])
            P.op("dve", lambda e, k_=k_, h=h: e.scalar_tensor_tensor(out=T["kk"][:], in0=k_, scalar=vcol(V_KK + h), in1=T["n"][:],
                                                                   op0=ALU.mult, op1=ALU.mult), reads=[kk_, "vec", "t_n"], writes=["t_kk"])
            P.op("pool", lambda e, h=h: e.tensor_scalar(out=T["km"][:], in0=T["as"][:], scalar1=vcol(V_KA + h), scalar2=vcol(V_OMKA + h),
                                                     op0=ALU.mult, op1=ALU.add), reads=["zsa", "vec"], writes=["t_km"])
            P.op("pool", lambda e, k_=k_: e.tensor_tensor(out=T["km"][:], in0=T["km"][:], in1=k_, op=ALU.mult), reads=["t_km", kk_], writes=["t_km"])
            P.op("pool", lambda e: e.tensor_tensor(out=T["bv"][:], in0=T["kk"][:], in1=T["as"][:], op=ALU.mult), reads=["t_kk", "zsa"], writes=["t_bv"])

            def c3(ap):
                return ap.rearrange("j (c s) -> j c s", s=64)
            P.op("dve", lambda e, h=h: e.scalar_tensor_tensor(out=RA[:, h, :, 0, :], in0=c3(T["kk"][:]), scalar=-1.0, in1=c3(T["ex"][:]),
                                                            op0=ALU.mult, op1=ALU.mult), reads=["t_kk", "t_ex"], writes=[("RA", h)])
            P.op("pool", lambda e, h=h, r_=r_: e.tensor_tensor(out=RA[:, h, :, 1, :], in0=c3(r_), in1=c3(T["ei"][:]), op=ALU.mult),
                 reads=[rk_, "t_ei"], writes=[("RA", h)])
            P.op("dve", lambda e, h=h: e.tensor_tensor(out=KB[:, h, :, 0, :], in0=c3(T["km"][:]), in1=c3(T["ev"][:]), op=ALU.mult),
                 reads=["t_km", "t_ev"], writes=[("KB", h)])
            P.op("pool", lambda e, h=h: e.tensor_tensor(out=KB[:, h, :, 1, :], in0=c3(T["bv"][:]), in1=c3(T["ev"][:]), op=ALU.mult),
                 reads=["t_bv", "t_ev"], writes=[("KB", h)])
            P.op("dve", lambda e, h=h: e.tensor_tensor(out=TR[:, h, :, 0, :], in0=c3(T["km"][:]), in1=c3(T["er"][:]), op=ALU.mult),
                 reads=["t_km", "t_er"], writes=[("TR", h)])
            P.op("pool", lambda e, h=h: e.tensor_tensor(out=TR[:, h, :, 1, :], in0=c3(T["bv"][:]), in1=c3(T["er"][:]), op=ALU.mult),
                 reads=["t_bv", "t_er"], writes=[("TR", h)])
            P.op("act", lambda e, h=h, v_=v_: e.activation(out=TR[:, h, :, 2, :], in_=c3(v_), func=AF.Copy), reads=[vk_], writes=[("TR", h)])
            P.op("pool", lambda e, h=h: e.tensor_copy(out=TR[:, h, :, 3, :], in_=RA[:, h, :, 0, :]), reads=[("RA", h)], writes=[("TR", h)])
            P.op("dve", lambda e, h=h, r_=r_: e.scalar_tensor_tensor(out=T["rk"][:], in0=r_, scalar=vcol(V_RK + h), in1=T["km"][:],
                                                                   op0=ALU.mult, op1=ALU.mult), reads=[rk_, "vec", "t_km"], writes=["t_rk"])
            P.op("pe", lambda e: e.matmul(paux[:], lhsT=cst[:, C_ONE:C_ONE + 64], rhs=T["rk"][:], start=True, stop=True),
                 reads=["cst", "t_rk"], writes=[("pg", 0)])
            P.op("dve", lambda e, h=h, v_=v_: e.tensor_tensor(out=bon[:, h, :], in0=paux[:], in1=v_, op=ALU.mult), reads=[("pg", 0), vk_], writes=[("bon", h)])
            P.op("pe", lambda e, hs=hs: e.matmul(paux[:], lhsT=gupb0[:, hs], rhs=gsb0[:], start=True, stop=False),
                 reads=["gupb0", "gsb0"], writes=[("pg", 0)], inc=False)
            P.op("pe", lambda e, hs=hs: e.matmul(paux[:], lhsT=gupb1[:, hs], rhs=gsb1[:], start=False, stop=True),
                 reads=["gupb1", "gsb1"], writes=[("pg", 0)])
            P.op("act", lambda e, h=h: e.activation(out=gT[:, h, :], in_=paux[:], func=AF.Copy), reads=[("pg", 0)], writes=[("gT", h)])

        def chunks():
            def chunk_group(c, g):
                hh = (2 * g, 2 * g + 1)
                hs2 = slice(2 * g, 2 * g + 2)
                K_ = lambda n, *a: (n, g) + a
                pt = ptrs[g]
                ptk = ("ptr", g)
                for hi, h in enumerate(hh):
                    for q in range(4):
                        P.op("pe", lambda e, hi=hi, h=h, q=q: e.transpose(out=pt[:, hi, q, :], in_=TR[:, h, c, q, :], identity=Ib[:]),
                             reads=[("TR", h), "Ib"], writes=[ptk], inc=(hi == 1 and q == 3))
                P.op("act", lambda e: e.activation(out=TK[:, hs2], in_=pt[:], func=AF.Copy), reads=[ptk], writes=[K_("TK")])
                yield
                b1, b1k = bank()
                v1 = b1[:, 0:512].rearrange("s (h a x) -> s h a x", h=2, a=2)
                for hi, h in enumerate(hh):
                    rhs = RA[:, h, c, :, :].rearrange("j a t -> j (a t)")
                    P.op("pe", lambda e, hi=hi, h=h, rhs=rhs: e.matmul(v1[:, hi, 0, :], lhsT=KB[:, h, c, 1, :], rhs=rhs, start=True, stop=True),
                         reads=[("KB", h), ("RA", h)], writes=[b1k], inc=False)
                    P.op("pe", lambda e, hi=hi, h=h, rhs=rhs: e.matmul(v1[:, hi, 1, :], lhsT=KB[:, h, c, 0, :], rhs=rhs, start=True, stop=True),
                         reads=[("KB", h), ("RA", h)], writes=[b1k], inc=(hi == 1))
                b2, b2k = bank()
                v2 = b2[:, 0:128].rearrange("s (h t) -> s h t", h=2)
                for hi, h in enumerate(hh):
                    P.op("pe", lambda e, hi=hi, h=h: e.matmul(v2[:, hi, :], lhsT=RA[:, h, c, 0, :], rhs=KB[:, h, c, 1, :], start=True, stop=True),
                         reads=[("KB", h), ("RA", h)], writes=[b2k], inc=(hi == 1))
                P.op("dve", lambda e: e.tensor_tensor(
                    out=MS[:, hs2].rearrange("s h a b t -> s h (a b t)"), in0=v1.rearrange("s h a x -> s h (a x)"),
                    in1=mask4[:].rearrange("s a b t -> s (a b t)").unsqueeze(1).to_broadcast([64, 2, 256]), op=ALU.mult),
                    reads=[b1k, "mask4"], writes=[K_("MS")])
                P.op("dve", lambda e: e.tensor_tensor(out=Lb[0][:, hs2], in0=v2, in1=cst[:, C_SL:C_SL + 64].unsqueeze(1).to_broadcast([64, 2, 64]),
                                                      op=ALU.mult), reads=[b2k, "cst"], writes=[K_("L", 0)])
                P.op("pool", lambda e: e.tensor_tensor(out=Pb[0][:, hs2], in0=MS[:, hs2, 0, 0, :], in1=cst[:, C_ID:C_ID + 64].unsqueeze(1).to_broadcast([64, 2, 64]),
                                                       op=ALU.add), reads=[K_("MS"), "cst"], writes=[K_("P", 0)])
                yield
                for k in range(5):
                    li, lo = k % 2, (k + 1) % 2
                    Nk = (lambda h: MS[:, h, 0, 0, :]) if k == 0 else (lambda h, li=li: Nb[li][:, h, :])
                    nkey = K_("MS") if k == 0 else K_("N", li)
                    bl, blk = bank()
                    vl = bl[:, 0:128].rearrange("s (h t) -> s h t", h=2)
                    for hi, h in enumerate(hh):
                        P.op("pe", lambda e, hi=hi, h=h, Nk=Nk, li=li, vl=vl: e.matmul(vl[:, hi, :], lhsT=Nk(h), rhs=Lb[li][:, h, :], start=True, stop=True),
                             reads=[nkey, K_("L", li)], writes=[blk], inc=(hi == 1))
                    if k < 4:
                        bn, bnk = bank()
                        vn = bn[:, 0:128].rearrange("s (h t) -> s h t", h=2)
                        for hi, h in enumerate(hh):
                            P.op("pe", lambda e, hi=hi, h=h, Nk=Nk, li=li, vn=vn: e.matmul(vn[:, hi, :], lhsT=Lb[li][:, h, :], rhs=Nk(h), start=True, stop=True),
                                 reads=[nkey, K_("L", li)], writes=[bnk], inc=(hi == 1))
                    P.op("act", lambda e, lo=lo, vl=vl: e.activation(out=Lb[lo][:, hs2], in_=vl, func=AF.Copy), reads=[blk], writes=[K_("L", lo)])
                    if k < 4:
                        P.op("dve", lambda e, lo=lo, vn=vn: e.tensor_copy(out=Nb[lo][:, hs2], in_=vn), reads=[bnk], writes=[K_("N", lo)])
                    yield
                    bp, bpk = bank()
                    vp = bp[:, 0:128].rearrange("s (h t) -> s h t", h=2)
                    for hi, h in enumerate(hh):
                        P.op("pe", lambda e, hi=hi, h=h, li=li, lo=lo, vp=vp: e.matmul(vp[:, hi, :], lhsT=Lb[lo][:, h, :], rhs=Pb[li][:, h, :], start=True, stop=False),
                             reads=[K_("L", lo), K_("P", li)], writes=[bpk], inc=False)
                        P.op("pe", lambda e, hi=hi, h=h, li=li, vp=vp: e.matmul(vp[:, hi, :], lhsT=Ib[:], rhs=Pb[li][:, h, :], start=False, stop=True),
                             reads=["Ib", K_("P", li)], writes=[bpk], inc=(hi == 1))
                    if k % 2 == 0:
                        P.op("dve", lambda e, lo=lo, vp=vp: e.tensor_copy(out=Pb[lo][:, hs2], in_=vp), reads=[bpk], writes=[K_("P", lo)])
                    else:
                        P.op("act", lambda e, lo=lo, vp=vp: e.activation(out=Pb[lo][:, hs2], in_=vp, func=AF.Copy), reads=[bpk], writes=[K_("P", lo)])
                    yield
                TT = Pb[1]
                ttk = K_("P", 1)
                bw, bwk = bank()
                vw = bw[:, 0:128].rearrange("s (h t) -> s h t", h=2)
                for hi, h in enumerate(hh):
                    P.op("pe", lambda e, hi=hi, h=h: e.matmul(vw[:, hi, :], lhsT=TK[:, h, 3, :], rhs=TT[:, h, :], start=True, stop=True),
                         reads=[K_("TK"), ttk], writes=[bwk], inc=(hi == 1))
                bz, bzk = bank()
                vz = bz[:, 0:128].rearrange("s (h t) -> s h t", h=2)
                for hi, h in enumerate(hh):
                    P.op("pe", lambda e, hi=hi, h=h: e.matmul(vz[:, hi, :], lhsT=MS[:, h, 1, 0, :], rhs=TK[:, h, 2, :], start=True, stop=True),
                         reads=[K_("MS"), K_("TK")], writes=[bzk], inc=(hi == 1))
                P.op("act", lambda e: e.activation(out=WT[:, hs2], in_=vw, func=AF.Copy), reads=[bwk], writes=[K_("WT")])
                P.op("dve", lambda e: e.tensor_copy(out=Zs[:, hs2], in_=vz), reads=[bzk], writes=[K_("Zs")])
                yield
                bu, buk = bank()
                vu = bu[:, 0:128].rearrange("s (h t) -> s h t", h=2)
                for hi, h in enumerate(hh):
                    P.op("pe", lambda e, hi=hi, h=h: e.matmul(vu[:, hi, :], lhsT=TT[:, h, :], rhs=Zs[:, h, :], start=True, stop=True),
                         reads=[ttk, K_("Zs")], writes=[buk], inc=(hi == 1))
                P.op("act", lambda e: e.activation(out=Ut[:, hs2], in_=vu, func=AF.Copy), reads=[buk], writes=[K_("Ut")])
                P.op("pool", lambda e: e.tensor_tensor(out=Hd[:, hs2], in0=Hf[:, hs2], in1=PC[:, hs2, c:c + 1].to_broadcast([64, 2, 64]), op=ALU.mult),
                     reads=[K_("Hf"), "PC"], writes=[K_("Hd")])
                yield
                b5, b5k = bank()
                v5 = b5[:, 0:128].rearrange("s (h t) -> s h t", h=2)
                for hi, h in enumerate(hh):
                    P.op("pe", lambda e, hi=hi, h=h: e.matmul(v5[:, hi, :], lhsT=WT[:, h, :], rhs=Hb[:, h, :], start=True, stop=True),
                         reads=[K_("WT"), K_("Hb")], writes=[b5k], inc=(hi == 1))
                P.op("dve", lambda e: e.tensor_tensor(out=Ub[:, hs2], in0=v5, in1=Ut[:, hs2], op=ALU.add), reads=[b5k, K_("Ut")], writes=[K_("Ub")])
                yield
                by, byk = bank()
                vy = by[:, 0:128].rearrange("s (h t) -> s h t", h=2)
                for hi, h in enumerate(hh):
                    P.op("pe", lambda e, hi=hi, h=h: e.matmul(vy[:, hi, :], lhsT=Hb[:, h, :], rhs=RA[:, h, c, 1, :], start=True, stop=False),
                         reads=[K_("Hb"), ("RA", h)], writes=[byk], inc=False)
                    P.op("pe", lambda e, hi=hi, h=h: e.matmul(vy[:, hi, :], lhsT=Ub[:, h, :], rhs=MS[:, h, 0, 1, :], start=False, stop=False),
                         reads=[K_("Ub"), K_("MS")], writes=[byk], inc=False)
                    P.op("pe", lambda e, hi=hi, h=h: e.matmul(vy[:, hi, :], lhsT=TK[:, h, 2, :], rhs=MS[:, h, 1, 1, :], start=False, stop=True),
                         reads=[K_("TK"), K_("MS")], writes=[byk], inc=(hi == 1))
                bh, bhk = bank()
                vh = bh[:, 0:128].rearrange("s (h t) -> s h t", h=2)
                for hi, h in enumerate(hh):
                    P.op("pe", lambda e, hi=hi, h=h: e.matmul(vh[:, hi, :], lhsT=TK[:, h, 1, :], rhs=Ub[:, h, :], start=True, stop=False),
                         reads=[K_("TK"), K_("Ub")], writes=[bhk], inc=False)
                    P.op("pe", lambda e, hi=hi, h=h: e.matmul(vh[:, hi, :], lhsT=TK[:, h, 0, :], rhs=TK[:, h, 2, :], start=False, stop=True),
                         reads=[K_("TK")], writes=[bhk], inc=(hi == 1))
                P.op("dve", lambda e: e.tensor_tensor(out=Hb[:, hs2], in0=vh, in1=Hd[:, hs2], op=ALU.add), reads=[bhk, K_("Hd")], writes=[K_("Hb")])
                P.op("dve", lambda e: e.tensor_tensor(out=Hf[:, hs2], in0=vh, in1=Hd[:, hs2], op=ALU.add), reads=[bhk, K_("Hd")], writes=[K_("Hf")])
                P.op("act", lambda e: e.activation(out=yT[:, hs2, c * 64:(c + 1) * 64], in_=vy, func=AF.Copy), reads=[byk], writes=[K_("yT")])
                yield

            for c in range(8):
                w_steps(per_chunk)
                gens = [chunk_group(c, 0), chunk_group(c, 1)]
                live = [True, True]
                while any(live):
                    for gi_, gn in enumerate(gens):
                        if live[gi_]:
                            try:
                                next(gn)
                            except StopIteration:
                                live[gi_] = False

        def output_head(h):
            y_ = yT[:, h, :]
            P.op("pe", lambda e, y_=y_: e.matmul(paux[:], lhsT=cst[:, C_AVG:C_AVG + 64], rhs=y_, start=True, stop=True),
                 reads=["cst", ("yT", 0), ("yT", 1)], writes=[("pg", 0)])
            P.op("dve", lambda e, y_=y_: e.tensor_tensor(out=T["yc"][:], in0=y_, in1=paux[:], op=ALU.subtract), reads=[("yT", 0), ("yT", 1), ("pg", 0)], writes=["t_rk"])
            P.op("act", lambda e: e.activation(out=T["sq"][:], in_=T["yc"][:], func=AF.Square), reads=["t_rk"], writes=["t_sq"])
            P.op("pe", lambda e: e.matmul(paux[:], lhsT=cst[:, C_AVG:C_AVG + 64], rhs=T["sq"][:], start=True, stop=True),
                 reads=["cst", "t_sq"], writes=[("pg", 0)])
            P.op("act", lambda e: e.activation(out=T["n"][:], in_=paux[:], func=AF.Sqrt, bias=GN_EPS), reads=[("pg", 0)], writes=["t_n"])
            P.op("dve", lambda e: e.reciprocal(out=T["n"][:], in_=T["n"][:]), reads=["t_n"], writes=["t_n"])
            P.op("pool", lambda e: e.tensor_tensor(out=T["yc"][:], in0=T["yc"][:], in1=T["n"][:], op=ALU.mult), reads=["t_rk", "t_n"], writes=["t_rk"])
            P.op("pool", lambda e, h=h: e.tensor_scalar(out=T["yc"][:], in0=T["yc"][:], scalar1=vcol(V_LNW + h), scalar2=vcol(V_LNB + h),
                                                     op0=ALU.mult, op1=ALU.add), reads=["t_rk", "vec"], writes=["t_rk"])
            P.op("pool", lambda e, h=h: e.tensor_tensor(out=T["yc"][:], in0=T["yc"][:], in1=bon[:, h, :], op=ALU.add), reads=["t_rk", ("bon", h)], writes=["t_rk"])
            P.op("dve", lambda e, h=h: e.tensor_tensor(out=mo[:, h, :], in0=T["yc"][:], in1=gT[:, h, :], op=ALU.mult),
                 reads=["t_rk", ("gT", h)], writes=["mo"])
        def store_mo(t):
            P.dma("sp", io["mix_loc"][t // 2, 256:512, (t % 2) * 512:(t % 2) * 512 + 512].rearrange("(h i) t -> i h t", h=4), mo[:], "rmo",
                  reads=["mo"], writes=[("mixslab", t // 2)])
        return proj_group, lora_acts, preproc, chunks, output_head, store_mo

    proj_group, lora_acts, preproc, chunks, output_head, store_mo = make_tile_fns()

    def gather_slab(j):
        P.custom("pool", lambda e, j=j: e.collective_compute(
            "AllGather", ALU.bypass, replica_groups=[[0, 1, 2, 3], [4, 5, 6, 7]],
            ins=[io["mix_loc"][j].opt()], outs=[io["mix_gath"][j].opt()]),
            key="cc", amt=1, reads=[("mixslab", j)], writes=[])

    witems = []
    for (src, dst, K, N) in (wspecs or []):
        nkc = K // 128
        for kg in range((nkc + 15) // 16):
            nk = min(16, nkc - kg * 16)
            for nb in range(N // 512):
                tix = wtile_index(N, kg, nb)
                for h0 in range(0, nk, 2):
                    r0 = (kg * 16 + h0) * 128
                    witems.append((src[r0:r0 + 256, nb * 512:(nb + 1) * 512].rearrange("(kc p) c -> p kc c", p=128),
                                   dst[tix].rearrange("p (kc c) -> p kc c", c=512)[:, h0:h0 + 2, :]))
    wst_ = {"i": 0}

    def w_load(i):
        if i < len(witems):
            P.dma("sp", wstg[:, i % 2], witems[i][0], ("wl", i % 2), writes=[("wstg", i % 2)])

    def w_steps(n):
        for _ in range(n):
            i = wst_["i"]
            if i >= len(witems):
                return
            if i == 0:
                w_load(0)
            w_load(i + 1)
            sl = i % 2
            P.op("pool", lambda e, sl=sl: e.tensor_copy(out=wob[:, sl], in_=wstg[:, sl]), reads=[("wstg", sl)], writes=[("wob", sl)])
            P.dma("sp", witems[i][1], wob[:, sl], ("ws", sl), reads=[("wob", sl)])
            wst_["i"] += 1
    per_chunk = -(-len(witems) // max(1, (ntile * 8 - 8))) if witems else 0
    for t in range(ntile):
        P.dma("sp", uT[:], io["uT_scr"][t].rearrange("p (kc t) -> p kc t", t=512), "utl", writes=["uT"])
        for gi in (12, 13, 14, 15):
            proj_group(gi)
        lora_acts()
        for h in range(4):
            if t > 0:
                output_head(h)
            for gi in (h, 4 + h, 8 + h):
                proj_group(gi)
            if h >= 1:
                preproc(h - 1)
        if t > 0:
            store_mo(t - 1)
        preproc(3)
        if gather and t > 0 and (t - 1) % 2 == 1:
            gather_slab((t - 1) // 2)
        chunks()
    for h in range(4):
        output_head(h)
    store_mo(ntile - 1)
    if gather:
        gather_slab((ntile - 1) // 2)
    w_steps(len(witems))
    P.flush()
    A.close()


def phase_r2(P, nc, io, ntile=SEQ // 512, wspecs=None, gather=False):
    A = Alloc(nc)
    UW = 256
    NCU = UW // 64
    nunit = ntile * 2
    P0 = P
    DS = DECAY_SCALE
    wR = A.sb("r_wR", [128, 16, RW_COLS], BF16)
    wstg = A.sb("r_wstg", [128, 2, 2, 512], F32)
    wob = A.sb("r_wob", [128, 2, 2, 512], BF16)
    wst = [wstg[:].rearrange("p a b c -> p (a b c)")[:, 0:RW_COLS]]
    uT2 = [A.sb("r_uT", [128, 16, 512], BF16)] * 2
    vec = A.sb("r_vec", [128, 64], F32)
    cst = A.sb("r_cst", [64, 1024], F32)
    wupb = A.sb("r_wupb", [64, 256], BF16)
    aupb = A.sb("r_aupb", [64, 256], BF16)
    gupb0 = A.sb("r_gupb0", [128, 256], BF16)
    gupb1 = A.sb("r_gupb1", [32, 256], BF16)
    Ib = A.sb("r_Ib", [64, 64], BF16)
    mask4 = A.sb("r_mask4", [64, 2, 2, 64], F32)
    zf = [A.sb(f"r_zf{i}", [128, 516], F32) for i in range(2)]
    tmpd = [A.sb(f"r_tmpd{i}", [128, 512], F32) for i in range(2)]
    cr = A.sb("r_cr", [128, 16], F32)
    zs_rkv2 = [A.sb("r_zs%d" % i, [64, 12, UW], F32) for i in range(2)]
    zs_w2 = [A.sb("r_zsw%d" % i, [64, UW], F32) for i in range(2)]
    zs_a2 = [A.sb("r_zsa%d" % i, [64, UW], F32) for i in range(2)]
    zs_g02 = [A.sb("r_zsg0%d" % i, [128, UW], F32) for i in range(2)]
    zs_g12 = [A.sb("r_zsg1%d" % i, [32, UW], F32) for i in range(2)]
    lob_w2 = [A.sb("r_lobw%d" % i, [64, UW], BF16) for i in range(2)]
    lob_a2 = [A.sb("r_loba%d" % i, [64, UW], BF16) for i in range(2)]
    gsb02 = [A.sb("r_gsb0%d" % i, [128, UW], BF16) for i in range(2)]
    gsb12 = [A.sb("r_gsb1%d" % i, [32, UW], BF16) for i in range(2)]
    tn = ["cs", "ei", "ev", "ex", "er", "sq", "n", "kk", "km", "bv", "rk"]
    T0 = {n: A.sb(f"r_t_{n}", [64, UW], F32) for n in tn}
    To = {n: A.sb(f"r_o_{n}", [64, UW], F32) for n in ("sq", "n", "yc")}
    RA2 = [A.sb("r_RA%d" % i, [64, 4, NCU, 2, 64], BF16) for i in range(2)]
    KB2 = [A.sb("r_KB%d" % i, [64, 4, NCU, 2, 64], BF16) for i in range(2)]
    TR2 = [A.sb("r_TR%d" % i, [64, 4, NCU, 4, 64], BF16) for i in range(2)]
    PC2 = [A.sb("r_PC%d" % i, [64, 4, NCU], F32) for i in range(2)]
    bon2 = [A.sb("r_bon%d" % i, [64, 4, UW], F32) for i in range(2)]
    gT2 = [A.sb("r_gT%d" % i, [64, 4, UW], F32) for i in range(2)]
    yT2 = [A.sb("r_yT%d" % i, [64, 4, UW], F32) for i in range(2)]
    mo2 = [A.sb("r_mo%d" % i, [64, 4, UW], BF16) for i in range(2)]
    TK = A.sb("r_TK", [64, 4, 4, 64], BF16)
    MS = A.sb("r_MS", [64, 4, 2, 2, 64], BF16)
    Lb = [A.sb(f"r_L{i}", [64, 4, 64], BF16) for i in range(2)]
    Nb = [A.sb(f"r_N{i}", [64, 4, 64], BF16) for i in range(2)]
    Pb = [A.sb(f"r_P{i}", [64, 4, 64], BF16) for i in range(2)]
    ILb = A.sb("r_IL", [64, 4, 64], BF16)
    WT = A.sb("r_WT", [64, 4, 64], BF16)
    Zs = A.sb("r_Zs", [64, 4, 64], BF16)
    Ut = A.sb("r_Ut", [64, 4, 64], F32)
    Ub = A.sb("r_Ub", [64, 4, 64], BF16)
    Hf = A.sb("r_Hf", [64, 4, 64], F32)
    Hd = A.sb("r_Hd", [64, 4, 64], F32)
    Hb = A.sb("r_Hb", [64, 4, 64], BF16)
    pproj = [A.ps(f"r_pp{i}", [128, 512], F32) for i in range(2)]
    ptr1 = A.ps("r_ptr", [64, 2, 2, 4, 64], BF16)
    ptrs = [ptr1[:, 0], ptr1[:, 1]]
    pgen = [A.ps(f"r_pg{i}", [64, 512], F32) for i in range(5)]
    paux = pgen[0]
    banks = [(pgen[i], ("pg", i)) for i in range(1, 5)]
    bctr = {"b": 0}

    def bank():
        b_ = banks[bctr["b"] % len(banks)]
        bctr["b"] += 1
        return b_

    P.dma("sp", vec[:], io["rw_vec"], "c0", writes=["vec"])
    P.dma("sp", cst[:], io["rw_const"], "c1", writes=["cst"])
    P.dma("sp", tmpd[0][:], io["rw_mats"][:, 0:512], "c2", writes=[("tmpd", 0)])
    P.dma("sp", tmpd[1][:], io["rw_mats"][:, 512:1024], "c3", writes=[("tmpd", 1)])
    P.op("dve", lambda e: e.tensor_copy(out=wupb[:], in_=tmpd[0][0:64, 0:256]), reads=[("tmpd", 0)], writes=["wupb"])
    P.op("dve", lambda e: e.tensor_copy(out=aupb[:], in_=tmpd[0][0:64, 256:512]), reads=[("tmpd", 0)], writes=["aupb"])
    P.op("dve", lambda e: e.tensor_copy(out=gupb0[:], in_=tmpd[1][:, 0:256]), reads=[("tmpd", 1)], writes=["gupb0"])
    P.op("dve", lambda e: e.tensor_copy(out=gupb1[:], in_=tmpd[1][0:32, 256:512]), reads=[("tmpd", 1)], writes=["gupb1"])
    P.op("dve", lambda e: e.tensor_copy(out=Ib[:], in_=cst[:, C_ID:C_ID + 64]), reads=["cst"], writes=["Ib"])
    for a in range(2):
        P.op("dve", lambda e, a=a: e.tensor_copy(out=mask4[:, a, 0, :], in_=cst[:, C_SU:C_SU + 64]), reads=["cst"], writes=["mask4"])
        P.op("dve", lambda e, a=a: e.tensor_copy(out=mask4[:, a, 1, :], in_=cst[:, C_IU:C_IU + 64]), reads=["cst"], writes=["mask4"])
    P.op("dve", lambda e: e.tensor_scalar(out=vec[:, V_OMKA:V_OMKA + 4], in0=vec[:, V_KA:V_KA + 4], scalar1=-1.0, scalar2=1.0, op0=ALU.mult, op1=ALU.add),
         reads=["vec"], writes=["vec"])
    for kc in range(16):
        s = 0
        P.dma("sp", wst[s], io["w_in_r"][kc * 128:(kc + 1) * 128, :], ("rwl", s), writes=[("wstg", 0), ("wstg", 1)])
        P.op("pool" if kc % 2 else "dve", lambda e, s=s, kc=kc: e.tensor_copy(out=wR[:, kc, :], in_=wst[s]),
             reads=[("wstg", 0), ("wstg", 1)], writes=["wR"])
    P.op("pool", lambda e: e.memset(cr[:], 0.0), writes=["cr"])
    P.op("pool", lambda e: e.memset(Hf[:], 0.0), writes=[("Hf", 0), ("Hf", 1)])
    P.op("pool", lambda e: e.memset(Hb[:], 0.0), writes=[("Hb", 0), ("Hb", 1)])

    def mk_groups(zs_rkv, zs_w, zs_a, zs_g0, zs_g1):
        groups = []
        for qi, mv in enumerate((V_MIX_R, V_MIX_K, V_MIX_V)):
            for h in range(4):
                groups.append((qi * 256 + h * 64, 64, zs_rkv[:, qi * 4 + h, :], mv + h, ("zs", qi, h)))
        groups.append((768, 64, zs_w[:], V_MIX_W, "zsw"))
        groups.append((832, 64, zs_a[:], V_MIX_A, "zsa"))
        groups.append((896, 128, zs_g0[:], V_MIX_G0, "zsg0"))
        groups.append((1024, 32, zs_g1[:], V_MIX_G1, "zsg1"))
        return groups

    ctr = {"pp": 0, "z": 0}

    def vcol(c, m=64):
        return vec[0:m, c:c + 1]

    PZ_NAMES = {"zs", "zsw", "zsa", "zsg0", "zsg1", "lobw", "loba", "gsb0", "gsb1", "RA", "KB", "TR", "PC", "bon", "gT", "yT", "mo"}

    class PX:
        def __init__(self, pz):
            self.pz = pz
            self.buf = None

        def _k(self, k):
            n = k[0] if isinstance(k, tuple) else k
            return (k, "pz", self.pz) if n in PZ_NAMES else k

        def op(self, eng, fn, reads=(), writes=(), **kw):
            rd, wr = [self._k(k) for k in reads], [self._k(k) for k in writes]
            if self.buf is not None:
                self.buf.append(lambda: P0.op(eng, fn, reads=rd, writes=wr, **kw))
                return None
            return P0.op(eng, fn, reads=rd, writes=wr, **kw)

        def dma(self, q, out, in_, slot, reads=(), writes=(), **kw):
            rd, wr = [self._k(k) for k in reads], [self._k(k) for k in writes]
            if self.buf is not None:
                self.buf.append(lambda: P0.dma(q, out, in_, slot, reads=rd, writes=wr, **kw))
                return None
            return P0.dma(q, out, in_, slot, reads=rd, writes=wr, **kw)

    def make_tile_fns(pz):
        P = PX(pz)
        zs_rkv, zs_w, zs_a, zs_g0, zs_g1 = zs_rkv2[pz], zs_w2[pz], zs_a2[pz], zs_g02[pz], zs_g12[pz]
        lob_w, lob_a, gsb0, gsb1 = lob_w2[pz], lob_a2[pz], gsb02[pz], gsb12[pz]
        RA, KB, TR, PC, bon, gT, yT, mo = RA2[pz], KB2[pz], TR2[pz], PC2[pz], bon2[pz], gT2[pz], yT2[pz], mo2[pz]
        T = dict(T0)
        T["sw"] = zs_w
        T["as"] = zs_a
        groups = mk_groups(zs_rkv, zs_w, zs_a, zs_g0, zs_g1)
        uts = {"ap": None}
        def proj_group(gi):
            (c0, M, dst, mcol, zkey) = groups[gi]
            pp = pproj[ctr["pp"] % 2]
            ppk = ("pp", ctr["pp"] % 2)
            ctr["pp"] += 1
            for kc in range(16):
                P.op("pe", lambda e, pp=pp, M=M, c0=c0, kc=kc, ua=uts["ap"]: e.matmul(pp[0:M, 0:UW], lhsT=wR[:, kc, c0:c0 + M], rhs=ua[:, kc, :],
                                                                     start=(kc == 0), stop=(kc == 15)),
                     reads=["wR", uts["key"]], writes=[ppk], inc=(kc == 15))
            zi = ctr["z"] % 2
            ctr["z"] += 1
            z = zf[zi]
            zk = ("zf", zi)
            td = tmpd[zi]
            tdk = ("tmpd", zi)
            P.op("act", lambda e, z=z, pp=pp, M=M: e.activation(out=z[0:M, 1:UW + 1], in_=pp[0:M, 0:UW], func=AF.Copy), reads=[ppk], writes=[zk])
            P.op("pool", lambda e, z=z, M=M, gi=gi: e.tensor_copy(out=z[0:M, 0:1], in_=cr[0:M, gi:gi + 1]), reads=["cr", zk], writes=[zk])
            P.op("dve", lambda e, z=z, td=td, pp=pp, M=M: e.tensor_tensor(out=td[0:M, 0:UW], in0=z[0:M, 0:UW], in1=pp[0:M, 0:UW], op=ALU.subtract),
                 reads=[zk, ppk], writes=[tdk])
            P.op("dve", lambda e, td=td, pp=pp, M=M, dst=dst, mcol=mcol: e.scalar_tensor_tensor(
                out=dst, in0=td[0:M, 0:UW], scalar=vec[0:M, mcol:mcol + 1], in1=pp[0:M, 0:UW], op0=ALU.mult, op1=ALU.add),
                reads=[tdk, ppk, "vec"], writes=[zkey])
            P.op("pool", lambda e, z=z, M=M, gi=gi: e.tensor_copy(out=cr[0:M, gi:gi + 1], in_=z[0:M, UW:UW + 1]), reads=[zk], writes=["cr"])
        def lora_acts():
            P.op("act", lambda e: e.activation(out=lob_w[:], in_=zs_w[:], func=AF.Tanh), reads=["zsw"], writes=["lobw"])
            P.op("pool", lambda e: e.tensor_copy(out=lob_a[:], in_=zs_a[:]), reads=["zsa"], writes=["loba"])
            P.op("act", lambda e: e.activation(out=gsb0[:], in_=zs_g0[:], func=AF.Sigmoid), reads=["zsg0"], writes=["gsb0"])
            P.op("act", lambda e: e.activation(out=gsb1[:], in_=zs_g1[:], func=AF.Sigmoid), reads=["zsg1"], writes=["gsb1"])
        def preproc(h):
            r_ = zs_rkv[:, h, :]
            k_ = zs_rkv[:, 4 + h, :]
            v_ = zs_rkv[:, 8 + h, :]
            rk_, kk_, vk_ = ("zs", 0, h), ("zs", 1, h), ("zs", 2, h)
            hs = slice(h * 64, (h + 1) * 64)
            P.op("pe", lambda e, hs=hs: e.matmul(paux[:, 0:UW], lhsT=wupb[:, hs], rhs=lob_w[:], start=True, stop=True),
                 reads=["wupb", "lobw"], writes=[("pg", 0)])
            P.op("act", lambda e, h=h: e.activation(out=T["sw"][:], in_=paux[:, 0:UW], func=AF.Sigmoid, bias=vcol(V_W0 + h)),
                 reads=[("pg", 0), "vec"], writes=["zsw"])
            P.op("dve", lambda e: e.tensor_tensor_scan(out=T["cs"][:], data0=cst[:, C_RST:C_RST + UW], data1=T["sw"][:], initial=0.0,
                                                       op0=ALU.mult, op1=ALU.add), reads=["zsw", "cst"], writes=["t_cs"])
            P.op("act", lambda e: e.activation(out=T["ei"][:], in_=T["cs"][:], func=AF.Exp, scale=-DS), reads=["t_cs"], writes=["t_ei"])
            P.op("act", lambda e: e.activation(out=T["ev"][:], in_=T["cs"][:], func=AF.Exp, scale=DS), reads=["t_cs"], writes=["t_ev"])
            P.op("pool", lambda e: e.tensor_tensor(out=T["ex"][:], in0=T["cs"][:], in1=T["sw"][:], op=ALU.subtract),
                 reads=["t_cs", "zsw"], writes=["t_ex"])
            P.op("act", lambda e: e.activation(out=T["ex"][:], in_=T["ex"][:], func=AF.Exp, scale=-DS), reads=["t_ex"], writes=["t_ex"])
            P.op("pool", lambda e, h=h: e.tensor_copy(out=PC[:, h, :], in_=T["ei"][:, 63:UW:64]), reads=["t_ei"], writes=["PC"])
            P.op("dve", lambda e: e.tensor_tensor(out=T["er"][:].rearrange("j (c s) -> j c s", s=64),
                                                  in0=T["ev"][:].rearrange("j (c s) -> j c s", s=64),
                                                  in1=T["ei"][:].rearrange("j (c s) -> j c s", s=64)[:, :, 63:64].to_broadcast([64, NCU, 64]),
                                                  op=ALU.mult), reads=["t_ev", "t_ei"], writes=["t_er"])
            P.op("pe", lambda e, hs=hs: e.matmul(paux[:, 0:UW], lhsT=aupb[:, hs], rhs=lob_a[:], start=True, stop=True),
                 reads=["aupb", "loba"], writes=[("pg", 0)])
            P.op("act", lambda e, h=h: e.activation(out=T["as"][:], in_=paux[:, 0:UW], func=AF.Sigmoid, bias=vcol(V_A0 + h)),
                 reads=[("pg", 0), "vec"], writes=["zsa"])
            P.op("act", lambda e, k_=k_, h=h: e.activation(out=T["sq"][:], in_=k_, func=AF.Square, scale=vcol(V_KK + h)),
                 reads=[kk_, "vec"], writes=["t_sq"])
            P.op("pe", lambda e: e.matmul(paux[:, 0:UW], lhsT=cst[:, C_ONE:C_ONE + 64], rhs=T["sq"][:], start=True, stop=True),
                 reads=["cst", "t_sq"], writes=[("pg", 0)])
            P.op("act", lambda e: e.activation(out=T["n"][:], in_=paux[:, 0:UW], func=AF.Sqrt), reads=[("pg", 0)], writes=["t_n"])
            P.op("dve", lambda e: e.tensor_scalar(out=T["n"][:], in0=T["n"][:], scalar1=1e-12, scalar2=None, op0=ALU.max), reads=["t_n"], writes=["t_n"])
            P.op("dve", lambda e: e.reciprocal(out=T["n"][:], in_=T["n"][:]), reads=["t_n"], writes=["t_n"])
            P.op("dve", lambda e, k_=k_, h=h: e.scalar_tensor_tensor(out=T["kk"][:], in0=k_, scalar=vcol(V_KK + h), in1=T["n"][:],
                                                                   op0=ALU.mult, op1=ALU.mult), reads=[kk_, "vec", "t_n"], writes=["t_kk"])
            P.op("pool", lambda e, h=h: e.tensor_scalar(out=T["km"][:], in0=T["as"][:], scalar1=vcol(V_KA + h), scalar2=vcol(V_OMKA + h),
                                                     op0=ALU.mult, op1=ALU.add), reads=["zsa", "vec"], writes=["t_km"])
            P.op("pool", lambda e, k_=k_: e.tensor_tensor(out=T["km"][:], in0=T["km"][:], in1=k_, op=ALU.mult), reads=["t_km", kk_], writes=["t_km"])
            P.op("pool", lambda e: e.tensor_tensor(out=T["bv"][:], in0=T["kk"][:], in1=T["as"][:], op=ALU.mult), reads=["t_kk", "zsa"], writes=["t_bv"])

            def c3(ap):
                return ap.rearrange("j (c s) -> j c s", s=64)
            P.op("dve", lambda e, h=h: e.scalar_tensor_tensor(out=RA[:, h, :, 0, :], in0=c3(T["kk"][:]), scalar=-1.0, in1=c3(T["ex"][:]),
                                                            op0=ALU.mult, op1=ALU.mult), reads=["t_kk", "t_ex"], writes=[("RA", h)])
            P.op("pool", lambda e, h=h, r_=r_: e.tensor_tensor(out=RA[:, h, :, 1, :], in0=c3(r_), in1=c3(T["ei"][:]), op=ALU.mult),
                 reads=[rk_, "t_ei"], writes=[("RA", h)])
            P.op("dve", lambda e, h=h: e.tensor_tensor(out=KB[:, h, :, 0, :], in0=c3(T["km"][:]), in1=c3(T["ev"][:]), op=ALU.mult),
                 reads=["t_km", "t_ev"], writes=[("KB", h)])
            P.op("pool", lambda e, h=h: e.tensor_tensor(out=KB[:, h, :, 1, :], in0=c3(T["bv"][:]), in1=c3(T["ev"][:]), op=ALU.mult),
                 reads=["t_bv", "t_ev"], writes=[("KB", h)])
            P.op("dve", lambda e, h=h: e.tensor_tensor(out=TR[:, h, :, 0, :], in0=c3(T["km"][:]), in1=c3(T["er"][:]), op=ALU.mult),
                 reads=["t_km", "t_er"], writes=[("TR", h)])
            P.op("pool", lambda e, h=h: e.tensor_tensor(out=TR[:, h, :, 1, :], in0=c3(T["bv"][:]), in1=c3(T["er"][:]), op=ALU.mult),
                 reads=["t_bv", "t_er"], writes=[("TR", h)])
            P.op("act", lambda e, h=h, v_=v_: e.activation(out=TR[:, h, :, 2, :], in_=c3(v_), func=AF.Copy), reads=[vk_], writes=[("TR", h)])
            P.op("pool", lambda e, h=h: e.tensor_copy(out=TR[:, h, :, 3, :], in_=RA[:, h, :, 0, :]), reads=[("RA", h)], writes=[("TR", h)])
            P.op("dve", lambda e, h=h, r_=r_: e.scalar_tensor_tensor(out=T["rk"][:], in0=r_, scalar=vcol(V_RK + h), in1=T["km"][:],
                                                                   op0=ALU.mult, op1=ALU.mult), reads=[rk_, "vec", "t_km"], writes=["t_rk"])
            P.op("pe", lambda e: e.matmul(paux[:, 0:UW], lhsT=cst[:, C_ONE:C_ONE + 64], rhs=T["rk"][:], start=True, stop=True),
                 reads=["cst", "t_rk"], writes=[("pg", 0)])
            P.op("dve", lambda e, h=h, v_=v_: e.tensor_tensor(out=bon[:, h, :], in0=paux[:, 0:UW], in1=v_, op=ALU.mult), reads=[("pg", 0), vk_], writes=[("bon", h)])
            P.op("pe", lambda e, hs=hs: e.matmul(paux[:, 0:UW], lhsT=gupb0[:, hs], rhs=gsb0[:], start=True, stop=False),
                 reads=["gupb0", "gsb0"], writes=[("pg", 0)], inc=False)
            P.op("pe", lambda e, hs=hs: e.matmul(paux[:, 0:UW], lhsT=gupb1[:, hs], rhs=gsb1[:], start=False, stop=True),
                 reads=["gupb1", "gsb1"], writes=[("pg", 0)])
            P.op("act", lambda e, h=h: e.activation(out=gT[:, h, :], in_=paux[:, 0:UW], func=AF.Copy), reads=[("pg", 0)], writes=[("gT", h)])

        def chunks():
            def chunk_group(c, g):
                hh = (2 * g, 2 * g + 1)
                hs2 = slice(2 * g, 2 * g + 2)
                K_ = lambda n, *a: (n, g) + a
                pt = ptrs[g]
                ptk = ("ptr", 0)
                for hi, h in enumerate(hh):
                    for q in range(4):
                        P.op("pe", lambda e, hi=hi, h=h, q=q: e.transpose(out=pt[:, hi, q, :], in_=TR[:, h, c, q, :], identity=Ib[:]),
                             reads=[("TR", h), "Ib"], writes=[ptk], inc=(hi == 1 and q == 3))
                P.op("act", lambda e: e.activation(out=TK[:, hs2], in_=pt[:], func=AF.Copy), reads=[ptk], writes=[K_("TK")])
                yield
                b1, b1k = bank()
                v1 = b1[:, 0:512].rearrange("s (h a x) -> s h a x", h=2, a=2)
                for hi, h in enumerate(hh):
                    rhs = RA[:, h, c, :, :].rearrange("j a t -> j (a t)")
                    P.op("pe", lambda e, hi=hi, h=h, rhs=rhs: e.matmul(v1[:, hi, 0, :], lhsT=KB[:, h, c, 1, :], rhs=rhs, start=True, stop=True),
                         reads=[("KB", h), ("RA", h)], writes=[b1k], inc=False)
                    P.op("pe", lambda e, hi=hi, h=h, rhs=rhs: e.matmul(v1[:, hi, 1, :], lhsT=KB[:, h, c, 0, :], rhs=rhs, start=True, stop=True),
                         reads=[("KB", h), ("RA", h)], writes=[b1k], inc=(hi == 1))
                b2, b2k = bank()
                v2 = b2[:, 0:128].rearrange("s (h t) -> s h t", h=2)
                for hi, h in enumerate(hh):
                    P.op("pe", lambda e, hi=hi, h=h: e.matmul(v2[:, hi, :], lhsT=RA[:, h, c, 0, :], rhs=KB[:, h, c, 1, :], start=True, stop=True),
                         reads=[("KB", h), ("RA", h)], writes=[b2k], inc=(hi == 1))
                P.op("dve", lambda e: e.tensor_tensor(
                    out=MS[:, hs2].rearrange("s h a b t -> s h (a b t)"), in0=v1.rearrange("s h a x -> s h (a x)"),
                    in1=mask4[:].rearrange("s a b t -> s (a b t)").unsqueeze(1).to_broadcast([64, 2, 256]), op=ALU.mult),
                    reads=[b1k, "mask4"], writes=[K_("MS")])
                P.op("dve", lambda e: e.tensor_tensor(out=Lb[0][:, hs2], in0=v2, in1=cst[:, C_SL:C_SL + 64].unsqueeze(1).to_broadcast([64, 2, 64]),
                                                      op=ALU.mult), reads=[b2k, "cst"], writes=[K_("L", 0)])
                P.op("pool", lambda e: e.tensor_tensor(out=Pb[0][:, hs2], in0=MS[:, hs2, 0, 0, :], in1=cst[:, C_ID:C_ID + 64].unsqueeze(1).to_broadcast([64, 2, 64]),
                                                       op=ALU.add), reads=[K_("MS"), "cst"], writes=[K_("P", 0)])
                yield
                for k in range(5):
                    li, lo = k % 2, (k + 1) % 2
                    Nk = (lambda h: MS[:, h, 0, 0, :]) if k == 0 else (lambda h, li=li: Nb[li][:, h, :])
                    nkey = K_("MS") if k == 0 else K_("N", li)
                    bl, blk = bank()
                    vl = bl[:, 0:128].rearrange("s (h t) -> s h t", h=2)
                    for hi, h in enumerate(hh):
                        P.op("pe", lambda e, hi=hi, h=h, Nk=Nk, li=li, vl=vl: e.matmul(vl[:, hi, :], lhsT=Nk(h), rhs=Lb[li][:, h, :], start=True, stop=True),
                             reads=[nkey, K_("L", li)], writes=[blk], inc=(hi == 1))
                    if k < 4:
                        bn, bnk = bank()
                        vn = bn[:, 0:128].rearrange("s (h t) -> s h t", h=2)
                        for hi, h in enumerate(hh):
                            P.op("pe", lambda e, hi=hi, h=h, Nk=Nk, li=li, vn=vn: e.matmul(vn[:, hi, :], lhsT=Lb[li][:, h, :], rhs=Nk(h), start=True, stop=True),
                                 reads=[nkey, K_("L", li)], writes=[bnk], inc=(hi == 1))
                    P.op("act", lambda e, lo=lo, vl=vl: e.activation(out=Lb[lo][:, hs2], in_=vl, func=AF.Copy), reads=[blk], writes=[K_("L", lo)])
                    if k < 4:
                        P.op("dve", lambda e, lo=lo, vn=vn: e.tensor_copy(out=Nb[lo][:, hs2], in_=vn), reads=[bnk], writes=[K_("N", lo)])
                    yield
                    bp, bpk = bank()
                    vp = bp[:, 0:128].rearrange("s (h t) -> s h t", h=2)
                    for hi, h in enumerate(hh):
                        P.op("pe", lambda e, hi=hi, h=h, li=li, lo=lo, vp=vp: e.matmul(vp[:, hi, :], lhsT=Lb[lo][:, h, :], rhs=Pb[li][:, h, :], start=True, stop=False),
                             reads=[K_("L", lo), K_("P", li)], writes=[bpk], inc=False)
                        P.op("pe", lambda e, hi=hi, h=h, li=li, vp=vp: e.matmul(vp[:, hi, :], lhsT=Ib[:], rhs=Pb[li][:, h, :], start=False, stop=True),
                             reads=["Ib", K_("P", li)], writes=[bpk], inc=(hi == 1))
                    if k % 2 == 0:
                        P.op("dve", lambda e, lo=lo, vp=vp: e.tensor_copy(out=Pb[lo][:, hs2], in_=vp), reads=[bpk], writes=[K_("P", lo)])
                    else:
                        P.op("act", lambda e, lo=lo, vp=vp: e.activation(out=Pb[lo][:, hs2], in_=vp, func=AF.Copy), reads=[bpk], writes=[K_("P", lo)])
                    yield
                TT = Pb[1]
                ttk = K_("P", 1)
                bw, bwk = bank()
                vw = bw[:, 0:128].rearrange("s (h t) -> s h t", h=2)
                for hi, h in enumerate(hh):
                    P.op("pe", lambda e, hi=hi, h=h: e.matmul(vw[:, hi, :], lhsT=TK[:, h, 3, :], rhs=TT[:, h, :], start=True, stop=True),
                         reads=[K_("TK"), ttk], writes=[bwk], inc=(hi == 1))
                bz, bzk = bank()
                vz = bz[:, 0:128].rearrange("s (h t) -> s h t", h=2)
                for hi, h in enumerate(hh):
                    P.op("pe", lambda e, hi=hi, h=h: e.matmul(vz[:, hi, :], lhsT=MS[:, h, 1, 0, :], rhs=TK[:, h, 2, :], start=True, stop=True),
                         reads=[K_("MS"), K_("TK")], writes=[bzk], inc=(hi == 1))
                P.op("act", lambda e: e.activation(out=WT[:, hs2], in_=vw, func=AF.Copy), reads=[bwk], writes=[K_("WT")])
                P.op("dve", lambda e: e.tensor_copy(out=Zs[:, hs2], in_=vz), reads=[bzk], writes=[K_("Zs")])
                yield
                bu, buk = bank()
                vu = bu[:, 0:128].rearrange("s (h t) -> s h t", h=2)
                for hi, h in enumerate(hh):
                    P.op("pe", lambda e, hi=hi, h=h: e.matmul(vu[:, hi, :], lhsT=TT[:, h, :], rhs=Zs[:, h, :], start=True, stop=True),
                         reads=[ttk, K_("Zs")], writes=[buk], inc=(hi == 1))
                P.op("act", lambda e: e.activation(out=Ut[:, hs2], in_=vu, func=AF.Copy), reads=[buk], writes=[K_("Ut")])
                P.op("pool", lambda e: e.tensor_tensor(out=Hd[:, hs2], in0=Hf[:, hs2], in1=PC[:, hs2, c:c + 1].to_broadcast([64, 2, 64]), op=ALU.mult),
                     reads=[K_("Hf"), "PC"], writes=[K_("Hd")])
                yield
                b5, b5k = bank()
                v5 = b5[:, 0:128].rearrange("s (h t) -> s h t", h=2)
                for hi, h in enumerate(hh):
                    P.op("pe", lambda e, hi=hi, h=h: e.matmul(v5[:, hi, :], lhsT=WT[:, h, :], rhs=Hb[:, h, :], start=True, stop=True),
                         reads=[K_("WT"), K_("Hb")], writes=[b5k], inc=(hi == 1))
                P.op("dve", lambda e: e.tensor_tensor(out=Ub[:, hs2], in0=v5, in1=Ut[:, hs2], op=ALU.add), reads=[b5k, K_("Ut")], writes=[K_("Ub")])
                yield
                by, byk = bank()
                vy = by[:, 0:128].rearrange("s (h t) -> s h t", h=2)
                for hi, h in enumerate(hh):
                    P.op("pe", lambda e, hi=hi, h=h: e.matmul(vy[:, hi, :], lhsT=Hb[:, h, :], rhs=RA[:, h, c, 1, :], start=True, stop=False),
                         reads=[K_("Hb"), ("RA", h)], writes=[byk], inc=False)
                    P.op("pe", lambda e, hi=hi, h=h: e.matmul(vy[:, hi, :], lhsT=Ub[:, h, :], rhs=MS[:, h, 0, 1, :], start=False, stop=False),
                         reads=[K_("Ub"), K_("MS")], writes=[byk], inc=False)
                    P.op("pe", lambda e, hi=hi, h=h: e.matmul(vy[:, hi, :], lhsT=TK[:, h, 2, :], rhs=MS[:, h, 1, 1, :], start=False, stop=True),
                         reads=[K_("TK"), K_("MS")], writes=[byk], inc=(hi == 1))
                bh, bhk = bank()
                vh = bh[:, 0:128].rearrange("s (h t) -> s h t", h=2)
                for hi, h in enumerate(hh):
                    P.op("pe", lambda e, hi=hi, h=h: e.matmul(vh[:, hi, :], lhsT=TK[:, h, 1, :], rhs=Ub[:, h, :], start=True, stop=False),
                         reads=[K_("TK"), K_("Ub")], writes=[bhk], inc=False)
                    P.op("pe", lambda e, hi=hi, h=h: e.matmul(vh[:, hi, :], lhsT=TK[:, h, 0, :], rhs=TK[:, h, 2, :], start=False, stop=True),
                         reads=[K_("TK")], writes=[bhk], inc=(hi == 1))
                P.op("dve", lambda e: e.tensor_tensor(out=Hb[:, hs2], in0=vh, in1=Hd[:, hs2], op=ALU.add), reads=[bhk, K_("Hd")], writes=[K_("Hb")])
                P.op("dve", lambda e: e.tensor_tensor(out=Hf[:, hs2], in0=vh, in1=Hd[:, hs2], op=ALU.add), reads=[bhk, K_("Hd")], writes=[K_("Hf")])
                P.op("act", lambda e: e.activation(out=yT[:, hs2, c * 64:(c + 1) * 64], in_=vy, func=AF.Copy), reads=[byk], writes=[K_("yT")])
                yield

            return chunk_group

        def output_head(h):
            y_ = yT[:, h, :]
            P.op("pe", lambda e, y_=y_: e.matmul(paux[:, 0:UW], lhsT=cst[:, C_AVG:C_AVG + 64], rhs=y_, start=True, stop=True),
                 reads=["cst", ("yT", 0), ("yT", 1)], writes=[("pg", 0)])
            P.op("dve", lambda e, y_=y_: e.tensor_tensor(out=To["yc"][:], in0=y_, in1=paux[:, 0:UW], op=ALU.subtract), reads=[("yT", 0), ("yT", 1), ("pg", 0)], writes=["o_yc"])
            P.op("act", lambda e: e.activation(out=To["sq"][:], in_=To["yc"][:], func=AF.Square), reads=["o_yc"], writes=["o_sq"])
            P.op("pe", lambda e: e.matmul(paux[:, 0:UW], lhsT=cst[:, C_AVG:C_AVG + 64], rhs=To["sq"][:], start=True, stop=True),
                 reads=["cst", "o_sq"], writes=[("pg", 0)])
            P.op("act", lambda e: e.activation(out=To["n"][:], in_=paux[:, 0:UW], func=AF.Sqrt, bias=GN_EPS), reads=[("pg", 0)], writes=["o_n"])
            P.op("dve", lambda e: e.reciprocal(out=To["n"][:], in_=To["n"][:]), reads=["o_n"], writes=["o_n"])
            P.op("pool", lambda e: e.tensor_tensor(out=To["yc"][:], in0=To["yc"][:], in1=To["n"][:], op=ALU.mult), reads=["o_yc", "o_n"], writes=["o_yc"])
            P.op("pool", lambda e, h=h: e.tensor_scalar(out=To["yc"][:], in0=To["yc"][:], scalar1=vcol(V_LNW + h), scalar2=vcol(V_LNB + h),
                                                     op0=ALU.mult, op1=ALU.add), reads=["o_yc", "vec"], writes=["o_yc"])
            P.op("pool", lambda e, h=h: e.tensor_tensor(out=To["yc"][:], in0=To["yc"][:], in1=bon[:, h, :], op=ALU.add), reads=["o_yc", ("bon", h)], writes=["o_yc"])
            P.op("dve", lambda e, h=h: e.tensor_tensor(out=mo[:, h, :], in0=To["yc"][:], in1=gT[:, h, :], op=ALU.mult),
                 reads=["o_yc", ("gT", h)], writes=["mo"])
        def store_mo(u):
            off = (u % 4) * UW
            P.dma("sp", io["mix_loc"][u // 4, 256:512, off:off + UW].rearrange("(h i) t -> i h t", h=4), mo[:], ("rmo", pz),
                  reads=["mo"], writes=[("mixslab", u // 4)])
        return proj_group, lora_acts, preproc, chunks, output_head, store_mo, uts, P

    F = [make_tile_fns(0), make_tile_fns(1)]

    def gather_slab(j):
        P0.custom("pool", lambda e, j=j: e.collective_compute(
            "AllGather", ALU.bypass, replica_groups=[[0, 1, 2, 3], [4, 5, 6, 7]],
            ins=[io["mix_loc"][j].opt()], outs=[io["mix_gath"][j].opt()]),
            key="cc", amt=1, reads=[("mixslab", j)], writes=[])

    witems = []
    for (src, dst, K, N) in (wspecs or []):
        nkc = K // 128
        for kg in range((nkc + 15) // 16):
            nk = min(16, nkc - kg * 16)
            for nb in range(N // 512):
                tix = wtile_index(N, kg, nb)
                for h0 in range(0, nk, 2):
                    r0 = (kg * 16 + h0) * 128
                    witems.append((src[r0:r0 + 256, nb * 512:(nb + 1) * 512].rearrange("(kc p) c -> p kc c", p=128),
                                   dst[tix].rearrange("p (kc c) -> p kc c", c=512)[:, h0:h0 + 2, :]))
    wst_ = {"i": 0}

    def w_load(i):
        if i < len(witems):
            P.dma("sp", wstg[:, i % 2], witems[i][0], ("wl", i % 2), writes=[("wstg", i % 2)])

    def w_steps(n):
        for _ in range(n):
            i = wst_["i"]
            if i >= len(witems):
                return
            if i == 0:
                w_load(0)
            w_load(i + 1)
            sl = i % 2
            P.op("pool", lambda e, sl=sl: e.tensor_copy(out=wob[:, sl], in_=wstg[:, sl]), reads=[("wstg", sl)], writes=[("wob", sl)])
            P.dma("sp", witems[i][1], wob[:, sl], ("ws", sl), reads=[("wob", sl)])
            wst_["i"] += 1
    per_chunk = -(-len(witems) // max(1, (nunit * NCU - NCU))) if witems else 0

    def load_uT(t):
        P0.dma("sp", uT2[0][:], io["uT_scr"][t].rearrange("p (kc t) -> p kc t", t=512), "utl", writes=["uT"])

    def stage1(u):
        proj_group, lora_acts, preproc, chunks, output_head, store_mo, uts, px = F[u % 2]
        t, half = u // 2, u % 2
        uts["ap"] = uT2[0][:, :, half * UW:(half + 1) * UW]
        uts["key"] = "uT"
        for gi in (12, 13, 14, 15):
            proj_group(gi)
            yield
        lora_acts()
        yield
        for h in range(4):
            for gi in (h, 4 + h, 8 + h):
                proj_group(gi)
                yield
            if h >= 1:
                yield from sliced(px, preproc, h - 1)
        yield from sliced(px, preproc, 3)
        if half == 1 and t + 1 < ntile:
            load_uT(t + 1)

    def sliced(px, fn, *a):
        px.buf = []
        fn(*a)
        ops, px.buf = px.buf, None
        for i, th in enumerate(ops):
            th()
            if i % 3 == 2:
                yield
        yield

    def stage3(u):
        proj_group, lora_acts, preproc, chunks, output_head, store_mo, uts, px = F[u % 2]
        for h in range(4):
            output_head(h)
            yield
        store_mo(u)
        if gather and u % 4 == 3:
            gather_slab(u // 4)
        yield

    def drain(g):
        for _ in g:
            pass

    load_uT(0)
    drain(stage1(0))
    for u in range(nunit):
        chunk_group = F[u % 2][3]()
        side = []
        if u + 1 < nunit:
            side.append(stage1(u + 1))
        if u > 0:
            side.append(stage3(u - 1))
        for c in range(NCU):
            w_steps(per_chunk)
            gens = [chunk_group(c, 0), chunk_group(c, 1)] + side
            live = [True] * len(gens)
            nchunk_live = 2
            while nchunk_live > 0:
                for gi_, gn in enumerate(gens):
                    if live[gi_]:
                        try:
                            next(gn)
                        except StopIteration:
                            live[gi_] = False
                            if gi_ < 2:
                                nchunk_live -= 1
            side = [g for gi_, g in enumerate(gens) if gi_ >= 2 and live[gi_]]
        for g in side:
            drain(g)
    drain(stage3(nunit - 1))
    w_steps(len(witems))
    P.flush()
    A.close()


def phase_x(P, nc, io, flush=True):
    groups = [[0, 1, 2, 3], [4, 5, 6, 7]]
    for j in range(8):
        P.custom("pool", lambda e, j=j: e.collective_compute(
            "AllGather", ALU.bypass, replica_groups=groups,
            ins=[io["mix_loc"][j].opt()], outs=[io["mix_gath"][j].opt()]),
            key="cc", amt=1, reads=[], writes=[])
    if flush:
        P.flush()


def build(cfg):
    nc = bass.Bass("TRN2", target_bir_lowering=False)
    phases = cfg.get("phases", "WARXD")
    io = {}

    def ein(name, shape, dt=F32):
        io[name] = nc.dram_tensor(name, list(shape), dt, kind="ExternalInput").ap()

    ein("xb", [SEQ, D_MODEL])
    ein("xs", [2048, D_MODEL])
    ein("norm1_g", [1, D_MODEL])
    ein("norm2_g", [1, D_MODEL])
    ein("final_g", [1, D_MODEL])
    ein("w_in_a", [D_MODEL, 768])
    ein("w_in_r", [D_MODEL, RW_COLS])
    ein("abias", [128, 3 * 4 * 256])
    ein("amask", [128, 3 * 256])
    ein("ident", [128, 128])
    ein("w_out_p", [D_MODEL, D_MODEL])
    ein("w_gu", [D_MODEL, 2 * FFN])
    ein("w_down", [FFN, D_MODEL])
    ein("rw_vec", [128, 64])
    ein("rw_mats", [128, 1024])
    ein("rw_const", [64, 1024])
    io["out"] = nc.dram_tensor("out", [2048, D_MODEL], F32, kind="ExternalOutput").ap()
    if cfg.get("mix_in"):
        ein("mix_in", [8, 512, 1024], BF16)
    io["uT_scr"] = nc.dram_tensor("uT_scr", [SEQ // 512, 128, 16 * 512], BF16).ap()
    io["wo_scr"] = nc.dram_tensor("wo_scr", [4, 128, 16 * 512], BF16).ap()
    io["wgu_scr"] = nc.dram_tensor("wgu_scr", [22, 128, 16 * 512], BF16).ap()
    io["wd_scr"] = nc.dram_tensor("wd_scr", [12, 128, 16 * 512], BF16).ap()
    if cfg.get("dump_mix"):
        io["mix_loc"] = nc.dram_tensor("mix_loc", [8, 512, 1024], BF16, kind="ExternalOutput").ap()
    else:
        io["mix_loc"] = nc.dram_tensor("mix_loc", [8, 512, 1024], BF16).ap()
    io["mix_gath"] = nc.dram_tensor("mix_gath", [8, 2048, 1024], BF16).ap()

    P = Prog(nc, same_eng_sync=cfg.get("same_eng_sync", True))
    if cfg.get("mix_in"):
        rows = cfg["mix_in"]
        P.dma("sp", io["mix_loc"][:, rows[0]:rows[1], :], io["mix_in"][:, rows[0]:rows[1], :], "mixin")
        P.flush()
    wspecs = [(io["w_out_p"], io["wo_scr"], D_MODEL, D_MODEL),
              (io["w_gu"], io["wgu_scr"], D_MODEL, 2 * FFN),
              (io["w_down"], io["wd_scr"], FFN, D_MODEL)]
    if "A" in phases:
        phase_a(P, nc, io, nsc=cfg.get("nsc", SEQ // 2048))
    w_in_r = ("W" in phases and "R" in phases and cfg.get("nrt", SEQ // 512) == SEQ // 512)
    if "R" in phases:
        (phase_r2 if cfg.get("r2", False) else phase_r)(P, nc, io, ntile=cfg.get("nrt", SEQ // 512), wspecs=wspecs if w_in_r else None,
                gather=(w_in_r and "X" in phases))
    if w_in_r:
        pass
    elif "W" in phases and "X" in phases:
        phase_w(P, nc, wspecs, pre=lambda: phase_x(P, nc, io, flush=False), engs=("dve", "act"))
    else:
        if "W" in phases:
            phase_w(P, nc, wspecs)
        if "X" in phases:
            phase_x(P, nc, io)
    if "D" in phases:
        phase_d(P, nc, io, ntile=cfg.get("ndt", 4))
    P.close()
    return nc


def host_inputs(inputs):
    f32 = np.float32
    x = np.asarray(inputs["x"], f32)
    w_in = np.asarray(inputs["w_in"], f32)[0]
    w_out = np.asarray(inputs["w_out"], f32)[0]
    w_gu = np.ascontiguousarray(np.asarray(inputs["w_gate_up"], f32)[0])
    w_down = np.ascontiguousarray(np.asarray(inputs["w_down"], f32)[0])
    table = np.asarray(inputs["rel_bias_table"], f32)
    bidx, amask = _attn_tables()
    ident = np.eye(128, dtype=f32)
    maps = []
    zoff = 3 * 1024
    for core in range(NCORES):
        b, g = core // 4, core % 4
        hs = list(range(4 * g, 4 * g + 4))
        acols = np.concatenate([np.arange(o + 64 * h, o + 64 * h + 64) for o in (0, 1024, 2048) for h in hs])
        rcols = np.concatenate(
            [np.arange(zoff + o + 64 * h, zoff + o + 64 * h + 64) for o in (0, 1024, 2048) for h in hs]
            + [np.arange(zoff + 3072, zoff + 3072 + 64 + 64 + 160)])
        perm = np.concatenate([np.concatenate([np.arange(256 * r, 256 * r + 256), np.arange(1024 + 256 * r, 1024 + 256 * r + 256)])
                               for r in range(4)])
        abias = table[bidx][:, :, :, hs]
        abias = np.ascontiguousarray(np.transpose(abias, (1, 0, 3, 2))).reshape(128, 3 * 4 * 256)
        m = {
            "xb": np.ascontiguousarray(x[b]),
            "xs": np.ascontiguousarray(x[b, 2048 * g:2048 * (g + 1)]),
            "norm1_g": np.asarray(inputs["norm1_g"], f32).reshape(1, D_MODEL),
            "norm2_g": np.asarray(inputs["norm2_g"], f32).reshape(1, D_MODEL),
            "final_g": np.asarray(inputs["final_g"], f32).reshape(1, D_MODEL),
            "w_in_a": np.ascontiguousarray(w_in[:, acols]),
            "w_in_r": np.ascontiguousarray(w_in[:, rcols]),
            "abias": abias.astype(f32),
            "amask": np.ascontiguousarray(np.transpose(amask, (1, 0, 2))).reshape(128, 768),
            "ident": ident,
            "w_out_p": np.ascontiguousarray(w_out[perm]),
            "w_gu": w_gu,
            "w_down": w_down,
        }
        m.update(host_rwkv(inputs, hs))
        maps.append(m)
    return maps


def host_rwkv(inputs, hs):
    f32 = np.float32
    mixv = np.asarray(inputs["rwkv_shift_mix"], f32)[0]
    vec = np.zeros((128, 64), f32)
    mats = np.zeros((128, 1024), f32)
    hc = np.concatenate([np.arange(64 * h, 64 * h + 64) for h in hs])
    for hl, h in enumerate(hs):
        sl = slice(64 * h, 64 * h + 64)
        vec[0:64, V_MIX_R + hl] = mixv[0:1024][sl]
        vec[0:64, V_MIX_K + hl] = mixv[1024:2048][sl]
        vec[0:64, V_MIX_V + hl] = mixv[2048:3072][sl]
        vec[0:64, V_W0 + hl] = np.asarray(inputs["rwkv_w0"], f32)[0][sl]
        vec[0:64, V_A0 + hl] = np.asarray(inputs["rwkv_a0"], f32)[0][sl]
        vec[0:64, V_KK + hl] = np.asarray(inputs["rwkv_k_k"], f32)[0][sl]
        vec[0:64, V_KA + hl] = np.asarray(inputs["rwkv_k_a"], f32)[0][sl]
        vec[0:64, V_RK + hl] = np.asarray(inputs["rwkv_r_k"], f32)[0][h]
        vec[0:64, V_LNW + hl] = np.asarray(inputs["rwkv_ln_w"], f32)[0][sl]
        vec[0:64, V_LNB + hl] = np.asarray(inputs["rwkv_ln_b"], f32)[0][sl]
    vec[0:64, V_MIX_W] = mixv[3072:3136]
    vec[0:64, V_MIX_A] = mixv[3136:3200]
    vec[0:128, V_MIX_G0] = mixv[3200:3328]
    vec[0:32, V_MIX_G1] = mixv[3328:3360]
    mats[0:64, 0:256] = np.asarray(inputs["rwkv_w_up"], f32)[0][:, hc]
    mats[0:64, 256:512] = np.asarray(inputs["rwkv_a_up"], f32)[0][:, hc]
    gup = np.asarray(inputs["rwkv_g_up"], f32)[0]
    mats[0:128, 512:768] = gup[0:128][:, hc]
    mats[0:32, 768:1024] = gup[128:160][:, hc]
    cst = np.zeros((64, 1024), f32)
    a = np.arange(64)
    cst[:, C_SU:C_SU + 64] = (a[:, None] < a[None, :])
    cst[:, C_IU:C_IU + 64] = (a[:, None] <= a[None, :])
    cst[:, C_SL:C_SL + 64] = (a[None, :] < a[:, None])
    cst[:, C_ID:C_ID + 64] = np.eye(64)
    cst[:, C_AVG:C_AVG + 64] = 1.0 / 64
    cst[:, C_ONE:C_ONE + 64] = 1.0
    rst = np.ones(512, f32)
    rst[0::64] = 0.0
    cst[:, C_RST:C_RST + 512] = rst[None, :]
    return {"rw_vec": vec, "rw_mats": mats, "rw_const": cst}


_NC_CACHE = {}


def kernel(**inputs):
    cfg = {"phases": "WARXD"}
    key = "full"
    if key not in _NC_CACHE:
        _NC_CACHE[key] = build(cfg)
    nc = _NC_CACHE[key]
    maps = host_inputs(inputs)
    res = run_bass_kernel_spmd(nc, maps, core_ids=list(range(NCORES)))
    out = np.empty((NBATCH, SEQ, D_MODEL), np.float32)
    for core in range(NCORES):
        b, g = core // 4, core % 4
        out[b, 2048 * g:2048 * (g + 1)] = res.results[core]["out"]
    return out
```
